# Optimizing a Trainium2 kernel written in Bass

```python
import math, functools
import jax, jax.numpy as jnp
from jax import lax
import numpy as np

D_MODEL = 1024
BATCH = 4
SEQ = 8192
DEPTH = 1
DEC_BATCH = 32
DEC_SEQ = 64
PAST_LEN = 1024

CHUNK = 64
N_HEADS = 8
HEAD_DIM = 64
V_DIM = 2 * HEAD_DIM
Q_WIDTH = N_HEADS * 2 * HEAD_DIM
ATTN_WIDTH = N_HEADS * V_DIM
LRU_WIDTH = D_MODEL
LRU_BLOCKS = 16
LRU_BLOCK = LRU_WIDTH // LRU_BLOCKS
CONV_WIDTH = 4
LRU_C = 8.0
D_FF = 4 * D_MODEL
Q_BLOCK = 128
EPS = 1e-6
IN_WIDTH = 3 * Q_WIDTH + 2 * LRU_WIDTH + 2 * D_MODEL
IN_SPLITS = (Q_WIDTH, 2 * Q_WIDTH, 3 * Q_WIDTH, 3 * Q_WIDTH + LRU_WIDTH,
             3 * Q_WIDTH + 2 * LRU_WIDTH, 3 * Q_WIDTH + 2 * LRU_WIDTH + D_MODEL)

kernel_name = "diffattn_rglru_gated_streaming_encoder"


def _rms_norm(x, g):
    xf = x.astype(jnp.float32)
    xf = xf * lax.rsqrt(jnp.mean(jnp.square(xf), axis=-1, keepdims=True) + EPS)
    return (xf * g.astype(jnp.float32)).astype(x.dtype)


def _diff_attention(q, k, v, q_pos, k_pos, lam, head_gain, lam_init):
    scale = HEAD_DIM ** -0.5
    s = jnp.einsum("bqhcd,bkhcd->bhcqk", q.astype(jnp.float32), k.astype(jnp.float32)) * scale
    visible = (k_pos[None, :] // CHUNK) <= (q_pos[:, None] // CHUNK)
    s = jnp.where(visible, s, -jnp.inf)
    p = jax.nn.softmax(s, axis=-1)
    a = p[:, :, 0] - lam * p[:, :, 1]
    o = jnp.einsum("bhqk,bkhe->bqhe", a, v.astype(jnp.float32))
    o = o * lax.rsqrt(jnp.mean(jnp.square(o), axis=-1, keepdims=True) + EPS)
    return o * head_gain.astype(jnp.float32) * (1.0 - lam_init)


def _attend_prompt(q, k, v, lam, head_gain, lam_init):
    b, s = q.shape[0], q.shape[1]
    nb = s // Q_BLOCK
    qb = q.reshape(b, nb, Q_BLOCK, N_HEADS, 2, HEAD_DIM).swapaxes(0, 1)
    k_pos = jnp.arange(s)

    def one_block(args):
        q_blk, start = args
        q_pos = start + jnp.arange(Q_BLOCK)
        return _diff_attention(q_blk, k, v, q_pos, k_pos, lam, head_gain, lam_init)

    o = lax.map(one_block, (qb, jnp.arange(nb) * Q_BLOCK))
    return o.swapaxes(0, 1).reshape(b, s, ATTN_WIDTH)


def _attend_sample(cache_k, cache_v, q, k, v, lam, head_gain, lam_init):
    b, s = q.shape[0], q.shape[1]
    past = cache_k.shape[1]
    kc = jnp.concatenate([cache_k.astype(k.dtype), k], axis=1)
    vc = jnp.concatenate([cache_v.astype(v.dtype), v], axis=1)
    q_pos = past + jnp.arange(s)
    k_pos = jnp.arange(past + s)
    o = _diff_attention(q, kc, vc, q_pos, k_pos, lam, head_gain, lam_init)
    return o.reshape(b, s, ATTN_WIDTH)


def _lin_combine(left, right):
    a1, b1 = left
    a2, b2 = right
    return a1 * a2, a2 * b1 + b2


def _rglru_branch(x_lru, g_lru, conv_state, h0, conv_w, conv_b, w_r, b_r, w_i, b_i, lru_lambda):
    b, s, _ = x_lru.shape
    xpad = jnp.concatenate([conv_state.astype(x_lru.dtype), x_lru], axis=1)
    xc = conv_b + sum(conv_w[j] * xpad[:, j:j + s] for j in range(CONV_WIDTH))
    new_conv = xpad[:, -(CONV_WIDTH - 1):]
    xb = xc.reshape(b, s, LRU_BLOCKS, LRU_BLOCK)
    r = jax.nn.sigmoid(jnp.einsum("bsnc,ncd->bsnd", xb, w_r).reshape(b, s, LRU_WIDTH) + b_r)
    i = jax.nn.sigmoid(jnp.einsum("bsnc,ncd->bsnd", xb, w_i).reshape(b, s, LRU_WIDTH) + b_i)
    log_a = -LRU_C * r.astype(jnp.float32) * jax.nn.softplus(-lru_lambda.astype(jnp.float32))
    a = jnp.exp(log_a)
    u = jnp.sqrt(-jnp.expm1(2.0 * log_a)) * (i * xc).astype(jnp.float32)
    a_cum, h = lax.associative_scan(_lin_combine, (a, u), axis=1)
    h = h + a_cum * h0.astype(jnp.float32)[:, None, :]
    y = h * jax.nn.gelu(g_lru.astype(jnp.float32))
    return y, new_conv, h[:, -1]


def _layer(x, conv_state, h0, attend, g_mix, g_mlp, w_in, lam_q, lam_k, head_gain,
           conv_w, conv_b, w_r, b_r, w_i, b_i, lru_lambda, w_ba, w_bl, w_o, w_up, w_down, lam_init):
    b, s, _ = x.shape
    xn = _rms_norm(x, g_mix)
    proj = xn @ w_in
    q, k, v, x_lru, g_lru, gate_a, gate_b = jnp.split(proj, IN_SPLITS, axis=-1)
    q = q.reshape(b, s, N_HEADS, 2, HEAD_DIM)
    k = k.reshape(b, s, N_HEADS, 2, HEAD_DIM)
    v = v.reshape(b, s, N_HEADS, V_DIM)
    lqf, lkf = lam_q.astype(jnp.float32), lam_k.astype(jnp.float32)
    lam = jnp.exp(jnp.sum(lqf[0] * lkf[0])) - jnp.exp(jnp.sum(lqf[1] * lkf[1])) + lam_init
    o_attn = attend(q, k, v, lam, head_gain, lam_init)
    y_lru, conv_new, h_new = _rglru_branch(x_lru, g_lru, conv_state, h0, conv_w, conv_b,
                                           w_r, b_r, w_i, b_i, lru_lambda)
    merged = (jax.nn.sigmoid(gate_a.astype(jnp.float32)) * (o_attn.astype(x.dtype) @ w_ba)
              + jax.nn.sigmoid(gate_b.astype(jnp.float32)) * (y_lru.astype(x.dtype) @ w_bl))
    h = x + (merged.astype(x.dtype) @ w_o).astype(x.dtype)
    hn = _rms_norm(h, g_mlp)
    out = h + (jnp.square(jax.nn.relu(hn @ w_up)) @ w_down).astype(x.dtype)
    return out, k, v, conv_new, h_new.astype(x.dtype)


def setup_inputs(seed: int = 0) -> dict:
    key = jax.random.key(seed)
    ks = jax.random.split(key, 32)
    f32 = jnp.float32
    nrm = lambda k, shape, sc: jax.random.normal(k, shape, f32) * sc
    a_c = jax.random.uniform(ks[17], (DEPTH, LRU_WIDTH), f32, 0.9, 0.999)
    s_l = a_c ** (1.0 / LRU_C)
    lru_lambda = jnp.log(s_l) - jnp.log1p(-s_l)
    return {
        "x_prompt": nrm(ks[0], (BATCH, SEQ, D_MODEL), 1.0),
        "x_sample": nrm(ks[1], (DEC_BATCH, DEC_SEQ, D_MODEL), 1.0),
        "cache_k": nrm(ks[2], (DEPTH, DEC_BATCH, PAST_LEN, N_HEADS, 2, HEAD_DIM), 1.0),
        "cache_v": nrm(ks[3], (DEPTH, DEC_BATCH, PAST_LEN, N_HEADS, V_DIM), 1.0),
        "state_conv": nrm(ks[4], (DEPTH, DEC_BATCH, CONV_WIDTH - 1, LRU_WIDTH), 1.0),
        "state_lru": nrm(ks[5], (DEPTH, DEC_BATCH, LRU_WIDTH), 0.5),
        "norm_mix": 1.0 + nrm(ks[6], (DEPTH, D_MODEL), 0.02),
        "norm_mlp": 1.0 + nrm(ks[7], (DEPTH, D_MODEL), 0.02),
        "norm_final": 1.0 + nrm(ks[8], (D_MODEL,), 0.02),
        "w_in": nrm(ks[9], (DEPTH, D_MODEL, IN_WIDTH), D_MODEL ** -0.5),
        "lambda_q": nrm(ks[10], (DEPTH, 2, HEAD_DIM), 0.1),
        "lambda_k": nrm(ks[11], (DEPTH, 2, HEAD_DIM), 0.1),
        "head_gain": 1.0 + nrm(ks[12], (DEPTH, V_DIM), 0.02),
        "conv_w": nrm(ks[13], (DEPTH, CONV_WIDTH, LRU_WIDTH), CONV_WIDTH ** -0.5),
        "conv_b": nrm(ks[14], (DEPTH, LRU_WIDTH), 0.01),
        "w_rgate": nrm(ks[15], (DEPTH, LRU_BLOCKS, LRU_BLOCK, LRU_BLOCK), LRU_BLOCK ** -0.5),
        "b_rgate": nrm(ks[16], (DEPTH, LRU_WIDTH), 0.01),
        "w_igate": nrm(ks[18], (DEPTH, LRU_BLOCKS, LRU_BLOCK, LRU_BLOCK), LRU_BLOCK ** -0.5),
        "b_igate": nrm(ks[19], (DEPTH, LRU_WIDTH), 0.01),
        "lru_lambda": lru_lambda,
        "w_branch_attn": nrm(ks[20], (DEPTH, ATTN_WIDTH, D_MODEL), ATTN_WIDTH ** -0.5),
        "w_branch_lru": nrm(ks[21], (DEPTH, LRU_WIDTH, D_MODEL), LRU_WIDTH ** -0.5),
        "w_out": nrm(ks[22], (DEPTH, D_MODEL, D_MODEL), D_MODEL ** -0.5),
        "w_mlp_up": nrm(ks[23], (DEPTH, D_MODEL, D_FF), D_MODEL ** -0.5),
        "w_mlp_down": nrm(ks[24], (DEPTH, D_FF, D_MODEL), D_FF ** -0.5),
    }


def reference(x_prompt, x_sample, cache_k, cache_v, state_conv, state_lru,
              norm_mix, norm_mlp, norm_final, w_in, lambda_q, lambda_k, head_gain,
              conv_w, conv_b, w_rgate, b_rgate, w_igate, b_igate, lru_lambda,
              w_branch_attn, w_branch_lru, w_out, w_mlp_up, w_mlp_down):
    xp, xs = x_prompt, x_sample
    bp = xp.shape[0]
    kp_l, vp_l, cp_l, hp_l, ks_l, vs_l, cs_l, hs_l = [], [], [], [], [], [], [], []
    for l in range(DEPTH):
        lam_init = 0.8 - 0.6 * math.exp(-0.3 * l)
        lw = (norm_mix[l], norm_mlp[l], w_in[l], lambda_q[l], lambda_k[l], head_gain[l],
              conv_w[l], conv_b[l], w_rgate[l], b_rgate[l], w_igate[l], b_igate[l], lru_lambda[l],
              w_branch_attn[l], w_branch_lru[l], w_out[l], w_mlp_up[l], w_mlp_down[l], lam_init)
        conv0 = jnp.zeros((bp, CONV_WIDTH - 1, LRU_WIDTH), xp.dtype)
        h0 = jnp.zeros((bp, LRU_WIDTH), xp.dtype)
        xp, kp, vp, cp, hp = _layer(xp, conv0, h0, _attend_prompt, *lw)
        attend_s = functools.partial(_attend_sample, cache_k[l], cache_v[l])
        xs, kn, vn, cn, hn = _layer(xs, state_conv[l], state_lru[l], attend_s, *lw)
        kp_l.append(kp); vp_l.append(vp); cp_l.append(cp); hp_l.append(hp)
        ks_l.append(kn); vs_l.append(vn); cs_l.append(cn); hs_l.append(hn)
    y_prompt = _rms_norm(xp, norm_final)
    y_sample = _rms_norm(xs, norm_final)
    k_prompt = jnp.stack(kp_l, axis=0)
    v_prompt = jnp.stack(vp_l, axis=0)
    conv_prompt = jnp.stack(cp_l, axis=0)
    lru_prompt = jnp.stack(hp_l, axis=0)
    k_sample = jnp.stack(ks_l, axis=0)
    v_sample = jnp.stack(vs_l, axis=0)
    conv_sample = jnp.stack(cs_l, axis=0)
    lru_sample = jnp.stack(hs_l, axis=0)
    return (y_prompt, y_sample, k_prompt, v_prompt, conv_prompt, lru_prompt,
            k_sample, v_sample, conv_sample, lru_sample)
```

```python
import numpy as np
import concourse.bass as bass
import concourse.mybir as mybir
from concourse.bass_utils import run_bass_kernel_spmd

F32 = mybir.dt.float32
BF16 = mybir.dt.bfloat16
AF = mybir.ActivationFunctionType
ALU = mybir.AluOpType

D = 1024
NEG = -30000.0
LAM_INIT = 0.2
EPS = 1e-6
SEM_LIMIT = 28000
NDMA_SLOTS = 20


class Res:
    __slots__ = ("name", "w", "r")

    def __init__(self, name):
        self.name = name
        self.w = None
        self.r = {}


class Sched:
    def __init__(self, nc):
        self.nc = nc
        self.sems = []
        self.prog = {e: [] for e in ("pe", "act", "dve", "pool", "sp")}
        self.known = {e: {} for e in self.prog}
        self.cur = {}
        self.cnt = {}
        self.slots = {}
        self.slot_next = {}
        for e in ("pe", "act", "dve", "pool"):
            self.cur[e] = self._newsem()
            self.cnt[e] = 0
        for q in ("sp", "pool"):
            self.slots[q] = [[self._newsem(), 0] for _ in range(NDMA_SLOTS)]
            self.slot_next[q] = 0
        self.nops = 0
        self.enabled = True
        self.limit = 10 ** 9

    def _newsem(self):
        self.sems.append(None)
        return len(self.sems) - 1

    def _need(self, eng, deps, tok, war):
        if tok is None:
            return
        teng, si, val, isdma = tok
        if not isdma and teng == eng:
            if eng == "pe":
                return
        if self.known[eng].get(si, 0) >= val:
            return
        if deps.get(si, 0) < val:
            deps[si] = val

    def stage(self, n):
        if n > self.limit:
            self.enabled = False

    def op(self, eng, fn, R=(), W=(), dma=False):
        if not self.enabled:
            return None
        deps = {}
        for r in R:
            self._need(eng, deps, r.w, False)
        for w in W:
            self._need(eng, deps, w.w, False)
            for t in w.r.values():
                self._need(eng, deps, t, True)
        if dma:
            k = self.slot_next[eng]
            self.slot_next[eng] = (k + 1) % NDMA_SLOTS
            slot = self.slots[eng][k]
            if slot[1] > 0 and self.known[eng].get(slot[0], 0) < slot[1]:
                if deps.get(slot[0], 0) < slot[1]:
                    deps[slot[0]] = slot[1]
            slot[1] += 16
            tok = (eng, slot[0], slot[1], True)
            inc = 16
        else:
            if self.cnt[eng] >= SEM_LIMIT:
                self.cur[eng] = self._newsem()
                self.cnt[eng] = 0
            self.cnt[eng] += 1
            tok = (eng, self.cur[eng], self.cnt[eng], False)
            inc = 1
        P = self.prog[eng]
        for si, val in deps.items():
            P.append(("w", si, val))
            self.known[eng][si] = val
        P.append(("o", fn, tok[1], inc))
        self.nops += 1
        for r in R:
            r.r[(eng, tok[1])] = tok
        for w in W:
            w.w = tok
            w.r = {}
        return tok

    def finish(self):
        P = self.prog["sp"]
        for q in ("sp", "pool"):
            for si, val in self.slots[q]:
                if val > 0 and self.known["sp"].get(si, 0) < val:
                    P.append(("w", si, val))
        for e in ("pe", "act", "dve", "pool"):
            if self.cnt[e] > 0:
                P.append(("w", self.cur[e], self.cnt[e]))

    def replay(self, eng, e):
        sems = self.sems
        for it in self.prog[eng]:
            if it[0] == "w":
                e.wait_ge(sems[it[1]], it[2])
            else:
                ins = it[1](e)
                ins.then_inc(sems[it[2]], it[3])


class Ring:
    def __init__(self, tiles):
        self.tiles = tiles
        self.i = 0

    def next(self):
        t = self.tiles[self.i]
        self.i = (self.i + 1) % len(self.tiles)
        return t


class T:
    def __init__(self, h, nunits=1, name=""):
        self.h = h
        self.u = [Res("%s.%d" % (name, i)) for i in range(nunits)]

    @property
    def r(self):
        return self.u[0]

    def ap(self, c, lo, hi):
        return self.h[:, c, lo:hi]

    def res(self, c):
        return self.u[c]

    def allres(self):
        return self.u[0:8]


class HidView:
    def __init__(self, hid, base):
        self.hid, self.base = hid, base

    def ap(self, c, lo, hi):
        o = (self.base + c) * 512
        return self.hid.h[:, o + lo:o + hi]

    def res(self, c):
        return self.hid.u[self.base + c]

    def allres(self):
        return self.hid.u[self.base:self.base + 8]


def build_program(NP, do_prompt=True, do_sample=True, limit=10 ** 9):
    nc = bass.Bass("TRN2", target_bir_lowering=False)
    NSB = 2 * NP
    NTOK = NSB * 512
    NKB = NTOK // 128

    def din(name, shape, dt=F32):
        return nc.dram_tensor(name, list(shape), dt, kind="ExternalInput").ap()

    def dout(name, shape, dt=F32):
        return nc.dram_tensor(name, list(shape), dt, kind="ExternalOutput").ap()

    def dscr(name, shape, dt=BF16):
        return nc.dram_tensor(name, list(shape), dt).ap()

    xs = din("xs", [NTOK, D])
    xo = din("xo", [NP * 512, D])
    xsm = din("xsm", [256, D])
    ck = din("ck", [4, 1024, D])
    cv = din("cv", [4, 1024, D])
    sconvT = din("sconvT", [128, 8, 4, 3])
    slruT = din("slruT", [128, 8, 4])
    w_in = din("w_in", [D, 7168])
    w_ba = din("w_ba", [D, D])
    w_bl = din("w_bl", [D, D])
    w_o = din("w_o", [D, D])
    w_up = din("w_up", [D, 4096])
    w_dn = din("w_dn", [4096, D])
    wr_bd = din("wr_bd", [8, 128, 128])
    wi_bd = din("wi_bd", [8, 128, 128])
    NV = 93
    vecs_d = din("vecs", [128, NV])
    nfb_d = din("nfb", [128, D])
    lamb_d = din("lamb", [128, 256])
    dmask_d = din("dmask", [4, 128, 512])

    y_own = dout("y_own", [NP * 512, D])
    y_smp = dout("y_smp", [256, D])
    k_all = dout("k_all", [NTOK, D])
    v_all = dout("v_all", [NTOK, D])
    conv_p = dout("conv_p", [128, 8, 3])
    lru_p = dout("lru_p", [128, 8])
    k_smp = dout("k_smp", [256, D])
    v_smp = dout("v_smp", [256, D])
    conv_s = dout("conv_s", [128, 8, 4, 3])
    lru_s = dout("lru_s", [128, 8, 4])

    Wsc_in = dscr("Wsc_in", [128, 8, 7168])
    Wsc_ba = dscr("Wsc_ba", [128, 8, D])
    Wsc_bl = dscr("Wsc_bl", [128, 8, D])
    Wsc_o = dscr("Wsc_o", [128, 8, D])
    Wsc_up = dscr("Wsc_up", [128, 8, 4096])
    Wsc_dn = dscr("Wsc_dn", [128, 32, D])
    KTsc = dscr("KTsc", [128, 8, NTOK])
    Vsc = dscr("Vsc", [128, NKB, D])
    Vsm_sc = dscr("Vsm_sc", [4, 64, D])

    S = Sched(nc)
    S.limit = limit
    A = nc.alloc_sbuf_tensor

    def sb(name, shape, dt, nunits=1):
        return T(A("sb_" + name, list(shape), dt), nunits, name)

    ident = sb("ident", [128, 128], BF16)
    ones = sb("ones", [128, 128], BF16)
    vecs = sb("vecs", [128, NV], F32)
    drv = sb("drv", [128, 32], F32)
    nfb = sb("nfb", [128, D], F32)
    lamb = sb("lamb", [128, 256], F32)
    dmask = sb("dmask", [128, 4, 512], BF16)
    identM = sb("identM", [128, 4, 128], BF16)
    wr = sb("wr", [128, 8, 128], BF16)
    wi = sb("wi", [128, 8, 128], BF16)
    T8a = sb("T8a", [128, 8, 512], BF16, 8)
    T8b = sb("T8b", [128, 8, 512], BF16, 8)
    T8d = sb("T8d", [128, 8, 512], BF16, 8)
    QA = sb("QA", [128, 8, 512], BF16, 8)
    QB = sb("QB", [128, 8, 512], BF16, 8)
    hid = sb("hid", [128, 32 * 512], BF16, 32)
    ysel = sb("ysel", [128, 8, 512], F32, 8)
    xres = sb("xres", [128, 4, D], F32, 4)
    xtiles = Ring([sb("xt%d" % i, [128, D], F32) for i in range(2)])
    xns = Ring([sb("xn%d" % i, [128, D], BF16) for i in range(2)])
    kouts = Ring([sb("kout%d" % i, [128, D], F32) for i in range(1)])
    vouts = Ring([sb("vout%d" % i, [128, D], F32) for i in range(1)])
    vbs = Ring([sb("vb%d" % i, [128, D], BF16) for i in range(1)])
    wring = Ring([sb("wbuf%d" % i, [128, 8, 512], BF16) for i in range(3)])
    tmpF = Ring([sb("tF%d" % i, [128, 512], F32) for i in range(8)])
    tmpX = Ring([sb("tX%d" % i, [128, 520], F32) for i in range(2)])
    tmpB = Ring([sb("tB%d" % i, [128, 512], BF16) for i in range(6)])
    ksts = Ring([sb("kst%d" % i, [128, 1024], BF16) for i in range(3)])
    vsts = Ring([sb("vst%d" % i, [128, 8, 128], BF16) for i in range(3)])
    vnew = sb("vnew", [128, D], BF16)
    Pz = [sb("Pz%d" % i, [128, 64], BF16) for i in range(2)]
    stats = Ring([sb("st%d" % i, [128, 4], F32) for i in range(12)])
    hist_p = sb("hist_p", [128, 8, 1, 3], F32, 8)
    hprev_p = sb("hprev_p", [128, 8, 1], F32, 8)
    hist_s = sb("hist_s", [128, 8, 4, 3], F32, 8)
    hprev_s = sb("hprev_s", [128, 8, 4], F32, 8)
    mergedV = HidView(hid, 0)

    class AliasF:
        def __init__(self, k):
            self.h = hid.h[:, k * 1024:(k + 1) * 1024].bitcast(F32)
            self.rs = [hid.u[2 * k], hid.u[2 * k + 1]]

    lru_xc = Ring([AliasF(k) for k in range(0, 5)])
    lru_rt = Ring([AliasF(k) for k in range(5, 8)])
    lru_it = Ring([AliasF(k) for k in range(8, 11)])
    lru_at = Ring([AliasF(k) for k in range(11, 14)])
    onS = HidView(hid, 8)

    PS = [T(nc.alloc_psum_tensor("ps%d" % i, [128, 512], F32), 1, "ps%d" % i) for i in range(8)]

    C_GMIX, C_GMLP, C_CW, C_CB, C_BR, C_BI, C_LAM, C_GAIN, C_SEL = 0, 8, 16, 48, 56, 64, 72, 80, 81
    V_CNEG, V_EPS, V_ONE, V_ZERO, V_NEGLAM, V_GAINS, V_BIASB = 0, 8, 9, 10, 11, 12, 13

    def vcol(c, n=1):
        return vecs.h[:, c:c + n]

    def dcol(c, n=1):
        return drv.h[:, c:c + n]

    RW = {}
    R_kt = [Res("ktsc%d" % i) for i in range(NSB)]
    R_v = [Res("vsc%d" % i) for i in range(NKB)]
    R_vsm = [Res("vsm%d" % i) for i in range(4)]

    S.op("pool", lambda e: e.memset(ident.h[:], 1.0), W=[ident.r])
    S.op("pool", lambda e: e.affine_select(out=ident.h[:], in_=ident.h[:], pattern=[[-1, 128]],
                                           compare_op=ALU.is_equal, fill=0.0, base=0, channel_multiplier=1),
         R=[ident.r], W=[ident.r])
    S.op("pool", lambda e: e.memset(ones.h[:], 1.0), W=[ones.r])
    S.op("pool", lambda e: e.memset(drv.h[:], 0.0), W=[drv.r])
    S.op("pool", lambda e: e.memset(drv.h[:, V_EPS:V_EPS + 1], EPS), R=[drv.r], W=[drv.r])
    S.op("pool", lambda e: e.memset(drv.h[:, V_ONE:V_ONE + 1], 1.0), R=[drv.r], W=[drv.r])
    S.op("pool", lambda e: e.memset(QA.h[:], 0.0), W=QA.u)
    S.op("pool", lambda e: e.memset(QB.h[:], 0.0), W=QB.u)
    S.op("pool", lambda e: e.memset(hist_p.h[:], 0.0), W=hist_p.u)
    S.op("pool", lambda e: e.memset(vnew.h[:], 0.0), W=[vnew.r])
    for _pz in Pz:
        S.op("pool", (lambda _pz=_pz: lambda e: e.memset(_pz.h[:], 0.0))(), W=[_pz.r])
    S.op("pool", lambda e: e.memset(hprev_p.h[:], 0.0), W=hprev_p.u)
    S.op("sp", lambda e: e.dma_start(out=vecs.h[:], in_=vecs_d), W=[vecs.r], dma=True)
    S.op("sp", lambda e: e.dma_start(out=nfb.h[:], in_=nfb_d), W=[nfb.r], dma=True)
    S.op("sp", lambda e: e.dma_start(out=lamb.h[:], in_=lamb_d), W=[lamb.r], dma=True)
    S.op("sp", lambda e: e.dma_start(out=hist_s.h[:], in_=sconvT), W=hist_s.u, dma=True)
    S.op("sp", lambda e: e.dma_start(out=hprev_s.h[:], in_=slruT), W=hprev_s.u, dma=True)
    S.op("pool", lambda e: e.dma_start(out=dmask.h[:], in_=dmask_d.rearrange("r p q -> p r q")), W=[dmask.r], dma=True)
    S.op("pool", lambda e: e.dma_start(out=wr.h[:], in_=wr_bd.rearrange("c p n -> p c n")), W=[wr.r], dma=True)
    S.op("pool", lambda e: e.dma_start(out=wi.h[:], in_=wi_bd.rearrange("c p n -> p c n")), W=[wi.r], dma=True)
    w_in_v = w_in.rearrange("(c p) n -> p c n", p=128)

    def cast_w_in(key, c0, c1):
        RW[key] = []
        for c in range(0, 8, 2):
            r = Res("%s_%d" % (key, c))
            RW[key].append(r)
            S.op("pool", (lambda c=c: lambda e: e.dma_start(out=Wsc_in[:, c:c + 2, c0:c1], in_=w_in_v[:, c:c + 2, c0:c1]))(), W=[r], dma=True)

    def cast_w(key, src, dst, kc):
        RW[key] = []
        sv = src.rearrange("(c p) n -> p c n", p=128)
        for c in range(0, kc, 4):
            r = Res("%s_%d" % (key, c))
            RW[key].append(r)
            S.op("pool", (lambda c=c: lambda e: e.dma_start(out=dst[:, c:c + 4, :], in_=sv[:, c:c + 4, :]))(), W=[r], dma=True)

    cast_w_in("kvx", 1024, 4096)

    def prologue_rest():
        cast_w_in("q", 0, 1024)
        cast_w_in("gab", 4096, 7168)
        cast_w("ba", w_ba, Wsc_ba, 8)
        cast_w("bl", w_bl, Wsc_bl, 8)
        cast_w("o", w_o, Wsc_o, 8)
        cast_w("up", w_up, Wsc_up, 8)
        cast_w("dn", w_dn, Wsc_dn, 32)

    S.stage(1)
    tA = stats.next()
    S.op("act", lambda e: e.activation(out=drv.h[:, 16:24], in_=vcol(C_LAM, 8), func=AF.Exp, scale=-1.0), R=[vecs.r, drv.r], W=[drv.r])
    S.op("act", lambda e: e.activation(out=drv.h[:, 16:24], in_=drv.h[:, 16:24], func=AF.Ln, bias=dcol(V_ONE), scale=1.0), R=[drv.r], W=[drv.r])
    S.op("dve", lambda e: e.tensor_scalar(out=drv.h[:, V_CNEG:V_CNEG + 8], in0=drv.h[:, 16:24], scalar1=-8.0, scalar2=None, op0=ALU.mult),
         R=[drv.r], W=[drv.r])
    lp = tmpF.next()
    S.op("dve", lambda e: e.tensor_tensor(out=lp.h[:, 0:128], in0=lamb.h[:, 0:128], in1=lamb.h[:, 128:256], op=ALU.mult), R=[lamb.r], W=[lp.r])
    S.op("act", lambda e: e.activation(out=lp.h[:, 128:192], in_=lp.h[:, 0:64], func=AF.Copy, accum_out=tA.h[:, 0:1]), R=[lp.r], W=[tA.r, lp.r])
    S.op("act", lambda e: e.activation(out=lp.h[:, 192:256], in_=lp.h[:, 64:128], func=AF.Copy, accum_out=tA.h[:, 1:2]), R=[lp.r, tA.r], W=[tA.r, lp.r])
    S.op("act", lambda e: e.activation(out=tA.h[:, 2:4], in_=tA.h[:, 0:2], func=AF.Exp), R=[tA.r], W=[tA.r])
    S.op("dve", lambda e: e.tensor_tensor(out=tA.h[:, 0:1], in0=tA.h[:, 3:4], in1=tA.h[:, 2:3], op=ALU.subtract), R=[tA.r], W=[tA.r])
    S.op("dve", lambda e: e.tensor_scalar(out=drv.h[:, V_NEGLAM:V_NEGLAM + 1], in0=tA.h[:, 0:1], scalar1=-LAM_INIT, scalar2=None, op0=ALU.add),
         R=[tA.r, drv.r], W=[drv.r])
    S.op("dve", lambda e: e.tensor_scalar(out=drv.h[:, V_GAINS:V_GAINS + 1], in0=vcol(C_GAIN), scalar1=1.0 - LAM_INIT, scalar2=None, op0=ALU.mult),
         R=[vecs.r, drv.r], W=[drv.r])
    for par in range(2):
        S.op("dve", (lambda par=par: lambda e: e.tensor_scalar(out=drv.h[:, V_BIASB + par:V_BIASB + par + 1], in0=vcol(C_SEL + 2 * par),
                                                               scalar1=NEG, scalar2=None, op0=ALU.mult))(),
             R=[vecs.r, drv.r], W=[drv.r])
        for ab in range(2):
            S.op("dve", (lambda par=par, ab=ab: lambda e: e.tensor_scalar(out=identM.h[:, 2 * par + ab, :], in0=ident.h[:],
                                                                         scalar1=vcol(C_SEL + 2 * par + ab), scalar2=None, op0=ALU.mult))(),
                 R=[vecs.r, ident.r, identM.r], W=[identM.r])

    ps_rot = {"i": 0}

    def next_ps(lo=0, n=4):
        key = (lo, n)
        c = ps_rot.get(key, 0)
        ps_rot[key] = c + 1
        return PS[lo + (c % n)]

    def wres(wb):
        return wb.rs if hasattr(wb, "rs") else [wb.r]

    class WBuf:
        def __init__(self, h, rs):
            self.h, self.rs = h, rs

    def wload(scr_ap, res, wb=None):
        if wb is None:
            wb = wring.next()
        S.op("sp", lambda e: e.dma_start(out=wb.h[:, :, :], in_=scr_ap), R=res, W=wres(wb), dma=True)
        return wb

    wb_T8d = WBuf(T8d.h, T8d.u)
    wb_xr = [WBuf(xres.h[:, 2 * k:2 * k + 2, :].rearrange("p a b -> p (a b)").bitcast(BF16).rearrange("p (c n) -> p c n", c=8),
                  [xres.u[2 * k], xres.u[2 * k + 1]]) for k in range(2)]

    def chain_fm(ps, wb, ocl, xT, NT):
        def fn(e):
            for kc in range(8):
                ins = e.matmul(ps.h[:, 0:NT], lhsT=wb.h[:, kc, ocl * 128:(ocl + 1) * 128], rhs=xT.ap(kc, 0, NT),
                               start=(kc == 0), stop=(kc == 7))
            return ins
        S.op("pe", fn, R=wres(wb) + xT.allres(), W=[ps.r])

    def chain_tm(ps, wb, xT, t0, TT):
        def fn(e):
            for kc in range(8):
                ins = e.matmul(ps.h[0:TT, 0:512], lhsT=xT.ap(kc, t0, t0 + TT), rhs=wb.h[:, kc, :],
                               start=(kc == 0), stop=(kc == 7))
            return ins
        S.op("pe", fn, R=wres(wb) + xT.allres(), W=[ps.r])

    def rstd_of(src_ap, src_res, width, jk):
        st = stats.next()
        S.op("act", lambda e: e.activation(out=jk.h[:, 0:width], in_=src_ap, func=AF.Square, accum_out=st.h[:, 0:1]),
             R=src_res, W=[st.r, jk.r])
        S.op("act", lambda e: e.activation(out=st.h[:, 1:2], in_=st.h[:, 0:1], func=AF.Sqrt, scale=1.0 / width, bias=dcol(V_EPS)),
             R=[st.r, drv.r], W=[st.r])
        S.op("dve", lambda e: e.reciprocal(out=st.h[:, 2:3], in_=st.h[:, 1:2]), R=[st.r], W=[st.r])
        return st

    def norm_T(src_rows, NT, gcol0, dstT, keep):
        for t in range(NT // 128):
            if src_rows is not None and not keep:
                xtile = xtiles.next()
                xt_ap, xt_res = xtile.h[:], [xtile.r]
            else:
                xt_ap, xt_res = xres.h[:, t, :], [xres.u[t]]
            if src_rows is not None:
                S.op("sp", (lambda xt_ap=xt_ap, t=t: lambda e: e.dma_start(out=xt_ap, in_=src_rows[t * 128:(t + 1) * 128, :]))(),
                     W=xt_res, dma=True)
            xn = xns.next()
            st = rstd_of(xt_ap, xt_res, D, xn)
            S.op("dve", (lambda xn=xn, xt_ap=xt_ap, st=st: lambda e: e.tensor_scalar(out=xn.h[:], in0=xt_ap, scalar1=st.h[:, 2:3], scalar2=None, op0=ALU.mult))(),
                 R=xt_res + [st.r], W=[xn.r])
            ps = next_ps(4, 2)
            psb = ps.h[:].bitcast(BF16)

            def fn(e, xn=xn, psb=psb):
                for c in range(8):
                    ins = e.transpose(out=psb[:, c * 128:(c + 1) * 128], in_=xn.h[:, c * 128:(c + 1) * 128], identity=ident.h[:])
                return ins
            S.op("pe", fn, R=[xn.r, ident.r], W=[ps.r])
            S.op("dve", (lambda psb=psb, t=t: lambda e: e.tensor_tensor(
                out=dstT.h[:, :, t * 128:(t + 1) * 128], in0=psb[:, 0:1024].rearrange("p (c t) -> p c t", c=8),
                in1=vecs.h[:, gcol0:gcol0 + 8].unsqueeze(2).to_broadcast([128, 8, 128]), op=ALU.mult))(),
                 R=[ps.r, vecs.r], W=dstT.u)

    def phaseA(src_rows, NT, nseq, hist, hprev, selcol, first_slot, kdst, vdst, ktsc_dst, vsc_dst, TT):
        L = NT // nseq
        S.stage(2)
        norm_T(src_rows, NT, C_GMIX, T8a, keep=False)
        S.stage(3)
        wX = [wload(Wsc_in[:, :, 3072 + n * 512:3072 + (n + 1) * 512], RW["kvx"], wb_xr[n]) for n in range(2)]
        wK = [wload(Wsc_in[:, :, 1024 + n * 512:1024 + (n + 1) * 512], RW["kvx"]) for n in range(2)]
        wV = [wload(Wsc_in[:, :, 2048:2560], RW["kvx"]), wload(Wsc_in[:, :, 2560:3072], RW["kvx"], wb_T8d)]
        items = []

        def it_kT(oc):
            ps = next_ps(5, 3)
            chain_fm(ps, wK[oc // 4], oc % 4, T8a, NT)

            def ev():
                S.op("dve", lambda e: e.tensor_copy(out=T8b.h[:, oc, 0:NT], in_=ps.h[:, 0:NT]), R=[ps.r], W=[T8b.u[oc]])
                if oc == 7 and ktsc_dst is not None:
                    S.op("pool", lambda e: e.dma_start(out=ktsc_dst[0], in_=T8b.h[:, :, 0:NT]), R=T8b.u, W=[ktsc_dst[1]], dma=True)
            return ev

        def it_tok(t, wW, ring, dst, isv):
            pss = []
            for n in range(2):
                ps = next_ps(5, 3)
                chain_tm(ps, wW[n], T8a, t * TT, TT)
                pss.append(ps)

            def ev():
                o_ = ring.next()
                S.op("act", lambda e: e.activation(out=o_.h[0:TT, 0:512], in_=pss[0].h[0:TT, :], func=AF.Copy), R=[pss[0].r], W=[o_.r])
                S.op("dve", lambda e: e.tensor_copy(out=o_.h[0:TT, 512:1024], in_=pss[1].h[0:TT, :]), R=[pss[1].r, o_.r], W=[o_.r])
                S.op("pool", lambda e: e.dma_start(out=dst(t), in_=o_.h[0:TT, :]), R=[o_.r], dma=True)
                if isv:
                    vb = vbs.next()
                    S.op("dve", lambda e: e.tensor_copy(out=vb.h[0:TT, :], in_=o_.h[0:TT, :]), R=[o_.r], W=[vb.r])
                    dst_ap, dst_res = vsc_dst(t)
                    S.op("pool", lambda e: e.dma_start(out=dst_ap, in_=vb.h[0:TT, :]), R=[vb.r], W=[dst_res], dma=True)
            return ev

        ntok = NT // TT
        tok_items = []
        for t in range(ntok):
            tok_items.append((2, (lambda t=t: it_tok(t, wK, kouts, kdst, False))))
            tok_items.append((2, (lambda t=t: it_tok(t, wV, vouts, vdst, True))))
        kt_items = [(1, (lambda oc=oc: it_kT(oc))) for oc in range(8)]
        while kt_items or tok_items:
            if kt_items:
                items.append(kt_items.pop(0))
            if tok_items:
                items.append(tok_items.pop(0))

        st_ = {}

        def p0(cc):
            ps = next_ps(0, 5)
            chain_fm(ps, wX[cc // 4], cc % 4, T8a, NT)
            st_[cc] = dict(ps=ps)

        def p1(cc):
            d_ = st_[cc]
            ps = d_["ps"]
            xp = tmpX.next()
            xp3 = xp.h[:, 0:nseq * (L + 3)].rearrange("p (s l) -> p s l", s=nseq)
            S.op("dve", lambda e: e.tensor_copy(out=xp3[:, :, 0:3], in_=hist.h[:, cc, :, :]), R=[hist.u[cc]], W=[xp.r])
            S.op("act", lambda e: e.activation(out=xp3[:, :, 3:3 + L], in_=ps.h[:, 0:NT].rearrange("p (s l) -> p s l", s=nseq), func=AF.Copy),
                 R=[ps.r, xp.r], W=[xp.r])
            xc = lru_xc.next()
            xc3 = xc.h[:, 0:NT].rearrange("p (s l) -> p s l", s=nseq)
            d_.update(xp=xp, xp3=xp3, xc=xc, xc3=xc3)

        def p2(cc):
            d_ = st_[cc]
            xp, xp3, xc, xc3 = d_["xp"], d_["xp3"], d_["xc"], d_["xc3"]
            S.op("dve", lambda e: e.tensor_copy(out=hist.h[:, cc, :, :], in_=xp3[:, :, L:L + 3]), R=[xp.r], W=[hist.u[cc]])
            S.op("dve", lambda e: e.tensor_scalar(out=xc3, in0=xp3[:, :, 0:L], scalar1=vcol(C_CW + cc * 4 + 0), scalar2=vcol(C_CB + cc), op0=ALU.mult, op1=ALU.add),
                 R=[xp.r, vecs.r], W=xc.rs)
            for jj in range(1, 4):
                S.op("dve", (lambda jj=jj: lambda e: e.scalar_tensor_tensor(
                    out=xc3, in0=xp3[:, :, jj:jj + L], scalar=vcol(C_CW + cc * 4 + jj), in1=xc3, op0=ALU.mult, op1=ALU.add))(),
                     R=[xp.r, vecs.r] + xc.rs, W=xc.rs)
            xcb = tmpB.next()
            S.op("dve", lambda e: e.tensor_copy(out=xcb.h[:, 0:NT], in_=xc.h[:, 0:NT]), R=xc.rs, W=[xcb.r])
            psr = next_ps(0, 5)
            psi = next_ps(0, 5)
            S.op("pe", lambda e: e.matmul(psr.h[:, 0:NT], lhsT=wr.h[:, cc, :], rhs=xcb.h[:, 0:NT], start=True, stop=True), R=[wr.r, xcb.r], W=[psr.r])
            S.op("pe", lambda e: e.matmul(psi.h[:, 0:NT], lhsT=wi.h[:, cc, :], rhs=xcb.h[:, 0:NT], start=True, stop=True), R=[wi.r, xcb.r], W=[psi.r])
            d_["psr"], d_["psi"] = psr, psi

        def p3(cc):
            d_ = st_[cc]
            psr, psi = d_["psr"], d_["psi"]
            rt_, it_, at_ = lru_rt.next(), lru_it.next(), lru_at.next()
            S.op("act", lambda e: e.activation(out=rt_.h[:, 0:NT], in_=psr.h[:, 0:NT], func=AF.Sigmoid, bias=vcol(C_BR + cc), scale=1.0),
                 R=[psr.r, vecs.r], W=rt_.rs)
            S.op("act", lambda e: e.activation(out=it_.h[:, 0:NT], in_=psi.h[:, 0:NT], func=AF.Sigmoid, bias=vcol(C_BI + cc), scale=1.0),
                 R=[psi.r, vecs.r], W=it_.rs)
            S.op("act", lambda e: e.activation(out=at_.h[:, 0:NT], in_=rt_.h[:, 0:NT], func=AF.Exp, scale=dcol(V_CNEG + cc)),
                 R=rt_.rs + [drv.r], W=at_.rs)
            S.op("dve", lambda e: e.tensor_tensor(out=rt_.h[:, 0:NT], in0=at_.h[:, 0:NT], in1=at_.h[:, 0:NT], op=ALU.mult), R=at_.rs + rt_.rs, W=rt_.rs)
            d_["rt"], d_["it"], d_["at"] = rt_, it_, at_

        def p4(cc):
            rt_ = st_[cc]["rt"]
            S.op("act", lambda e: e.activation(out=rt_.h[:, 0:NT], in_=rt_.h[:, 0:NT], func=AF.Ln, scale=-1.0, bias=dcol(V_ONE)),
                 R=rt_.rs + [drv.r], W=rt_.rs)
            S.op("act", lambda e: e.activation(out=rt_.h[:, 0:NT], in_=rt_.h[:, 0:NT], func=AF.Exp, scale=0.5), R=rt_.rs, W=rt_.rs)

        def p5(cc):
            d_ = st_[cc]
            xc, rt_, it_, at_ = d_["xc"], d_["rt"], d_["it"], d_["at"]
            S.op("dve", lambda e: e.tensor_tensor(out=it_.h[:, 0:NT], in0=it_.h[:, 0:NT], in1=xc.h[:, 0:NT], op=ALU.mult), R=it_.rs + xc.rs, W=it_.rs)
            S.op("dve", lambda e: e.tensor_tensor(out=it_.h[:, 0:NT], in0=it_.h[:, 0:NT], in1=rt_.h[:, 0:NT], op=ALU.mult), R=it_.rs + rt_.rs, W=it_.rs)
            ht = rt_
            for s in range(nseq):
                S.op("dve", (lambda s=s: lambda e: e.tensor_tensor_scan(
                    out=ht.h[:, s * L:(s + 1) * L], data0=at_.h[:, s * L:(s + 1) * L], data1=it_.h[:, s * L:(s + 1) * L],
                    initial=hprev.h[:, cc, s:s + 1], op0=ALU.mult, op1=ALU.add))(),
                     R=at_.rs + it_.rs + [hprev.u[cc]] + ht.rs, W=ht.rs)
            S.op("dve", lambda e: e.tensor_copy(out=hprev.h[:, cc, :], in_=ht.h[:, 0:NT].rearrange("p (s l) -> p s l", s=nseq)[:, :, L - 1]),
                 R=ht.rs, W=[hprev.u[cc]])
            if first_slot:
                S.op("dve", lambda e: e.tensor_scalar(out=ysel.h[:, cc, 0:NT], in0=ht.h[:, 0:NT], scalar1=vcol(selcol), scalar2=None, op0=ALU.mult),
                     R=ht.rs + [vecs.r], W=[ysel.u[cc]])
            else:
                S.op("dve", lambda e: e.scalar_tensor_tensor(out=ysel.h[:, cc, 0:NT], in0=ht.h[:, 0:NT], scalar=vcol(selcol),
                                                             in1=ysel.h[:, cc, 0:NT], op0=ALU.mult, op1=ALU.add),
                     R=ht.rs + [vecs.r, ysel.u[cc]], W=[ysel.u[cc]])

        stages = [p0, p1, p2, p3, p4, p5]
        niter = 8 + len(stages) - 1
        pend_ev = []
        for t in range(niter):
            for k in range(len(stages)):
                cc = t - k
                if 0 <= cc < 8:
                    stages[k](cc)
            for ev in pend_ev:
                ev()
            pend_ev = []
            budget = 3
            while items and items[0][0] <= budget:
                n_, f_ = items.pop(0)
                budget -= n_
                pend_ev.append(f_())
        while items or pend_ev:
            for ev in pend_ev:
                ev()
            pend_ev = []
            budget = 3
            while items and items[0][0] <= budget:
                n_, f_ = items.pop(0)
                budget -= n_
                pend_ev.append(f_())

    class Attn:
        LOOK = 4

        def __init__(self, h, NQ, qoff, nblocks):
            self.h, self.NQ, self.qoff, self.nb, self.i = h, NQ, qoff, nblocks, 0
            self.si = 0
            self.pend = []

        def _flush(self, keep):
            while len(self.pend) > keep:
                self.pend.pop(0)()

        def block(self, kt_ap, v_ap, nk, Rk, Rv, mask=None):
            h, NQ, qoff = self.h, self.NQ, self.qoff
            first, last = (self.i == 0), (self.i == self.nb - 1)
            self.i += 1
            for c in range(2):
                Sb = PS[self.si % 4]
                self.si += 1
                Q = QA if c == 0 else QB

                def fn(e, Sb=Sb, Q=Q):
                    ins = e.matmul(Sb.h[0:nk, 0:NQ], lhsT=kt_ap, rhs=Q.h[:, h, qoff:qoff + NQ], start=True, stop=(mask is None))
                    if mask is not None:
                        ins = e.matmul(Sb.h[0:nk, 0:NQ], lhsT=mask[0], rhs=mask[1], start=False, stop=True)
                    return ins
                S.op("pe", fn, R=Rk + [Q.u[h]] + ([identM.r, dmask.r] if mask is not None else []), W=[Sb.r])
                pad = nk < 128
                Pc = Pz[c] if pad else tmpB.next()
                bias_ap = mask[2] if mask is not None else dcol(V_ZERO)
                S.op("act", (lambda Pc=Pc, Sb=Sb, bias_ap=bias_ap: lambda e: e.activation(out=Pc.h[0:nk, 0:NQ], in_=Sb.h[0:nk, 0:NQ], func=AF.Exp,
                                                                                       scale=0.125, bias=bias_ap[0:nk, :]))(),
                     R=[Sb.r, drv.r] + ([Pc.r] if pad else []), W=[Pc.r])
                Ob, Lb = PS[4 + c], PS[6 + c]

                def pv(Pc=Pc, Ob=Ob, Lb=Lb, first=first, last=last):
                    def fn2(e):
                        e.matmul(Ob.h[:, 0:NQ], lhsT=v_ap, rhs=Pc.h[:, 0:NQ], start=first, stop=last)
                        return e.matmul(Lb.h[:, 0:NQ], lhsT=ones.h[:], rhs=Pc.h[:, 0:NQ], start=first, stop=last)
                    S.op("pe", fn2, R=Rv + [Pc.r, ones.r], W=[Ob.r, Lb.r])
                self.pend.append(pv)
                self._flush(self.LOOK)

        def finish(self, dstV):
            self._flush(0)
            h, NQ, qoff = self.h, self.NQ, self.qoff
            o1, o2, l1, l2 = tmpF.next(), tmpF.next(), tmpF.next(), tmpF.next()
            S.op("dve", lambda e: e.tensor_copy(out=l1.h[:, 0:NQ], in_=PS[6].h[:, 0:NQ]), R=[PS[6].r], W=[l1.r])
            S.op("dve", lambda e: e.tensor_copy(out=o1.h[:, 0:NQ], in_=PS[4].h[:, 0:NQ]), R=[PS[4].r], W=[o1.r])
            S.op("dve", lambda e: e.tensor_copy(out=l2.h[:, 0:NQ], in_=PS[7].h[:, 0:NQ]), R=[PS[7].r], W=[l2.r])
            S.op("dve", lambda e: e.tensor_copy(out=o2.h[:, 0:NQ], in_=PS[5].h[:, 0:NQ]), R=[PS[5].r], W=[o2.r])
            S.op("dve", lambda e: e.reciprocal(out=l1.h[:, 0:NQ], in_=l1.h[:, 0:NQ]), R=[l1.r], W=[l1.r])
            S.op("dve", lambda e: e.reciprocal(out=l2.h[:, 0:NQ], in_=l2.h[:, 0:NQ]), R=[l2.r], W=[l2.r])
            S.op("dve", lambda e: e.tensor_tensor(out=o1.h[:, 0:NQ], in0=o1.h[:, 0:NQ], in1=l1.h[:, 0:NQ], op=ALU.mult), R=[o1.r, l1.r], W=[o1.r])
            S.op("dve", lambda e: e.tensor_tensor(out=o2.h[:, 0:NQ], in0=o2.h[:, 0:NQ], in1=l2.h[:, 0:NQ], op=ALU.mult), R=[o2.r, l2.r], W=[o2.r])
            t1 = o1
            S.op("dve", lambda e: e.scalar_tensor_tensor(out=t1.h[:, 0:NQ], in0=o2.h[:, 0:NQ], scalar=dcol(V_NEGLAM), in1=o1.h[:, 0:NQ], op0=ALU.mult, op1=ALU.add),
                 R=[o1.r, o2.r, drv.r], W=[t1.r])
            sq = xns.next()
            S.op("dve", lambda e: e.tensor_tensor(out=sq.h[:, 0:NQ], in0=t1.h[:, 0:NQ], in1=t1.h[:, 0:NQ], op=ALU.mult), R=[t1.r], W=[sq.r])
            Mb = PS[self.si % 4]
            self.si += 1

            def tail():
                S.op("pe", lambda e: e.matmul(Mb.h[:, 0:NQ], lhsT=ones.h[:], rhs=sq.h[:, 0:NQ], start=True, stop=True), R=[sq.r, ones.r], W=[Mb.r])
                r1 = l1
                S.op("dve", lambda e: e.tensor_scalar(out=r1.h[:, 0:NQ], in0=Mb.h[:, 0:NQ], scalar1=1.0 / 128, scalar2=EPS, op0=ALU.mult, op1=ALU.add), R=[Mb.r, r1.r], W=[r1.r])
                S.op("act", lambda e: e.activation(out=r1.h[:, 0:NQ], in_=r1.h[:, 0:NQ], func=AF.Ln), R=[r1.r], W=[r1.r])
                S.op("act", lambda e: e.activation(out=r1.h[:, 0:NQ], in_=r1.h[:, 0:NQ], func=AF.Exp, scale=-0.5), R=[r1.r], W=[r1.r])
                S.op("dve", lambda e: e.scalar_tensor_tensor(out=dstV.ap(h, qoff, qoff + NQ), in0=t1.h[:, 0:NQ], scalar=dcol(V_GAINS), in1=r1.h[:, 0:NQ],
                                                             op0=ALU.mult, op1=ALU.mult),
                     R=[t1.r, r1.r, drv.r], W=[dstV.res(h)])
            return tail

    def attend_prompt(j):
        par = j % 2
        tail = None
        for h in range(8):
            at = Attn(h, 512, 0, (j + 1) * 8)
            nblk = 0
            for jj in range(j + 1):
                kst, vst = ksts.next(), vsts.next()
                S.op("sp", (lambda kst=kst, jj=jj, h=h: lambda e: e.dma_start(out=kst.h[:], in_=KTsc[:, h, jj * 1024:(jj + 1) * 1024]))(),
                     R=[R_kt[2 * jj], R_kt[2 * jj + 1]], W=[kst.r], dma=True)
                S.op("sp", (lambda vst=vst, jj=jj, h=h: lambda e: e.dma_start(out=vst.h[:], in_=Vsc[:, jj * 8:(jj + 1) * 8, h * 128:(h + 1) * 128]))(),
                     R=[R_v[jj * 8 + k] for k in range(8)], W=[vst.r], dma=True)
                for kb in range(8):
                    mask = None
                    if jj == j:
                        ab = kb // 4
                        bias_ap = dcol(V_ZERO) if ab == 0 else dcol(V_BIASB + par)
                        mask = (identM.h[:, 2 * par + ab, :], dmask.h[:, kb % 4, :], bias_ap)
                    at.block(kst.h[:, kb * 128:(kb + 1) * 128], vst.h[:, kb, :], 128, [kst.r], [vst.r], mask)
                    nblk += 1
                    if nblk == 4 and tail is not None:
                        tail()
                        tail = None
            tail = at.finish(T8b)
        tail()

    def attend_sample():
        stail = [None]
        for s in range(4):
            S.op("sp", (lambda s=s: lambda e: e.dma_start(out=vnew.h[0:64, :], in_=Vsm_sc[s]))(), R=[R_vsm[s], vnew.r], W=[vnew.r], dma=True)
            for h in range(8):
                stg, vst, kst = vsts.next(), vsts.next(), ksts.next()
                S.op("pool", (lambda stg=stg, s=s, h=h: lambda e: e.dma_start(out=stg.h[:], in_=ck[s, :, h * 128:(h + 1) * 128].rearrange("(kb p) c -> p kb c", p=128)))(),
                     W=[stg.r], dma=True)
                S.op("pool", (lambda vst=vst, s=s, h=h: lambda e: e.dma_start(out=vst.h[:], in_=cv[s, :, h * 128:(h + 1) * 128].rearrange("(kb p) c -> p kb c", p=128)))(),
                     W=[vst.r], dma=True)
                ps = next_ps(0, 4)
                psb = ps.h[:].bitcast(BF16)

                def fn(e, stg=stg, psb=psb):
                    for kb in range(8):
                        ins = e.transpose(out=psb[:, kb * 128:(kb + 1) * 128], in_=stg.h[:, kb, :], identity=ident.h[:])
                    return ins
                S.op("pe", fn, R=[stg.r, ident.r], W=[ps.r])
                S.op("dve", (lambda kst=kst, psb=psb: lambda e: e.tensor_copy(out=kst.h[:], in_=psb[:, 0:1024]))(), R=[ps.r], W=[kst.r])
                at = Attn(h, 64, s * 64, 9)
                for kb in range(8):
                    at.block(kst.h[:, kb * 128:(kb + 1) * 128], vst.h[:, kb, :], 128, [kst.r], [vst.r], None)
                    if kb == 3 and stail[0] is not None:
                        stail[0]()
                        stail[0] = None
                at.block(T8b.h[:, h, s * 64:(s + 1) * 64], vnew.h[:, h * 128:(h + 1) * 128], 64, [T8b.u[h]], [vnew.r], None)
                stail[0] = at.finish(onS)
        stail[0]()

    def phaseB(src_rows, NT, attend, onV, ydst):
        S.stage(6)
        norm_T(src_rows, NT, C_GMIX, T8a, keep=True)
        wQ = [wload(Wsc_in[:, :, n * 512:(n + 1) * 512], RW["q"]) for n in range(2)]
        for oc in range(8):
            ps = next_ps(0, 4)
            chain_fm(ps, wQ[oc // 4], oc % 4, T8a, NT)
            S.op("dve", (lambda ps=ps, oc=oc: lambda e: e.tensor_copy(out=QA.h[0:64, oc, 0:NT], in_=ps.h[0:64, 0:NT]))(), R=[ps.r], W=[QA.u[oc]])
            S.op("act", (lambda ps=ps, oc=oc: lambda e: e.activation(out=QB.h[64:128, oc, 0:NT], in_=ps.h[64:128, 0:NT], func=AF.Copy))(), R=[ps.r], W=[QB.u[oc]])
        wG = [wload(Wsc_in[:, :, 4096 + n * 512:4096 + (n + 1) * 512], RW["gab"]) for n in range(2)]
        for oc in range(8):
            ps = next_ps(0, 4)
            chain_fm(ps, wG[oc // 4], oc % 4, T8a, NT)
            gt = tmpF.next()
            S.op("act", (lambda ps=ps, gt=gt: lambda e: e.activation(out=gt.h[:, 0:NT], in_=ps.h[:, 0:NT], func=AF.Gelu_apprx_tanh))(), R=[ps.r], W=[gt.r])
            S.op("dve", (lambda gt=gt, oc=oc: lambda e: e.tensor_tensor(out=T8d.h[:, oc, 0:NT], in0=gt.h[:, 0:NT], in1=ysel.h[:, oc, 0:NT], op=ALU.mult))(),
                 R=[gt.r, ysel.u[oc]], W=[T8d.u[oc]])
        S.stage(7)
        attend()
        S.stage(8)
        for n in range(2):
            wGA = wload(Wsc_in[:, :, 5120 + n * 512:5120 + (n + 1) * 512], RW["gab"])
            wBA = wload(Wsc_ba[:, :, n * 512:(n + 1) * 512], RW["ba"])
            parts = []
            for ocl in range(4):
                psa = next_ps(0, 4)
                chain_fm(psa, wGA, ocl, T8a, NT)
                sa = tmpF.next()
                S.op("act", (lambda psa=psa, sa=sa: lambda e: e.activation(out=sa.h[:, 0:NT], in_=psa.h[:, 0:NT], func=AF.Sigmoid))(), R=[psa.r], W=[sa.r])
                psA = next_ps(0, 4)
                chain_fm(psA, wBA, ocl, onV, NT)
                S.op("dve", (lambda psA=psA, sa=sa: lambda e: e.tensor_tensor(out=sa.h[:, 0:NT], in0=psA.h[:, 0:NT], in1=sa.h[:, 0:NT], op=ALU.mult))(),
                     R=[psA.r, sa.r], W=[sa.r])
                parts.append(sa)
            wGB = wload(Wsc_in[:, :, 6144 + n * 512:6144 + (n + 1) * 512], RW["gab"])
            wBL = wload(Wsc_bl[:, :, n * 512:(n + 1) * 512], RW["bl"])
            for ocl in range(4):
                oc = n * 4 + ocl
                psg = next_ps(0, 4)
                chain_fm(psg, wGB, ocl, T8a, NT)
                sbt = tmpF.next()
                S.op("act", (lambda psg=psg, sbt=sbt: lambda e: e.activation(out=sbt.h[:, 0:NT], in_=psg.h[:, 0:NT], func=AF.Sigmoid))(), R=[psg.r], W=[sbt.r])
                psL = next_ps(0, 4)
                chain_fm(psL, wBL, ocl, T8d, NT)
                S.op("dve", (lambda psL=psL, sbt=sbt: lambda e: e.tensor_tensor(out=sbt.h[:, 0:NT], in0=psL.h[:, 0:NT], in1=sbt.h[:, 0:NT], op=ALU.mult))(),
                     R=[psL.r, sbt.r], W=[sbt.r])
                sa = parts[ocl]
                S.op("dve", (lambda sa=sa, sbt=sbt, oc=oc: lambda e: e.tensor_tensor(out=mergedV.ap(oc, 0, NT), in0=sa.h[:, 0:NT], in1=sbt.h[:, 0:NT], op=ALU.add))(),
                     R=[sa.r, sbt.r], W=[mergedV.res(oc)])
        S.stage(9)
        wO = [wload(Wsc_o[:, :, n * 512:(n + 1) * 512], RW["o"]) for n in range(2)]
        for t in range(NT // 128):
            for n in range(2):
                ps = next_ps(6, 2)
                chain_tm(ps, wO[n], mergedV, t * 128, 128)
                S.op("dve", (lambda ps=ps, t=t, n=n: lambda e: e.tensor_tensor(out=xres.h[:, t, n * 512:(n + 1) * 512], in0=ps.h[:, 0:512],
                                                                            in1=xres.h[:, t, n * 512:(n + 1) * 512], op=ALU.add))(),
                     R=[ps.r, xres.u[t]], W=[xres.u[t]])
        S.stage(10)
        norm_T(None, NT, C_GMLP, T8a, keep=True)
        for ob in range(8):
            wU = wload(Wsc_up[:, :, ob * 512:(ob + 1) * 512], RW["up"])
            for ocl in range(4):
                oc = ob * 4 + ocl
                ps = next_ps(0, 4)
                chain_fm(ps, wU, ocl, T8a, NT)
                rl = tmpF.next()
                S.op("act", (lambda ps=ps, rl=rl: lambda e: e.activation(out=rl.h[:, 0:NT], in_=ps.h[:, 0:NT], func=AF.Relu))(), R=[ps.r], W=[rl.r])
                S.op("pool", (lambda rl=rl, oc=oc: lambda e: e.tensor_tensor(out=hid.h[:, oc * 512:oc * 512 + NT], in0=rl.h[:, 0:NT], in1=rl.h[:, 0:NT], op=ALU.mult))(),
                     R=[rl.r], W=[hid.u[oc]])
        ntt = NT // 128
        for n in range(2):
            for kg in range(4):
                wD = wload(Wsc_dn[:, kg * 8:(kg + 1) * 8, n * 512:(n + 1) * 512], RW["dn"])
                for t in range(ntt):
                    ps = PS[4 + t]

                    def fn(e, ps=ps, t=t, kg=kg, wD=wD):
                        for k8 in range(8):
                            kc = kg * 8 + k8
                            ins = e.matmul(ps.h[:, 0:512], lhsT=hid.h[:, kc * 512 + t * 128:kc * 512 + (t + 1) * 128], rhs=wD.h[:, k8, :],
                                           start=(kc == 0), stop=(kc == 31))
                        return ins
                    S.op("pe", fn, R=[wD.r] + hid.u[kg * 8:(kg + 1) * 8], W=[ps.r])
            for t in range(ntt):
                ps = PS[4 + t]
                S.op("dve", (lambda ps=ps, t=t, n=n: lambda e: e.tensor_tensor(out=xres.h[:, t, n * 512:(n + 1) * 512], in0=ps.h[:, 0:512],
                                                                            in1=xres.h[:, t, n * 512:(n + 1) * 512], op=ALU.add))(),
                     R=[ps.r, xres.u[t]], W=[xres.u[t]])
        S.stage(11)
        for t in range(ntt):
            jk = xns.next()
            st = rstd_of(xres.h[:, t, :], [xres.u[t]], D, jk)
            S.op("dve", (lambda st=st, t=t: lambda e: e.scalar_tensor_tensor(out=xres.h[:, t, :], in0=xres.h[:, t, :], scalar=st.h[:, 2:3], in1=nfb.h[:],
                                                                            op0=ALU.mult, op1=ALU.mult))(),
                 R=[st.r, nfb.r, xres.u[t]], W=[xres.u[t]])
            S.op("pool", (lambda t=t: lambda e: e.dma_start(out=ydst[t * 128:(t + 1) * 128, :], in_=xres.h[:, t, :]))(), R=[xres.u[t]], dma=True)

    rest_done = False
    if do_prompt:
        for j in range(NP):
            for slot in range(2):
                i = 2 * j + slot
                phaseA(xs[i * 512:(i + 1) * 512, :], 512, 1, hist_p, hprev_p, C_SEL + 2 * (j % 2) + slot, slot == 0,
                       kdst=lambda t, i=i: k_all[i * 512 + t * 128:i * 512 + (t + 1) * 128, :],
                       vdst=lambda t, i=i: v_all[i * 512 + t * 128:i * 512 + (t + 1) * 128, :],
                       ktsc_dst=(KTsc[:, :, i * 512:(i + 1) * 512], R_kt[i]),
                       vsc_dst=lambda t, i=i: (Vsc[:, i * 4 + t, :], R_v[i * 4 + t]), TT=128)
                if not rest_done:
                    S.stage(5.5)
                    prologue_rest()
                    rest_done = True
            phaseB(xo[j * 512:(j + 1) * 512, :], 512, (lambda j=j: attend_prompt(j)), T8b, y_own[j * 512:(j + 1) * 512, :])
        S.op("pool", lambda e: e.dma_start(out=conv_p, in_=hist_p.h[:, :, 0, :]), R=hist_p.u, dma=True)
        S.op("pool", lambda e: e.dma_start(out=lru_p, in_=hprev_p.h[:, :, 0]), R=hprev_p.u, dma=True)
    if do_sample:
        phaseA(xsm, 256, 4, hist_s, hprev_s, C_SEL + 4, True,
               kdst=lambda t: k_smp[t * 64:(t + 1) * 64, :], vdst=lambda t: v_smp[t * 64:(t + 1) * 64, :],
               ktsc_dst=None, vsc_dst=lambda t: (Vsm_sc[t], R_vsm[t]), TT=64)
        if not rest_done:
            prologue_rest()
            rest_done = True
        phaseB(xsm, 256, attend_sample, onS, y_smp)
        S.op("pool", lambda e: e.dma_start(out=conv_s, in_=hist_s.h[:]), R=hist_s.u, dma=True)
        S.op("pool", lambda e: e.dma_start(out=lru_s, in_=hprev_s.h[:]), R=hprev_s.u, dma=True)

    S.finish()
    from contextlib import ExitStack
    with ExitStack() as es:
        for i in range(len(S.sems)):
            S.sems[i] = es.enter_context(nc.semaphore("s%d" % i))
        block = es.enter_context(nc.Block())

        @block.tensor
        def _(e):
            S.replay("pe", e)

        @block.scalar
        def _(e):
            S.replay("act", e)

        @block.vector
        def _(e):
            S.replay("dve", e)

        @block.gpsimd
        def _(e):
            S.replay("pool", e)

        @block.sync
        def _(e):
            S.replay("sp", e)
    return nc, S


def _colvec(v):
    return np.ascontiguousarray(np.asarray(v, np.float32).reshape(8, 128).T)


def _own_index(half, j):
    return 2 * j + ((half + j) % 2)


def make_in_maps(inp, NP, n_cores=8):
    f = lambda a: np.ascontiguousarray(np.asarray(a, np.float32))
    x_prompt, x_sample = f(inp["x_prompt"]), f(inp["x_sample"])
    ck_all = f(inp["cache_k"])[0].reshape(32, 1024, D)
    cv_all = f(inp["cache_v"])[0].reshape(32, 1024, D)
    sconv, slru = f(inp["state_conv"])[0], f(inp["state_lru"])[0]
    w_r, w_i = f(inp["w_rgate"])[0], f(inp["w_igate"])[0]
    wr_bd = np.zeros((8, 128, 128), np.float32)
    wi_bd = np.zeros((8, 128, 128), np.float32)
    for cc in range(8):
        for k in range(2):
            wr_bd[cc, k * 64:(k + 1) * 64, k * 64:(k + 1) * 64] = w_r[2 * cc + k]
            wi_bd[cc, k * 64:(k + 1) * 64, k * 64:(k + 1) * 64] = w_i[2 * cc + k]
    nfb = np.ascontiguousarray(np.broadcast_to(f(inp["norm_final"])[None, :], (128, D)))
    lamb = np.ascontiguousarray(np.broadcast_to(
        np.concatenate([f(inp["lambda_q"])[0].reshape(-1), f(inp["lambda_k"])[0].reshape(-1)])[None, :], (128, 256)))
    dmask = np.zeros((4, 128, 512), np.float32)
    kk = np.arange(128)[:, None]
    qq = np.arange(512)[None, :]
    for r in range(4):
        dmask[r] = np.where((2 * r + kk // 64) <= (qq // 64), 0.0, NEG)
    shared = dict(
        w_in=f(inp["w_in"])[0], w_ba=f(inp["w_branch_attn"])[0], w_bl=f(inp["w_branch_lru"])[0], w_o=f(inp["w_out"])[0],
        w_up=f(inp["w_mlp_up"])[0], w_dn=f(inp["w_mlp_down"])[0], wr_bd=wr_bd, wi_bd=wi_bd, nfb=nfb, lamb=lamb, dmask=dmask)
    vbase = np.zeros((128, 93), np.float32)
    vbase[:, 0:8] = _colvec(inp["norm_mix"][0])
    vbase[:, 8:16] = _colvec(inp["norm_mlp"][0])
    cw = f(inp["conv_w"])[0]
    for jj in range(4):
        vbase[:, 16 + jj:48:4] = _colvec(cw[jj])
    vbase[:, 48:56] = _colvec(inp["conv_b"][0])
    vbase[:, 56:64] = _colvec(inp["b_rgate"][0])
    vbase[:, 64:72] = _colvec(inp["b_igate"][0])
    vbase[:, 72:80] = _colvec(inp["lru_lambda"][0])
    vbase[:, 80] = f(inp["head_gain"])[0]
    vbase[:, 85] = 1.0
    maps = []
    for c in range(n_cores):
        b, half = c // 2, c % 2
        v = vbase.copy()
        for par in range(2):
            o = (half + par) % 2
            v[:, 81 + 2 * par] = 1.0 if o == 0 else 0.0
            v[:, 82 + 2 * par] = 1.0 if o == 1 else 0.0
        xs = x_prompt[b, :NP * 1024]
        xo = np.concatenate([xs[_own_index(half, j) * 512:(_own_index(half, j) + 1) * 512] for j in range(NP)], axis=0)
        sq = slice(4 * c, 4 * c + 4)
        m = dict(shared)
        m.update(xs=np.ascontiguousarray(xs), xo=np.ascontiguousarray(xo),
                 xsm=np.ascontiguousarray(x_sample[sq].reshape(256, D)),
                 ck=np.ascontiguousarray(ck_all[sq]), cv=np.ascontiguousarray(cv_all[sq]),
                 sconvT=np.ascontiguousarray(sconv[sq].reshape(4, 3, 8, 128).transpose(3, 2, 0, 1)),
                 slruT=np.ascontiguousarray(slru[sq].reshape(4, 8, 128).transpose(2, 1, 0)),
                 vecs=v)
        maps.append(m)
    return maps


def assemble(results, NP, n_cores=8):
    SEQ = NP * 1024
    B = n_cores // 2
    y_prompt = np.zeros((B, SEQ, D), np.float32)
    y_sample = np.zeros((4 * n_cores, 64, D), np.float32)
    k_prompt = np.zeros((1, B, SEQ, 8, 2, 64), np.float32)
    v_prompt = np.zeros((1, B, SEQ, 8, 128), np.float32)
    conv_prompt = np.zeros((1, B, 3, D), np.float32)
    lru_prompt = np.zeros((1, B, D), np.float32)
    k_sample = np.zeros((1, 4 * n_cores, 64, 8, 2, 64), np.float32)
    v_sample = np.zeros((1, 4 * n_cores, 64, 8, 128), np.float32)
    conv_sample = np.zeros((1, 4 * n_cores, 3, D), np.float32)
    lru_sample = np.zeros((1, 4 * n_cores, D), np.float32)
    for c in range(n_cores):
        r = results[c]
        b, half = c // 2, c % 2
        for j in range(NP):
            i = _own_index(half, j)
            y_prompt[b, i * 512:(i + 1) * 512] = r["y_own"][j * 512:(j + 1) * 512]
        if half == 0:
            k_prompt[0, b] = r["k_all"].reshape(SEQ, 8, 2, 64)
            v_prompt[0, b] = r["v_all"].reshape(SEQ, 8, 128)
            conv_prompt[0, b] = r["conv_p"].transpose(2, 1, 0).reshape(3, D)
            lru_prompt[0, b] = r["lru_p"].T.reshape(D)
        sq = slice(4 * c, 4 * c + 4)
        y_sample[sq] = r["y_smp"].reshape(4, 64, D)
        k_sample[0, sq] = r["k_smp"].reshape(4, 64, 8, 2, 64)
        v_sample[0, sq] = r["v_smp"].reshape(4, 64, 8, 128)
        conv_sample[0, sq] = r["conv_s"].transpose(2, 3, 1, 0).reshape(4, 3, D)
        lru_sample[0, sq] = r["lru_s"].transpose(2, 1, 0).reshape(4, D)
    return (y_prompt, y_sample, k_prompt, v_prompt, conv_prompt, lru_prompt, k_sample, v_sample, conv_sample, lru_sample)


def kernel(**inputs):
    NP = inputs["x_prompt"].shape[1] // 1024
    nc, _ = build_program(NP)
    maps = make_in_maps(inputs, NP)
    res = run_bass_kernel_spmd(nc, maps, core_ids=list(range(8)))
    return assemble(res.results, NP)
```

```python
import numpy as np
import concourse.bass as bass
import concourse.mybir as mybir
from concourse.bass_utils import run_bass_kernel_spmd

F32 = mybir.dt.float32
BF16 = mybir.dt.bfloat16
AF = mybir.ActivationFunctionType
ALU = mybir.AluOpType

D = 1024
NEG = -30000.0
LAM_INIT = 0.2
EPS = 1e-6
SEM_LIMIT = 28000
NDMA_SLOTS = 20


class Res:
    __slots__ = ("name", "w", "r")

    def __init__(self, name):
        self.name = name
        self.w = None
        self.r = {}


class Sched:
    def __init__(self, nc):
        self.nc = nc
        self.sems = []
        self.prog = {e: [] for e in ("pe", "act", "dve", "pool", "sp")}
        self.known = {e: {} for e in self.prog}
        self.cur = {}
        self.cnt = {}
        self.slots = {}
        self.slot_next = {}
        for e in ("pe", "act", "dve", "pool"):
            self.cur[e] = self._newsem()
            self.cnt[e] = 0
        for q in ("sp", "pool"):
            self.slots[q] = [[self._newsem(), 0] for _ in range(NDMA_SLOTS)]
            self.slot_next[q] = 0
        self.nops = 0
        self.enabled = True
        self.limit = 10 ** 9

    def _newsem(self):
        self.sems.append(None)
        return len(self.sems) - 1

    def _need(self, eng, deps, tok, war):
        if tok is None:
            return
        teng, si, val, isdma = tok
        if not isdma and teng == eng:
            if eng == "pe":
                return
        if self.known[eng].get(si, 0) >= val:
            return
        if deps.get(si, 0) < val:
            deps[si] = val

    def stage(self, n):
        if n > self.limit:
            self.enabled = False

    def op(self, eng, fn, R=(), W=(), dma=False):
        if not self.enabled:
            return None
        deps = {}
        for r in R:
            self._need(eng, deps, r.w, False)
        for w in W:
            self._need(eng, deps, w.w, False)
            for t in w.r.values():
                self._need(eng, deps, t, True)
        if dma:
            k = self.slot_next[eng]
            self.slot_next[eng] = (k + 1) % NDMA_SLOTS
            slot = self.slots[eng][k]
            if slot[1] > 0 and self.known[eng].get(slot[0], 0) < slot[1]:
                if deps.get(slot[0], 0) < slot[1]:
                    deps[slot[0]] = slot[1]
            slot[1] += 16
            tok = (eng, slot[0], slot[1], True)
            inc = 16
        else:
            if self.cnt[eng] >= SEM_LIMIT:
                self.cur[eng] = self._newsem()
                self.cnt[eng] = 0
            self.cnt[eng] += 1
            tok = (eng, self.cur[eng], self.cnt[eng], False)
            inc = 1
        P = self.prog[eng]
        for si, val in deps.items():
            P.append(("w", si, val))
            self.known[eng][si] = val
        P.append(("o", fn, tok[1], inc))
        self.nops += 1
        for r in R:
            r.r[(eng, tok[1])] = tok
        for w in W:
            w.w = tok
            w.r = {}
        return tok

    def finish(self):
        P = self.prog["sp"]
        for q in ("sp", "pool"):
            for si, val in self.slots[q]:
                if val > 0 and self.known["sp"].get(si, 0) < val:
                    P.append(("w", si, val))
        for e in ("pe", "act", "dve", "pool"):
            if self.cnt[e] > 0:
                P.append(("w", self.cur[e], self.cnt[e]))

    def replay(self, eng, e):
        sems = self.sems
        for it in self.prog[eng]:
            if it[0] == "w":
                e.wait_ge(sems[it[1]], it[2])
            else:
                ins = it[1](e)
                ins.then_inc(sems[it[2]], it[3])


class Ring:
    def __init__(self, tiles):
        self.tiles = tiles
        self.i = 0

    def next(self):
        t = self.tiles[self.i]
        self.i = (self.i + 1) % len(self.tiles)
        return t


class T:
    def __init__(self, h, nunits=1, name=""):
        self.h = h
        self.u = [Res("%s.%d" % (name, i)) for i in range(nunits)]

    @property
    def r(self):
        return self.u[0]

    def ap(self, c, lo, hi):
        return self.h[:, c, lo:hi]

    def res(self, c):
        return self.u[c]

    def allres(self):
        return self.u[0:8]


class HidView:
    def __init__(self, hid, base):
        self.hid, self.base = hid, base

    def ap(self, c, lo, hi):
        o = (self.base + c) * 512
        return self.hid.h[:, o + lo:o + hi]

    def res(self, c):
        return self.hid.u[self.base + c]

    def allres(self):
        return self.hid.u[self.base:self.base + 8]


def build_program(NP, do_prompt=True, do_sample=True, limit=10 ** 9):
    nc = bass.Bass("TRN2", target_bir_lowering=False)
    NSB = 2 * NP
    NTOK = NSB * 512
    NKB = NTOK // 128

    def din(name, shape, dt=F32):
        return nc.dram_tensor(name, list(shape), dt, kind="ExternalInput").ap()

    def dout(name, shape, dt=F32):
        return nc.dram_tensor(name, list(shape), dt, kind="ExternalOutput").ap()

    def dscr(name, shape, dt=BF16):
        return nc.dram_tensor(name, list(shape), dt).ap()

    xs = din("xs", [NTOK, D])
    xo = din("xo", [NP * 512, D])
    xsm = din("xsm", [256, D])
    ck = din("ck", [4, 1024, D])
    cv = din("cv", [4, 1024, D])
    sconvT = din("sconvT", [128, 8, 4, 3])
    slruT = din("slruT", [128, 8, 4])
    w_in = din("w_in", [D, 7168])
    w_ba = din("w_ba", [D, D])
    w_bl = din("w_bl", [D, D])
    w_o = din("w_o", [D, D])
    w_up = din("w_up", [D, 4096])
    w_dn = din("w_dn", [4096, D])
    wr_bd = din("wr_bd", [8, 128, 128])
    wi_bd = din("wi_bd", [8, 128, 128])
    NV = 93
    vecs_d = din("vecs", [128, NV])
    nfb_d = din("nfb", [128, D])
    lamb_d = din("lamb", [128, 256])
    dmask_d = din("dmask", [4, 128, 512])

    y_own = dout("y_own", [NP * 512, D])
    y_smp = dout("y_smp", [256, D])
    k_all = dout("k_all", [NTOK, D])
    v_all = dout("v_all", [NTOK, D])
    conv_p = dout("conv_p", [128, 8, 3])
    lru_p = dout("lru_p", [128, 8])
    k_smp = dout("k_smp", [256, D])
    v_smp = dout("v_smp", [256, D])
    conv_s = dout("conv_s", [128, 8, 4, 3])
    lru_s = dout("lru_s", [128, 8, 4])

    Wsc_in = dscr("Wsc_in", [128, 8, 7168])
    Wsc_ba = dscr("Wsc_ba", [128, 8, D])
    Wsc_bl = dscr("Wsc_bl", [128, 8, D])
    Wsc_o = dscr("Wsc_o", [128, 8, D])
    Wsc_up = dscr("Wsc_up", [128, 8, 4096])
    Wsc_dn = dscr("Wsc_dn", [128, 32, D])
    KTsc = dscr("KTsc", [128, 8, NTOK])
    Vsc = dscr("Vsc", [128, NKB, D])
    Vsm_sc = dscr("Vsm_sc", [4, 64, D])

    S = Sched(nc)
    S.limit = limit
    A = nc.alloc_sbuf_tensor

    def sb(name, shape, dt, nunits=1):
        return T(A("sb_" + name, list(shape), dt), nunits, name)

    ident = sb("ident", [128, 128], BF16)
    ones = sb("ones", [128, 128], BF16)
    vecs = sb("vecs", [128, NV], F32)
    drv = sb("drv", [128, 32], F32)
    nfb = sb("nfb", [128, D], F32)
    lamb = sb("lamb", [128, 256], F32)
    dmask = sb("dmask", [128, 4, 512], BF16)
    identM = sb("identM", [128, 4, 128], BF16)
    wr = sb("wr", [128, 8, 128], BF16)
    wi = sb("wi", [128, 8, 128], BF16)
    T8a = sb("T8a", [128, 8, 512], BF16, 8)
    T8b = sb("T8b", [128, 8, 512], BF16, 8)
    T8d = sb("T8d", [128, 8, 512], BF16, 8)
    QA = sb("QA", [128, 8, 512], BF16, 8)
    QB = sb("QB", [128, 8, 512], BF16, 8)
    hid = sb("hid", [128, 32 * 512], BF16, 32)
    ysel = sb("ysel", [128, 8, 512], F32, 8)
    xres = sb("xres", [128, 4, D], F32, 4)
    xtiles = Ring([sb("xt%d" % i, [128, D], F32) for i in range(2)])
    xns = Ring([sb("xn%d" % i, [128, D], BF16) for i in range(2)])
    kouts = Ring([sb("kout%d" % i, [128, D], F32) for i in range(1)])
    vouts = Ring([sb("vout%d" % i, [128, D], F32) for i in range(1)])
    vbs = Ring([sb("vb%d" % i, [128, D], BF16) for i in range(1)])
    wring = Ring([sb("wbuf%d" % i, [128, 8, 512], BF16) for i in range(3)])
    tmpF = Ring([sb("tF%d" % i, [128, 512], F32) for i in range(8)])
    tmpX = Ring([sb("tX%d" % i, [128, 520], F32) for i in range(2)])
    tmpB = Ring([sb("tB%d" % i, [128, 512], BF16) for i in range(6)])
    ksts = Ring([sb("kst%d" % i, [128, 1024], BF16) for i in range(3)])
    vsts = Ring([sb("vst%d" % i, [128, 8, 128], BF16) for i in range(3)])
    vnew = sb("vnew", [128, D], BF16)
    Pz = [sb("Pz%d" % i, [128, 64], BF16) for i in range(2)]
    stats = Ring([sb("st%d" % i, [128, 4], F32) for i in range(12)])
    hist_p = sb("hist_p", [128, 8, 1, 3], F32, 8)
    hprev_p = sb("hprev_p", [128, 8, 1], F32, 8)
    hist_s = sb("hist_s", [128, 8, 4, 3], F32, 8)
    hprev_s = sb("hprev_s", [128, 8, 4], F32, 8)
    mergedV = HidView(hid, 0)

    class AliasF:
        def __init__(self, k):
            self.h = hid.h[:, k * 1024:(k + 1) * 1024].bitcast(F32)
            self.rs = [hid.u[2 * k], hid.u[2 * k + 1]]

    lru_xc = Ring([AliasF(k) for k in range(0, 5)])
    lru_rt = Ring([AliasF(k) for k in range(5, 8)])
    lru_it = Ring([AliasF(k) for k in range(8, 11)])
    lru_at = Ring([AliasF(k) for k in range(11, 14)])
    onS = HidView(hid, 8)

    PS = [T(nc.alloc_psum_tensor("ps%d" % i, [128, 512], F32), 1, "ps%d" % i) for i in range(8)]

    C_GMIX, C_GMLP, C_CW, C_CB, C_BR, C_BI, C_LAM, C_GAIN, C_SEL = 0, 8, 16, 48, 56, 64, 72, 80, 81
    V_CNEG, V_EPS, V_ONE, V_ZERO, V_NEGLAM, V_GAINS, V_BIASB = 0, 8, 9, 10, 11, 12, 13

    def vcol(c, n=1):
        return vecs.h[:, c:c + n]

    def dcol(c, n=1):
        return drv.h[:, c:c + n]

    RW = {}
    R_kt = [Res("ktsc%d" % i) for i in range(NSB)]
    R_v = [Res("vsc%d" % i) for i in range(NKB)]
    R_vsm = [Res("vsm%d" % i) for i in range(4)]

    S.op("pool", lambda e: e.memset(ident.h[:], 1.0), W=[ident.r])
    S.op("pool", lambda e: e.affine_select(out=ident.h[:], in_=ident.h[:], pattern=[[-1, 128]],
                                           compare_op=ALU.is_equal, fill=0.0, base=0, channel_multiplier=1),
         R=[ident.r], W=[ident.r])
    S.op("pool", lambda e: e.memset(ones.h[:], 1.0), W=[ones.r])
    S.op("pool", lambda e: e.memset(drv.h[:], 0.0), W=[drv.r])
    S.op("pool", lambda e: e.memset(drv.h[:, V_EPS:V_EPS + 1], EPS), R=[drv.r], W=[drv.r])
    S.op("pool", lambda e: e.memset(drv.h[:, V_ONE:V_ONE + 1], 1.0), R=[drv.r], W=[drv.r])
    S.op("pool", lambda e: e.memset(QA.h[:], 0.0), W=QA.u)
    S.op("pool", lambda e: e.memset(QB.h[:], 0.0), W=QB.u)
    S.op("pool", lambda e: e.memset(hist_p.h[:], 0.0), W=hist_p.u)
    S.op("pool", lambda e: e.memset(vnew.h[:], 0.0), W=[vnew.r])
    for _pz in Pz:
        S.op("pool", (lambda _pz=_pz: lambda e: e.memset(_pz.h[:], 0.0))(), W=[_pz.r])
    S.op("pool", lambda e: e.memset(hprev_p.h[:], 0.0), W=hprev_p.u)
    S.op("sp", lambda e: e.dma_start(out=vecs.h[:], in_=vecs_d), W=[vecs.r], dma=True)
    S.op("sp", lambda e: e.dma_start(out=nfb.h[:], in_=nfb_d), W=[nfb.r], dma=True)
    S.op("sp", lambda e: e.dma_start(out=lamb.h[:], in_=lamb_d), W=[lamb.r], dma=True)
    S.op("sp", lambda e: e.dma_start(out=hist_s.h[:], in_=sconvT), W=hist_s.u, dma=True)
    S.op("sp", lambda e: e.dma_start(out=hprev_s.h[:], in_=slruT), W=hprev_s.u, dma=True)
    S.op("pool", lambda e: e.dma_start(out=dmask.h[:], in_=dmask_d.rearrange("r p q -> p r q")), W=[dmask.r], dma=True)
    S.op("pool", lambda e: e.dma_start(out=wr.h[:], in_=wr_bd.rearrange("c p n -> p c n")), W=[wr.r], dma=True)
    S.op("pool", lambda e: e.dma_start(out=wi.h[:], in_=wi_bd.rearrange("c p n -> p c n")), W=[wi.r], dma=True)
    w_in_v = w_in.rearrange("(c p) n -> p c n", p=128)

    def cast_w_in(key, c0, c1):
        RW[key] = []
        for c in range(0, 8, 2):
            r = Res("%s_%d" % (key, c))
            RW[key].append(r)
            S.op("pool", (lambda c=c: lambda e: e.dma_start(out=Wsc_in[:, c:c + 2, c0:c1], in_=w_in_v[:, c:c + 2, c0:c1]))(), W=[r], dma=True)

    def cast_w(key, src, dst, kc):
        RW[key] = []
        sv = src.rearrange("(c p) n -> p c n", p=128)
        for c in range(0, kc, 4):
            r = Res("%s_%d" % (key, c))
            RW[key].append(r)
            S.op("pool", (lambda c=c: lambda e: e.dma_start(out=dst[:, c:c + 4, :], in_=sv[:, c:c + 4, :]))(), W=[r], dma=True)

    cast_w_in("kvx", 1024, 4096)

    def prologue_rest():
        cast_w_in("q", 0, 1024)
        cast_w_in("gab", 4096, 7168)
        cast_w("ba", w_ba, Wsc_ba, 8)
        cast_w("bl", w_bl, Wsc_bl, 8)
        cast_w("o", w_o, Wsc_o, 8)
        cast_w("up", w_up, Wsc_up, 8)
        cast_w("dn", w_dn, Wsc_dn, 32)

    S.stage(1)
    tA = stats.next()
    S.op("act", lambda e: e.activation(out=drv.h[:, 16:24], in_=vcol(C_LAM, 8), func=AF.Exp, scale=-1.0), R=[vecs.r, drv.r], W=[drv.r])
    S.op("act", lambda e: e.activation(out=drv.h[:, 16:24], in_=drv.h[:, 16:24], func=AF.Ln, bias=dcol(V_ONE), scale=1.0), R=[drv.r], W=[drv.r])
    S.op("dve", lambda e: e.tensor_scalar(out=drv.h[:, V_CNEG:V_CNEG + 8], in0=drv.h[:, 16:24], scalar1=-8.0, scalar2=None, op0=ALU.mult),
         R=[drv.r], W=[drv.r])
    lp = tmpF.next()
    S.op("dve", lambda e: e.tensor_tensor(out=lp.h[:, 0:128], in0=lamb.h[:, 0:128], in1=lamb.h[:, 128:256], op=ALU.mult), R=[lamb.r], W=[lp.r])
    S.op("act", lambda e: e.activation(out=lp.h[:, 128:192], in_=lp.h[:, 0:64], func=AF.Copy, accum_out=tA.h[:, 0:1]), R=[lp.r], W=[tA.r, lp.r])
    S.op("act", lambda e: e.activation(out=lp.h[:, 192:256], in_=lp.h[:, 64:128], func=AF.Copy, accum_out=tA.h[:, 1:2]), R=[lp.r, tA.r], W=[tA.r, lp.r])
    S.op("act", lambda e: e.activation(out=tA.h[:, 2:4], in_=tA.h[:, 0:2], func=AF.Exp), R=[tA.r], W=[tA.r])
    S.op("dve", lambda e: e.tensor_tensor(out=tA.h[:, 0:1], in0=tA.h[:, 3:4], in1=tA.h[:, 2:3], op=ALU.subtract), R=[tA.r], W=[tA.r])
    S.op("dve", lambda e: e.tensor_scalar(out=drv.h[:, V_NEGLAM:V_NEGLAM + 1], in0=tA.h[:, 0:1], scalar1=-LAM_INIT, scalar2=None, op0=ALU.add),
         R=[tA.r, drv.r], W=[drv.r])
    S.op("dve", lambda e: e.tensor_scalar(out=drv.h[:, V_GAINS:V_GAINS + 1], in0=vcol(C_GAIN), scalar1=1.0 - LAM_INIT, scalar2=None, op0=ALU.mult),
         R=[vecs.r, drv.r], W=[drv.r])
    for par in range(2):
        S.op("dve", (lambda par=par: lambda e: e.tensor_scalar(out=drv.h[:, V_BIASB + par:V_BIASB + par + 1], in0=vcol(C_SEL + 2 * par),
                                                               scalar1=NEG, scalar2=None, op0=ALU.mult))(),
             R=[vecs.r, drv.r], W=[drv.r])
        for ab in range(2):
            S.op("dve", (lambda par=par, ab=ab: lambda e: e.tensor_scalar(out=identM.h[:, 2 * par + ab, :], in0=ident.h[:],
                                                                         scalar1=vcol(C_SEL + 2 * par + ab), scalar2=None, op0=ALU.mult))(),
                 R=[vecs.r, ident.r, identM.r], W=[identM.r])

    ps_rot = {"i": 0}

    def next_ps(lo=0, n=4):
        key = (lo, n)
        c = ps_rot.get(key, 0)
        ps_rot[key] = c + 1
        return PS[lo + (c % n)]

    def wres(wb):
        return wb.rs if hasattr(wb, "rs") else [wb.r]

    class WBuf:
        def __init__(self, h, rs):
            self.h, self.rs = h, rs

    def wload(scr_ap, res, wb=None):
        if wb is None:
            wb = wring.next()
        S.op("sp", lambda e: e.dma_start(out=wb.h[:, :, :], in_=scr_ap), R=res, W=wres(wb), dma=True)
        return wb

    wb_T8d = WBuf(T8d.h, T8d.u)
    wb_xr = [WBuf(xres.h[:, 2 * k:2 * k + 2, :].rearrange("p a b -> p (a b)").bitcast(BF16).rearrange("p (c n) -> p c n", c=8),
                  [xres.u[2 * k], xres.u[2 * k + 1]]) for k in range(2)]

    def chain_fm(ps, wb, ocl, xT, NT):
        def fn(e):
            for kc in range(8):
                ins = e.matmul(ps.h[:, 0:NT], lhsT=wb.h[:, kc, ocl * 128:(ocl + 1) * 128], rhs=xT.ap(kc, 0, NT),
                               start=(kc == 0), stop=(kc == 7))
            return ins
        S.op("pe", fn, R=wres(wb) + xT.allres(), W=[ps.r])

    def chain_tm(ps, wb, xT, t0, TT):
        def fn(e):
            for kc in range(8):
                ins = e.matmul(ps.h[0:TT, 0:512], lhsT=xT.ap(kc, t0, t0 + TT), rhs=wb.h[:, kc, :],
                               start=(kc == 0), stop=(kc == 7))
            return ins
        S.op("pe", fn, R=wres(wb) + xT.allres(), W=[ps.r])

    def rstd_of(src_ap, src_res, width, jk):
        st = stats.next()
        S.op("act", lambda e: e.activation(out=jk.h[:, 0:width], in_=src_ap, func=AF.Square, accum_out=st.h[:, 0:1]),
             R=src_res, W=[st.r, jk.r])
        S.op("act", lambda e: e.activation(out=st.h[:, 1:2], in_=st.h[:, 0:1], func=AF.Sqrt, scale=1.0 / width, bias=dcol(V_EPS)),
             R=[st.r, drv.r], W=[st.r])
        S.op("dve", lambda e: e.reciprocal(out=st.h[:, 2:3], in_=st.h[:, 1:2]), R=[st.r], W=[st.r])
        return st

    def norm_T(src_rows, NT, gcol0, dstT, keep):
        for t in range(NT // 128):
            if src_rows is not None and not keep:
                xtile = xtiles.next()
                xt_ap, xt_res = xtile.h[:], [xtile.r]
            else:
                xt_ap, xt_res = xres.h[:, t, :], [xres.u[t]]
            if src_rows is not None:
                S.op("sp", (lambda xt_ap=xt_ap, t=t: lambda e: e.dma_start(out=xt_ap, in_=src_rows[t * 128:(t + 1) * 128, :]))(),
                     W=xt_res, dma=True)
            xn = xns.next()
            st = rstd_of(xt_ap, xt_res, D, xn)
            S.op("dve", (lambda xn=xn, xt_ap=xt_ap, st=st: lambda e: e.tensor_scalar(out=xn.h[:], in0=xt_ap, scalar1=st.h[:, 2:3], scalar2=None, op0=ALU.mult))(),
                 R=xt_res + [st.r], W=[xn.r])
            ps = next_ps(4, 2)
            psb = ps.h[:].bitcast(BF16)

            def fn(e, xn=xn, psb=psb):
                for c in range(8):
                    ins = e.transpose(out=psb[:, c * 128:(c + 1) * 128], in_=xn.h[:, c * 128:(c + 1) * 128], identity=ident.h[:])
                return ins
            S.op("pe", fn, R=[xn.r, ident.r], W=[ps.r])
            S.op("dve", (lambda psb=psb, t=t: lambda e: e.tensor_tensor(
                out=dstT.h[:, :, t * 128:(t + 1) * 128], in0=psb[:, 0:1024].rearrange("p (c t) -> p c t", c=8),
                in1=vecs.h[:, gcol0:gcol0 + 8].unsqueeze(2).to_broadcast([128, 8, 128]), op=ALU.mult))(),
                 R=[ps.r, vecs.r], W=dstT.u)

    def phaseA(src_rows, NT, nseq, hist, hprev, selcol, first_slot, kdst, vdst, ktsc_dst, vsc_dst, TT):
        L = NT // nseq
        S.stage(2)
        norm_T(src_rows, NT, C_GMIX, T8a, keep=False)
        S.stage(3)
        wX = [wload(Wsc_in[:, :, 3072 + n * 512:3072 + (n + 1) * 512], RW["kvx"], wb_xr[n]) for n in range(2)]
        wK = [wload(Wsc_in[:, :, 1024 + n * 512:1024 + (n + 1) * 512], RW["kvx"]) for n in range(2)]
        wV = [wload(Wsc_in[:, :, 2048:2560], RW["kvx"]), wload(Wsc_in[:, :, 2560:3072], RW["kvx"], wb_T8d)]
        items = []

        def it_kT(oc):
            ps = next_ps(5, 3)
            chain_fm(ps, wK[oc // 4], oc % 4, T8a, NT)

            def ev():
                S.op("dve", lambda e: e.tensor_copy(out=T8b.h[:, oc, 0:NT], in_=ps.h[:, 0:NT]), R=[ps.r], W=[T8b.u[oc]])
                if oc == 7 and ktsc_dst is not None:
                    S.op("pool", lambda e: e.dma_start(out=ktsc_dst[0], in_=T8b.h[:, :, 0:NT]), R=T8b.u, W=[ktsc_dst[1]], dma=True)
            return ev

        def it_tok(t, wW, ring, dst, isv):
            pss = []
            for n in range(2):
                ps = next_ps(5, 3)
                chain_tm(ps, wW[n], T8a, t * TT, TT)
                pss.append(ps)

            def ev():
                o_ = ring.next()
                S.op("act", lambda e: e.activation(out=o_.h[0:TT, 0:512], in_=pss[0].h[0:TT, :], func=AF.Copy), R=[pss[0].r], W=[o_.r])
                S.op("dve", lambda e: e.tensor_copy(out=o_.h[0:TT, 512:1024], in_=pss[1].h[0:TT, :]), R=[pss[1].r, o_.r], W=[o_.r])
                S.op("pool", lambda e: e.dma_start(out=dst(t), in_=o_.h[0:TT, :]), R=[o_.r], dma=True)
                if isv:
                    vb = vbs.next()
                    S.op("dve", lambda e: e.tensor_copy(out=vb.h[0:TT, :], in_=o_.h[0:TT, :]), R=[o_.r], W=[vb.r])
                    dst_ap, dst_res = vsc_dst(t)
                    S.op("pool", lambda e: e.dma_start(out=dst_ap, in_=vb.h[0:TT, :]), R=[vb.r], W=[dst_res], dma=True)
            return ev

        ntok = NT // TT
        tok_items = []
        for t in range(ntok):
            tok_items.append((2, (lambda t=t: it_tok(t, wK, kouts, kdst, False))))
            tok_items.append((2, (lambda t=t: it_tok(t, wV, vouts, vdst, True))))
        kt_items = [(1, (lambda oc=oc: it_kT(oc))) for oc in range(8)]
        while kt_items or tok_items:
            if kt_items:
                items.append(kt_items.pop(0))
            if tok_items:
                items.append(tok_items.pop(0))

        st_ = {}

        def p0(cc):
            ps = next_ps(0, 5)
            chain_fm(ps, wX[cc // 4], cc % 4, T8a, NT)
            st_[cc] = dict(ps=ps)

        def p1(cc):
            d_ = st_[cc]
            ps = d_["ps"]
            xp = tmpX.next()
            xp3 = xp.h[:, 0:nseq * (L + 3)].rearrange("p (s l) -> p s l", s=nseq)
            S.op("dve", lambda e: e.tensor_copy(out=xp3[:, :, 0:3], in_=hist.h[:, cc, :, :]), R=[hist.u[cc]], W=[xp.r])
            S.op("act", lambda e: e.activation(out=xp3[:, :, 3:3 + L], in_=ps.h[:, 0:NT].rearrange("p (s l) -> p s l", s=nseq), func=AF.Copy),
                 R=[ps.r, xp.r], W=[xp.r])
            xc = lru_xc.next()
            xc3 = xc.h[:, 0:NT].rearrange("p (s l) -> p s l", s=nseq)
            d_.update(xp=xp, xp3=xp3, xc=xc, xc3=xc3)

        def p2(cc):
            d_ = st_[cc]
            xp, xp3, xc, xc3 = d_["xp"], d_["xp3"], d_["xc"], d_["xc3"]
            S.op("dve", lambda e: e.tensor_copy(out=hist.h[:, cc, :, :], in_=xp3[:, :, L:L + 3]), R=[xp.r], W=[hist.u[cc]])
            S.op("dve", lambda e: e.tensor_scalar(out=xc3, in0=xp3[:, :, 0:L], scalar1=vcol(C_CW + cc * 4 + 0), scalar2=vcol(C_CB + cc), op0=ALU.mult, op1=ALU.add),
                 R=[xp.r, vecs.r], W=xc.rs)
            for jj in range(1, 4):
                S.op("dve", (lambda jj=jj: lambda e: e.scalar_tensor_tensor(
                    out=xc3, in0=xp3[:, :, jj:jj + L], scalar=vcol(C_CW + cc * 4 + jj), in1=xc3, op0=ALU.mult, op1=ALU.add))(),
                     R=[xp.r, vecs.r] + xc.rs, W=xc.rs)
            xcb = tmpB.next()
            S.op("dve", lambda e: e.tensor_copy(out=xcb.h[:, 0:NT], in_=xc.h[:, 0:NT]), R=xc.rs, W=[xcb.r])
            psr = next_ps(0, 5)
            psi = next_ps(0, 5)
            S.op("pe", lambda e: e.matmul(psr.h[:, 0:NT], lhsT=wr.h[:, cc, :], rhs=xcb.h[:, 0:NT], start=True, stop=True), R=[wr.r, xcb.r], W=[psr.r])
            S.op("pe", lambda e: e.matmul(psi.h[:, 0:NT], lhsT=wi.h[:, cc, :], rhs=xcb.h[:, 0:NT], start=True, stop=True), R=[wi.r, xcb.r], W=[psi.r])
            d_["psr"], d_["psi"] = psr, psi

        def p3(cc):
            d_ = st_[cc]
            psr, psi = d_["psr"], d_["psi"]
            rt_, it_, at_ = lru_rt.next(), lru_it.next(), lru_at.next()
            S.op("act", lambda e: e.activation(out=rt_.h[:, 0:NT], in_=psr.h[:, 0:NT], func=AF.Sigmoid, bias=vcol(C_BR + cc), scale=1.0),
                 R=[psr.r, vecs.r], W=rt_.rs)
            S.op("act", lambda e: e.activation(out=it_.h[:, 0:NT], in_=psi.h[:, 0:NT], func=AF.Sigmoid, bias=vcol(C_BI + cc), scale=1.0),
                 R=[psi.r, vecs.r], W=it_.rs)
            S.op("act", lambda e: e.activation(out=at_.h[:, 0:NT], in_=rt_.h[:, 0:NT], func=AF.Exp, scale=dcol(V_CNEG + cc)),
                 R=rt_.rs + [drv.r], W=at_.rs)
            S.op("dve", lambda e: e.tensor_tensor(out=rt_.h[:, 0:NT], in0=at_.h[:, 0:NT], in1=at_.h[:, 0:NT], op=ALU.mult), R=at_.rs + rt_.rs, W=rt_.rs)
            d_["rt"], d_["it"], d_["at"] = rt_, it_, at_

        def p4(cc):
            rt_ = st_[cc]["rt"]
            S.op("act", lambda e: e.activation(out=rt_.h[:, 0:NT], in_=rt_.h[:, 0:NT], func=AF.Ln, scale=-1.0, bias=dcol(V_ONE)),
                 R=rt_.rs + [drv.r], W=rt_.rs)
            S.op("act", lambda e: e.activation(out=rt_.h[:, 0:NT], in_=rt_.h[:, 0:NT], func=AF.Exp, scale=0.5), R=rt_.rs, W=rt_.rs)

        def p5(cc):
            d_ = st_[cc]
            xc, rt_, it_, at_ = d_["xc"], d_["rt"], d_["it"], d_["at"]
            S.op("dve", lambda e: e.tensor_tensor(out=it_.h[:, 0:NT], in0=it_.h[:, 0:NT], in1=xc.h[:, 0:NT], op=ALU.mult), R=it_.rs + xc.rs, W=it_.rs)
            S.op("dve", lambda e: e.tensor_tensor(out=it_.h[:, 0:NT], in0=it_.h[:, 0:NT], in1=rt_.h[:, 0:NT], op=ALU.mult), R=it_.rs + rt_.rs, W=it_.rs)
            ht = rt_
            for s in range(nseq):
                S.op("dve", (lambda s=s: lambda e: e.tensor_tensor_scan(
                    out=ht.h[:, s * L:(s + 1) * L], data0=at_.h[:, s * L:(s + 1) * L], data1=it_.h[:, s * L:(s + 1) * L],
                    initial=hprev.h[:, cc, s:s + 1], op0=ALU.mult, op1=ALU.add))(),
                     R=at_.rs + it_.rs + [hprev.u[cc]] + ht.rs, W=ht.rs)
            S.op("dve", lambda e: e.tensor_copy(out=hprev.h[:, cc, :], in_=ht.h[:, 0:NT].rearrange("p (s l) -> p s l", s=nseq)[:, :, L - 1]),
                 R=ht.rs, W=[hprev.u[cc]])
            if first_slot:
                S.op("dve", lambda e: e.tensor_scalar(out=ysel.h[:, cc, 0:NT], in0=ht.h[:, 0:NT], scalar1=vcol(selcol), scalar2=None, op0=ALU.mult),
                     R=ht.rs + [vecs.r], W=[ysel.u[cc]])
            else:
                S.op("dve", lambda e: e.scalar_tensor_tensor(out=ysel.h[:, cc, 0:NT], in0=ht.h[:, 0:NT], scalar=vcol(selcol),
                                                             in1=ysel.h[:, cc, 0:NT], op0=ALU.mult, op1=ALU.add),
                     R=ht.rs + [vecs.r, ysel.u[cc]], W=[ysel.u[cc]])

        stages = [p0, p1, p2, p3, p4, p5]
        niter = 8 + len(stages) - 1
        pend_ev = []
        for t in range(niter):
            for k in range(len(stages)):
                cc = t - k
                if 0 <= cc < 8:
                    stages[k](cc)
            for ev in pend_ev:
                ev()
            pend_ev = []
            budget = 3
            while items and items[0][0] <= budget:
                n_, f_ = items.pop(0)
                budget -= n_
                pend_ev.append(f_())
        while items or pend_ev:
            for ev in pend_ev:
                ev()
            pend_ev = []
            budget = 3
            while items and items[0][0] <= budget:
                n_, f_ = items.pop(0)
                budget -= n_
                pend_ev.append(f_())

    class Attn:
        LOOK = 3

        def __init__(self, h, NQ, qoff, nblocks):
            self.h, self.NQ, self.qoff, self.nb, self.i = h, NQ, qoff, nblocks, 0
            self.si = 0
            self.pend = []

        def _flush(self, keep):
            while len(self.pend) > keep:
                self.pend.pop(0)()

        def block(self, kt_ap, v_ap, nk, Rk, Rv, mask=None):
            h, NQ, qoff = self.h, self.NQ, self.qoff
            first, last = (self.i == 0), (self.i == self.nb - 1)
            self.i += 1
            for c in range(2):
                Sb = PS[self.si % 4]
                self.si += 1
                Q = QA if c == 0 else QB

                def fn(e, Sb=Sb, Q=Q):
                    ins = e.matmul(Sb.h[0:nk, 0:NQ], lhsT=kt_ap, rhs=Q.h[:, h, qoff:qoff + NQ], start=True, stop=(mask is None))
                    if mask is not None:
                        ins = e.matmul(Sb.h[0:nk, 0:NQ], lhsT=mask[0], rhs=mask[1], start=False, stop=True)
                    return ins
                S.op("pe", fn, R=Rk + [Q.u[h]] + ([identM.r, dmask.r] if mask is not None else []), W=[Sb.r])
                pad = nk < 128
                Pc = Pz[c] if pad else tmpB.next()
                bias_ap = mask[2] if mask is not None else dcol(V_ZERO)
                S.op("act", (lambda Pc=Pc, Sb=Sb, bias_ap=bias_ap: lambda e: e.activation(out=Pc.h[0:nk, 0:NQ], in_=Sb.h[0:nk, 0:NQ], func=AF.Exp,
                                                                                       scale=0.125, bias=bias_ap[0:nk, :]))(),
                     R=[Sb.r, drv.r] + ([Pc.r] if pad else []), W=[Pc.r])
                Ob, Lb = PS[4 + c], PS[6 + c]

                def pv(Pc=Pc, Ob=Ob, Lb=Lb, first=first, last=last):
                    def fn2(e):
                        e.matmul(Ob.h[:, 0:NQ], lhsT=v_ap, rhs=Pc.h[:, 0:NQ], start=first, stop=last)
                        return e.matmul(Lb.h[:, 0:NQ], lhsT=ones.h[:], rhs=Pc.h[:, 0:NQ], start=first, stop=last)
                    S.op("pe", fn2, R=Rv + [Pc.r, ones.r], W=[Ob.r, Lb.r])
                self.pend.append(pv)
                self._flush(self.LOOK)

        def finish(self, dstV):
            self._flush(0)
            h, NQ, qoff = self.h, self.NQ, self.qoff
            o1, o2, l1, l2 = tmpF.next(), tmpF.next(), tmpF.next(), tmpF.next()
            S.op("dve", lambda e: e.tensor_copy(out=l1.h[:, 0:NQ], in_=PS[6].h[:, 0:NQ]), R=[PS[6].r], W=[l1.r])
            S.op("dve", lambda e: e.tensor_copy(out=o1.h[:, 0:NQ], in_=PS[4].h[:, 0:NQ]), R=[PS[4].r], W=[o1.r])
            S.op("dve", lambda e: e.tensor_copy(out=l2.h[:, 0:NQ], in_=PS[7].h[:, 0:NQ]), R=[PS[7].r], W=[l2.r])
            S.op("dve", lambda e: e.tensor_copy(out=o2.h[:, 0:NQ], in_=PS[5].h[:, 0:NQ]), R=[PS[5].r], W=[o2.r])
            S.op("dve", lambda e: e.reciprocal(out=l1.h[:, 0:NQ], in_=l1.h[:, 0:NQ]), R=[l1.r], W=[l1.r])
            S.op("dve", lambda e: e.reciprocal(out=l2.h[:, 0:NQ], in_=l2.h[:, 0:NQ]), R=[l2.r], W=[l2.r])
            S.op("dve", lambda e: e.tensor_tensor(out=o1.h[:, 0:NQ], in0=o1.h[:, 0:NQ], in1=l1.h[:, 0:NQ], op=ALU.mult), R=[o1.r, l1.r], W=[o1.r])
            S.op("dve", lambda e: e.tensor_tensor(out=o2.h[:, 0:NQ], in0=o2.h[:, 0:NQ], in1=l2.h[:, 0:NQ], op=ALU.mult), R=[o2.r, l2.r], W=[o2.r])
            t1 = o1
            S.op("dve", lambda e: e.scalar_tensor_tensor(out=t1.h[:, 0:NQ], in0=o2.h[:, 0:NQ], scalar=dcol(V_NEGLAM), in1=o1.h[:, 0:NQ], op0=ALU.mult, op1=ALU.add),
                 R=[o1.r, o2.r, drv.r], W=[t1.r])
            sq = xns.next()
            S.op("dve", lambda e: e.tensor_tensor(out=sq.h[:, 0:NQ], in0=t1.h[:, 0:NQ], in1=t1.h[:, 0:NQ], op=ALU.mult), R=[t1.r], W=[sq.r])
            Mb = PS[self.si % 4]
            self.si += 1

            def tail():
                S.op("pe", lambda e: e.matmul(Mb.h[:, 0:NQ], lhsT=ones.h[:], rhs=sq.h[:, 0:NQ], start=True, stop=True), R=[sq.r, ones.r], W=[Mb.r])
                r1 = l1
                S.op("dve", lambda e: e.tensor_scalar(out=r1.h[:, 0:NQ], in0=Mb.h[:, 0:NQ], scalar1=1.0 / 128, scalar2=EPS, op0=ALU.mult, op1=ALU.add), R=[Mb.r, r1.r], W=[r1.r])
                S.op("act", lambda e: e.activation(out=r1.h[:, 0:NQ], in_=r1.h[:, 0:NQ], func=AF.Ln), R=[r1.r], W=[r1.r])
                S.op("act", lambda e: e.activation(out=r1.h[:, 0:NQ], in_=r1.h[:, 0:NQ], func=AF.Exp, scale=-0.5), R=[r1.r], W=[r1.r])
                S.op("dve", lambda e: e.scalar_tensor_tensor(out=dstV.ap(h, qoff, qoff + NQ), in0=t1.h[:, 0:NQ], scalar=dcol(V_GAINS), in1=r1.h[:, 0:NQ],
                                                             op0=ALU.mult, op1=ALU.mult),
                     R=[t1.r, r1.r, drv.r], W=[dstV.res(h)])
            return tail

    def attend_prompt(j):
        par = j % 2
        tail = None
        for h in range(8):
            at = Attn(h, 512, 0, (j + 1) * 8)
            nblk = 0
            for jj in range(j + 1):
                kst, vst = ksts.next(), vsts.next()
                S.op("sp", (lambda kst=kst, jj=jj, h=h: lambda e: e.dma_start(out=kst.h[:], in_=KTsc[:, h, jj * 1024:(jj + 1) * 1024]))(),
                     R=[R_kt[2 * jj], R_kt[2 * jj + 1]], W=[kst.r], dma=True)
                S.op("sp", (lambda vst=vst, jj=jj, h=h: lambda e: e.dma_start(out=vst.h[:], in_=Vsc[:, jj * 8:(jj + 1) * 8, h * 128:(h + 1) * 128]))(),
                     R=[R_v[jj * 8 + k] for k in range(8)], W=[vst.r], dma=True)
                for kb in range(8):
                    mask = None
                    if jj == j:
                        ab = kb // 4
                        bias_ap = dcol(V_ZERO) if ab == 0 else dcol(V_BIASB + par)
                        mask = (identM.h[:, 2 * par + ab, :], dmask.h[:, kb % 4, :], bias_ap)
                    at.block(kst.h[:, kb * 128:(kb + 1) * 128], vst.h[:, kb, :], 128, [kst.r], [vst.r], mask)
                    nblk += 1
                    if nblk == 6 and tail is not None:
                        tail()
                        tail = None
            tail = at.finish(T8b)
        tail()

    def attend_sample():
        stail = [None]
        for s in range(4):
            S.op("sp", (lambda s=s: lambda e: e.dma_start(out=vnew.h[0:64, :], in_=Vsm_sc[s]))(), R=[R_vsm[s], vnew.r], W=[vnew.r], dma=True)
            for h in range(8):
                stg, vst, kst = vsts.next(), vsts.next(), ksts.next()
                S.op("pool", (lambda stg=stg, s=s, h=h: lambda e: e.dma_start(out=stg.h[:], in_=ck[s, :, h * 128:(h + 1) * 128].rearrange("(kb p) c -> p kb c", p=128)))(),
                     W=[stg.r], dma=True)
                S.op("pool", (lambda vst=vst, s=s, h=h: lambda e: e.dma_start(out=vst.h[:], in_=cv[s, :, h * 128:(h + 1) * 128].rearrange("(kb p) c -> p kb c", p=128)))(),
                     W=[vst.r], dma=True)
                ps = next_ps(0, 4)
                psb = ps.h[:].bitcast(BF16)

                def fn(e, stg=stg, psb=psb):
                    for kb in range(8):
                        ins = e.transpose(out=psb[:, kb * 128:(kb + 1) * 128], in_=stg.h[:, kb, :], identity=ident.h[:])
                    return ins
                S.op("pe", fn, R=[stg.r, ident.r], W=[ps.r])
                S.op("dve", (lambda kst=kst, psb=psb: lambda e: e.tensor_copy(out=kst.h[:], in_=psb[:, 0:1024]))(), R=[ps.r], W=[kst.r])
                at = Attn(h, 64, s * 64, 9)
                for kb in range(8):
                    at.block(kst.h[:, kb * 128:(kb + 1) * 128], vst.h[:, kb, :], 128, [kst.r], [vst.r], None)
                    if kb == 5 and stail[0] is not None:
                        stail[0]()
                        stail[0] = None
                at.block(T8b.h[:, h, s * 64:(s + 1) * 64], vnew.h[:, h * 128:(h + 1) * 128], 64, [T8b.u[h]], [vnew.r], None)
                stail[0] = at.finish(onS)
        stail[0]()

    def phaseB(src_rows, NT, attend, onV, ydst):
        S.stage(6)
        norm_T(src_rows, NT, C_GMIX, T8a, keep=True)
        wQ = [wload(Wsc_in[:, :, n * 512:(n + 1) * 512], RW["q"]) for n in range(2)]
        for oc in range(8):
            ps = next_ps(0, 4)
            chain_fm(ps, wQ[oc // 4], oc % 4, T8a, NT)
            S.op("dve", (lambda ps=ps, oc=oc: lambda e: e.tensor_copy(out=QA.h[0:64, oc, 0:NT], in_=ps.h[0:64, 0:NT]))(), R=[ps.r], W=[QA.u[oc]])
            S.op("act", (lambda ps=ps, oc=oc: lambda e: e.activation(out=QB.h[64:128, oc, 0:NT], in_=ps.h[64:128, 0:NT], func=AF.Copy))(), R=[ps.r], W=[QB.u[oc]])
        wG = [wload(Wsc_in[:, :, 4096 + n * 512:4096 + (n + 1) * 512], RW["gab"]) for n in range(2)]
        for oc in range(8):
            ps = next_ps(0, 4)
            chain_fm(ps, wG[oc // 4], oc % 4, T8a, NT)
            gt = tmpF.next()
            S.op("act", (lambda ps=ps, gt=gt: lambda e: e.activation(out=gt.h[:, 0:NT], in_=ps.h[:, 0:NT], func=AF.Gelu_apprx_tanh))(), R=[ps.r], W=[gt.r])
            S.op("dve", (lambda gt=gt, oc=oc: lambda e: e.tensor_tensor(out=T8d.h[:, oc, 0:NT], in0=gt.h[:, 0:NT], in1=ysel.h[:, oc, 0:NT], op=ALU.mult))(),
                 R=[gt.r, ysel.u[oc]], W=[T8d.u[oc]])
        S.stage(7)
        attend()
        S.stage(8)
        for n in range(2):
            wGA = wload(Wsc_in[:, :, 5120 + n * 512:5120 + (n + 1) * 512], RW["gab"])
            wBA = wload(Wsc_ba[:, :, n * 512:(n + 1) * 512], RW["ba"])
            parts = []
            for ocl in range(4):
                psa = next_ps(0, 4)
                chain_fm(psa, wGA, ocl, T8a, NT)
                sa = tmpF.next()
                S.op("act", (lambda psa=psa, sa=sa: lambda e: e.activation(out=sa.h[:, 0:NT], in_=psa.h[:, 0:NT], func=AF.Sigmoid))(), R=[psa.r], W=[sa.r])
                psA = next_ps(0, 4)
                chain_fm(psA, wBA, ocl, onV, NT)
                S.op("dve", (lambda psA=psA, sa=sa: lambda e: e.tensor_tensor(out=sa.h[:, 0:NT], in0=psA.h[:, 0:NT], in1=sa.h[:, 0:NT], op=ALU.mult))(),
                     R=[psA.r, sa.r], W=[sa.r])
                parts.append(sa)
            wGB = wload(Wsc_in[:, :, 6144 + n * 512:6144 + (n + 1) * 512], RW["gab"])
            wBL = wload(Wsc_bl[:, :, n * 512:(n + 1) * 512], RW["bl"])
            for ocl in range(4):
                oc = n * 4 + ocl
                psg = next_ps(0, 4)
                chain_fm(psg, wGB, ocl, T8a, NT)
                sbt = tmpF.next()
                S.op("act", (lambda psg=psg, sbt=sbt: lambda e: e.activation(out=sbt.h[:, 0:NT], in_=psg.h[:, 0:NT], func=AF.Sigmoid))(), R=[psg.r], W=[sbt.r])
                psL = next_ps(0, 4)
                chain_fm(psL, wBL, ocl, T8d, NT)
                S.op("dve", (lambda psL=psL, sbt=sbt: lambda e: e.tensor_tensor(out=sbt.h[:, 0:NT], in0=psL.h[:, 0:NT], in1=sbt.h[:, 0:NT], op=ALU.mult))(),
                     R=[psL.r, sbt.r], W=[sbt.r])
                sa = parts[ocl]
                S.op("dve", (lambda sa=sa, sbt=sbt, oc=oc: lambda e: e.tensor_tensor(out=mergedV.ap(oc, 0, NT), in0=sa.h[:, 0:NT], in1=sbt.h[:, 0:NT], op=ALU.add))(),
                     R=[sa.r, sbt.r], W=[mergedV.res(oc)])
        S.stage(9)
        wO = [wload(Wsc_o[:, :, n * 512:(n + 1) * 512], RW["o"]) for n in range(2)]
        for t in range(NT // 128):
            for n in range(2):
                ps = next_ps(6, 2)
                chain_tm(ps, wO[n], mergedV, t * 128, 128)
                S.op("dve", (lambda ps=ps, t=t, n=n: lambda e: e.tensor_tensor(out=xres.h[:, t, n * 512:(n + 1) * 512], in0=ps.h[:, 0:512],
                                                                            in1=xres.h[:, t, n * 512:(n + 1) * 512], op=ALU.add))(),
                     R=[ps.r, xres.u[t]], W=[xres.u[t]])
        S.stage(10)
        norm_T(None, NT, C_GMLP, T8a, keep=True)
        for ob in range(8):
            wU = wload(Wsc_up[:, :, ob * 512:(ob + 1) * 512], RW["up"])
            for ocl in range(4):
                oc = ob * 4 + ocl
                ps = next_ps(0, 4)
                chain_fm(ps, wU, ocl, T8a, NT)
                rl = tmpF.next()
                S.op("act", (lambda ps=ps, rl=rl: lambda e: e.activation(out=rl.h[:, 0:NT], in_=ps.h[:, 0:NT], func=AF.Relu))(), R=[ps.r], W=[rl.r])
                S.op("pool", (lambda rl=rl, oc=oc: lambda e: e.tensor_tensor(out=hid.h[:, oc * 512:oc * 512 + NT], in0=rl.h[:, 0:NT], in1=rl.h[:, 0:NT], op=ALU.mult))(),
                     R=[rl.r], W=[hid.u[oc]])
        ntt = NT // 128
        for n in range(2):
            for kg in range(4):
                wD = wload(Wsc_dn[:, kg * 8:(kg + 1) * 8, n * 512:(n + 1) * 512], RW["dn"])
                for t in range(ntt):
                    ps = PS[4 + t]

                    def fn(e, ps=ps, t=t, kg=kg, wD=wD):
                        for k8 in range(8):
                            kc = kg * 8 + k8
                            ins = e.matmul(ps.h[:, 0:512], lhsT=hid.h[:, kc * 512 + t * 128:kc * 512 + (t + 1) * 128], rhs=wD.h[:, k8, :],
                                           start=(kc == 0), stop=(kc == 31))
                        return ins
                    S.op("pe", fn, R=[wD.r] + hid.u[kg * 8:(kg + 1) * 8], W=[ps.r])
            for t in range(ntt):
                ps = PS[4 + t]
                S.op("dve", (lambda ps=ps, t=t, n=n: lambda e: e.tensor_tensor(out=xres.h[:, t, n * 512:(n + 1) * 512], in0=ps.h[:, 0:512],
                                                                            in1=xres.h[:, t, n * 512:(n + 1) * 512], op=ALU.add))(),
                     R=[ps.r, xres.u[t]], W=[xres.u[t]])
        S.stage(11)
        for t in range(ntt):
            jk = xns.next()
            st = rstd_of(xres.h[:, t, :], [xres.u[t]], D, jk)
            S.op("dve", (lambda st=st, t=t: lambda e: e.scalar_tensor_tensor(out=xres.h[:, t, :], in0=xres.h[:, t, :], scalar=st.h[:, 2:3], in1=nfb.h[:],
                                                                            op0=ALU.mult, op1=ALU.mult))(),
                 R=[st.r, nfb.r, xres.u[t]], W=[xres.u[t]])
            S.op("pool", (lambda t=t: lambda e: e.dma_start(out=ydst[t * 128:(t + 1) * 128, :], in_=xres.h[:, t, :]))(), R=[xres.u[t]], dma=True)

    rest_done = False
    if do_prompt:
        for j in range(NP):
            for slot in range(2):
                i = 2 * j + slot
                phaseA(xs[i * 512:(i + 1) * 512, :], 512, 1, hist_p, hprev_p, C_SEL + 2 * (j % 2) + slot, slot == 0,
                       kdst=lambda t, i=i: k_all[i * 512 + t * 128:i * 512 + (t + 1) * 128, :],
                       vdst=lambda t, i=i: v_all[i * 512 + t * 128:i * 512 + (t + 1) * 128, :],
                       ktsc_dst=(KTsc[:, :, i * 512:(i + 1) * 512], R_kt[i]),
                       vsc_dst=lambda t, i=i: (Vsc[:, i * 4 + t, :], R_v[i * 4 + t]), TT=128)
                if not rest_done:
                    S.stage(5.5)
                    prologue_rest()
                    rest_done = True
            phaseB(xo[j * 512:(j + 1) * 512, :], 512, (lambda j=j: attend_prompt(j)), T8b, y_own[j * 512:(j + 1) * 512, :])
        S.op("pool", lambda e: e.dma_start(out=conv_p, in_=hist_p.h[:, :, 0, :]), R=hist_p.u, dma=True)
        S.op("pool", lambda e: e.dma_start(out=lru_p, in_=hprev_p.h[:, :, 0]), R=hprev_p.u, dma=True)
    if do_sample:
        phaseA(xsm, 256, 4, hist_s, hprev_s, C_SEL + 4, True,
               kdst=lambda t: k_smp[t * 64:(t + 1) * 64, :], vdst=lambda t: v_smp[t * 64:(t + 1) * 64, :],
               ktsc_dst=None, vsc_dst=lambda t: (Vsm_sc[t], R_vsm[t]), TT=64)
        if not rest_done:
            prologue_rest()
            rest_done = True
        phaseB(xsm, 256, attend_sample, onS, y_smp)
        S.op("pool", lambda e: e.dma_start(out=conv_s, in_=hist_s.h[:]), R=hist_s.u, dma=True)
        S.op("pool", lambda e: e.dma_start(out=lru_s, in_=hprev_s.h[:]), R=hprev_s.u, dma=True)

    S.finish()
    from contextlib import ExitStack
    with ExitStack() as es:
        for i in range(len(S.sems)):
            S.sems[i] = es.enter_context(nc.semaphore("s%d" % i))
        block = es.enter_context(nc.Block())

        @block.tensor
        def _(e):
            S.replay("pe", e)

        @block.scalar
        def _(e):
            S.replay("act", e)

        @block.vector
        def _(e):
            S.replay("dve", e)

        @block.gpsimd
        def _(e):
            S.replay("pool", e)

        @block.sync
        def _(e):
            S.replay("sp", e)
    return nc, S


def _colvec(v):
    return np.ascontiguousarray(np.asarray(v, np.float32).reshape(8, 128).T)


def _own_index(half, j):
    return 2 * j + ((half + j) % 2)


def make_in_maps(inp, NP, n_cores=8):
    f = lambda a: np.ascontiguousarray(np.asarray(a, np.float32))
    x_prompt, x_sample = f(inp["x_prompt"]), f(inp["x_sample"])
    ck_all = f(inp["cache_k"])[0].reshape(32, 1024, D)
    cv_all = f(inp["cache_v"])[0].reshape(32, 1024, D)
    sconv, slru = f(inp["state_conv"])[0], f(inp["state_lru"])[0]
    w_r, w_i = f(inp["w_rgate"])[0], f(inp["w_igate"])[0]
    wr_bd = np.zeros((8, 128, 128), np.float32)
    wi_bd = np.zeros((8, 128, 128), np.float32)
    for cc in range(8):
        for k in range(2):
            wr_bd[cc, k * 64:(k + 1) * 64, k * 64:(k + 1) * 64] = w_r[2 * cc + k]
            wi_bd[cc, k * 64:(k + 1) * 64, k * 64:(k + 1) * 64] = w_i[2 * cc + k]
    nfb = np.ascontiguousarray(np.broadcast_to(f(inp["norm_final"])[None, :], (128, D)))
    lamb = np.ascontiguousarray(np.broadcast_to(
        np.concatenate([f(inp["lambda_q"])[0].reshape(-1), f(inp["lambda_k"])[0].reshape(-1)])[None, :], (128, 256)))
    dmask = np.zeros((4, 128, 512), np.float32)
    kk = np.arange(128)[:, None]
    qq = np.arange(512)[None, :]
    for r in range(4):
        dmask[r] = np.where((2 * r + kk // 64) <= (qq // 64), 0.0, NEG)
    shared = dict(
        w_in=f(inp["w_in"])[0], w_ba=f(inp["w_branch_attn"])[0], w_bl=f(inp["w_branch_lru"])[0], w_o=f(inp["w_out"])[0],
        w_up=f(inp["w_mlp_up"])[0], w_dn=f(inp["w_mlp_down"])[0], wr_bd=wr_bd, wi_bd=wi_bd, nfb=nfb, lamb=lamb, dmask=dmask)
    vbase = np.zeros((128, 93), np.float32)
    vbase[:, 0:8] = _colvec(inp["norm_mix"][0])
    vbase[:, 8:16] = _colvec(inp["norm_mlp"][0])
    cw = f(inp["conv_w"])[0]
    for jj in range(4):
        vbase[:, 16 + jj:48:4] = _colvec(cw[jj])
    vbase[:, 48:56] = _colvec(inp["conv_b"][0])
    vbase[:, 56:64] = _colvec(inp["b_rgate"][0])
    vbase[:, 64:72] = _colvec(inp["b_igate"][0])
    vbase[:, 72:80] = _colvec(inp["lru_lambda"][0])
    vbase[:, 80] = f(inp["head_gain"])[0]
    vbase[:, 85] = 1.0
    maps = []
    for c in range(n_cores):
        b, half = c // 2, c % 2
        v = vbase.copy()
        for par in range(2):
            o = (half + par) % 2
            v[:, 81 + 2 * par] = 1.0 if o == 0 else 0.0
            v[:, 82 + 2 * par] = 1.0 if o == 1 else 0.0
        xs = x_prompt[b, :NP * 1024]
        xo = np.concatenate([xs[_own_index(half, j) * 512:(_own_index(half, j) + 1) * 512] for j in range(NP)], axis=0)
        sq = slice(4 * c, 4 * c + 4)
        m = dict(shared)
        m.update(xs=np.ascontiguousarray(xs), xo=np.ascontiguousarray(xo),
                 xsm=np.ascontiguousarray(x_sample[sq].reshape(256, D)),
                 ck=np.ascontiguousarray(ck_all[sq]), cv=np.ascontiguousarray(cv_all[sq]),
                 sconvT=np.ascontiguousarray(sconv[sq].reshape(4, 3, 8, 128).transpose(3, 2, 0, 1)),
                 slruT=np.ascontiguousarray(slru[sq].reshape(4, 8, 128).transpose(2, 1, 0)),
                 vecs=v)
        maps.append(m)
    return maps


def assemble(results, NP, n_cores=8):
    SEQ = NP * 1024
    B = n_cores // 2
    y_prompt = np.zeros((B, SEQ, D), np.float32)
    y_sample = np.zeros((4 * n_cores, 64, D), np.float32)
    k_prompt = np.zeros((1, B, SEQ, 8, 2, 64), np.float32)
    v_prompt = np.zeros((1, B, SEQ, 8, 128), np.float32)
    conv_prompt = np.zeros((1, B, 3, D), np.float32)
    lru_prompt = np.zeros((1, B, D), np.float32)
    k_sample = np.zeros((1, 4 * n_cores, 64, 8, 2, 64), np.float32)
    v_sample = np.zeros((1, 4 * n_cores, 64, 8, 128), np.float32)
    conv_sample = np.zeros((1, 4 * n_cores, 3, D), np.float32)
    lru_sample = np.zeros((1, 4 * n_cores, D), np.float32)
    for c in range(n_cores):
        r = results[c]
        b, half = c // 2, c % 2
        for j in range(NP):
            i = _own_index(half, j)
            y_prompt[b, i * 512:(i + 1) * 512] = r["y_own"][j * 512:(j + 1) * 512]
        if half == 0:
            k_prompt[0, b] = r["k_all"].reshape(SEQ, 8, 2, 64)
            v_prompt[0, b] = r["v_all"].reshape(SEQ, 8, 128)
            conv_prompt[0, b] = r["conv_p"].transpose(2, 1, 0).reshape(3, D)
            lru_prompt[0, b] = r["lru_p"].T.reshape(D)
        sq = slice(4 * c, 4 * c + 4)
        y_sample[sq] = r["y_smp"].reshape(4, 64, D)
        k_sample[0, sq] = r["k_smp"].reshape(4, 64, 8, 2, 64)
        v_sample[0, sq] = r["v_smp"].reshape(4, 64, 8, 128)
        conv_sample[0, sq] = r["conv_s"].transpose(2, 3, 1, 0).reshape(4, 3, D)
        lru_sample[0, sq] = r["lru_s"].transpose(2, 1, 0).reshape(4, D)
    return (y_prompt, y_sample, k_prompt, v_prompt, conv_prompt, lru_prompt, k_sample, v_sample, conv_sample, lru_sample)


def kernel(**inputs):
    NP = inputs["x_prompt"].shape[1] // 1024
    nc, _ = build_program(NP)
    maps = make_in_maps(inputs, NP)
    res = run_bass_kernel_spmd(nc, maps, core_ids=list(range(8)))
    return assemble(res.results, NP)
```

```python
import numpy as np
import concourse.bass as bass
import concourse.mybir as mybir
from concourse.bass_utils import run_bass_kernel_spmd

F32 = mybir.dt.float32
BF16 = mybir.dt.bfloat16
AF = mybir.ActivationFunctionType
ALU = mybir.AluOpType

D = 1024
NEG = -30000.0
LAM_INIT = 0.2
EPS = 1e-6
SEM_LIMIT = 28000
NDMA_SLOTS = 20


class Res:
    __slots__ = ("name", "w", "r")

    def __init__(self, name):
        self.name = name
        self.w = None
        self.r = {}


class Sched:
    def __init__(self, nc):
        self.nc = nc
        self.sems = []
        self.prog = {e: [] for e in ("pe", "act", "dve", "pool", "sp")}
        self.known = {e: {} for e in self.prog}
        self.cur = {}
        self.cnt = {}
        self.slots = {}
        self.slot_next = {}
        for e in ("pe", "act", "dve", "pool"):
            self.cur[e] = self._newsem()
            self.cnt[e] = 0
        for q in ("sp", "pool"):
            self.slots[q] = [[self._newsem(), 0] for _ in range(NDMA_SLOTS)]
            self.slot_next[q] = 0
        self.nops = 0
        self.enabled = True
        self.limit = 10 ** 9

    def _newsem(self):
        self.sems.append(None)
        return len(self.sems) - 1

    def _need(self, eng, deps, tok, war):
        if tok is None:
            return
        teng, si, val, isdma = tok
        if not isdma and teng == eng:
            if eng == "pe":
                return
        if self.known[eng].get(si, 0) >= val:
            return
        if deps.get(si, 0) < val:
            deps[si] = val

    def stage(self, n):
        if n > self.limit:
            self.enabled = False

    def op(self, eng, fn, R=(), W=(), dma=False):
        if not self.enabled:
            return None
        deps = {}
        for r in R:
            self._need(eng, deps, r.w, False)
        for w in W:
            self._need(eng, deps, w.w, False)
            for t in w.r.values():
                self._need(eng, deps, t, True)
        if dma:
            k = self.slot_next[eng]
            self.slot_next[eng] = (k + 1) % NDMA_SLOTS
            slot = self.slots[eng][k]
            if slot[1] > 0 and self.known[eng].get(slot[0], 0) < slot[1]:
                if deps.get(slot[0], 0) < slot[1]:
                    deps[slot[0]] = slot[1]
            slot[1] += 16
            tok = (eng, slot[0], slot[1], True)
            inc = 16
        else:
            if self.cnt[eng] >= SEM_LIMIT:
                self.cur[eng] = self._newsem()
                self.cnt[eng] = 0
            self.cnt[eng] += 1
            tok = (eng, self.cur[eng], self.cnt[eng], False)
            inc = 1
        P = self.prog[eng]
        for si, val in deps.items():
            P.append(("w", si, val))
            self.known[eng][si] = val
        P.append(("o", fn, tok[1], inc))
        self.nops += 1
        for r in R:
            r.r[(eng, tok[1])] = tok
        for w in W:
            w.w = tok
            w.r = {}
        return tok

    def finish(self):
        P = self.prog["sp"]
        for q in ("sp", "pool"):
            for si, val in self.slots[q]:
                if val > 0 and self.known["sp"].get(si, 0) < val:
                    P.append(("w", si, val))
        for e in ("pe", "act", "dve", "pool"):
            if self.cnt[e] > 0:
                P.append(("w", self.cur[e], self.cnt[e]))

    def replay(self, eng, e):
        sems = self.sems
        for it in self.prog[eng]:
            if it[0] == "w":
                e.wait_ge(sems[it[1]], it[2])
            else:
                ins = it[1](e)
                ins.then_inc(sems[it[2]], it[3])


class Ring:
    def __init__(self, tiles):
        self.tiles = tiles
        self.i = 0

    def next(self):
        t = self.tiles[self.i]
        self.i = (self.i + 1) % len(self.tiles)
        return t


class T:
    def __init__(self, h, nunits=1, name=""):
        self.h = h
        self.u = [Res("%s.%d" % (name, i)) for i in range(nunits)]

    @property
    def r(self):
        return self.u[0]

    def ap(self, c, lo, hi):
        return self.h[:, c, lo:hi]

    def res(self, c):
        return self.u[c]

    def allres(self):
        return self.u[0:8]


class HidView:
    def __init__(self, hid, base):
        self.hid, self.base = hid, base

    def ap(self, c, lo, hi):
        o = (self.base + c) * 512
        return self.hid.h[:, o + lo:o + hi]

    def res(self, c):
        return self.hid.u[self.base + c]

    def allres(self):
        return self.hid.u[self.base:self.base + 8]


def build_program(NP, do_prompt=True, do_sample=True, limit=10 ** 9):
    nc = bass.Bass("TRN2", target_bir_lowering=False)
    NSB = 2 * NP
    NTOK = NSB * 512
    NKB = NTOK // 128

    def din(name, shape, dt=F32):
        return nc.dram_tensor(name, list(shape), dt, kind="ExternalInput").ap()

    def dout(name, shape, dt=F32):
        return nc.dram_tensor(name, list(shape), dt, kind="ExternalOutput").ap()

    def dscr(name, shape, dt=BF16):
        return nc.dram_tensor(name, list(shape), dt).ap()

    xs = din("xs", [NTOK, D])
    xo = din("xo", [NP * 512, D])
    xsm = din("xsm", [256, D])
    ck = din("ck", [4, 1024, D])
    cv = din("cv", [4, 1024, D])
    sconvT = din("sconvT", [128, 8, 4, 3])
    slruT = din("slruT", [128, 8, 4])
    w_in = din("w_in", [D, 7168])
    w_ba = din("w_ba", [D, D])
    w_bl = din("w_bl", [D, D])
    w_o = din("w_o", [D, D])
    w_up = din("w_up", [D, 4096])
    w_dn = din("w_dn", [4096, D])
    wr_bd = din("wr_bd", [8, 128, 128])
    wi_bd = din("wi_bd", [8, 128, 128])
    NV = 93
    vecs_d = din("vecs", [128, NV])
    nfb_d = din("nfb", [128, D])
    lamb_d = din("lamb", [128, 256])
    dmask_d = din("dmask", [4, 128, 512])

    y_own = dout("y_own", [NP * 512, D])
    y_smp = dout("y_smp", [256, D])
    k_all = dout("k_all", [NTOK, D])
    v_all = dout("v_all", [NTOK, D])
    conv_p = dout("conv_p", [128, 8, 3])
    lru_p = dout("lru_p", [128, 8])
    k_smp = dout("k_smp", [256, D])
    v_smp = dout("v_smp", [256, D])
    conv_s = dout("conv_s", [128, 8, 4, 3])
    lru_s = dout("lru_s", [128, 8, 4])

    Wsc_in = dscr("Wsc_in", [128, 8, 7168])
    Wsc_ba = dscr("Wsc_ba", [128, 8, D])
    Wsc_bl = dscr("Wsc_bl", [128, 8, D])
    Wsc_o = dscr("Wsc_o", [128, 8, D])
    Wsc_up = dscr("Wsc_up", [128, 8, 4096])
    Wsc_dn = dscr("Wsc_dn", [128, 32, D])
    KTsc = dscr("KTsc", [128, 8, NTOK])
    Vsc = dscr("Vsc", [128, NKB, D])
    Vsm_sc = dscr("Vsm_sc", [4, 64, D])

    S = Sched(nc)
    S.limit = limit
    A = nc.alloc_sbuf_tensor

    def sb(name, shape, dt, nunits=1):
        return T(A("sb_" + name, list(shape), dt), nunits, name)

    ident = sb("ident", [128, 128], BF16)
    ones = sb("ones", [128, 128], BF16)
    vecs = sb("vecs", [128, NV], F32)
    drv = sb("drv", [128, 32], F32)
    nfb = sb("nfb", [128, D], F32)
    lamb = sb("lamb", [128, 256], F32)
    dmask = sb("dmask", [128, 4, 512], BF16)
    identM = sb("identM", [128, 4, 128], BF16)
    wr = sb("wr", [128, 8, 128], BF16)
    wi = sb("wi", [128, 8, 128], BF16)
    T8a = sb("T8a", [128, 8, 512], BF16, 8)
    T8b = sb("T8b", [128, 8, 512], BF16, 8)
    T8d = sb("T8d", [128, 8, 512], BF16, 8)
    QA = sb("QA", [128, 8, 512], BF16, 8)
    QB = sb("QB", [128, 8, 512], BF16, 8)
    hid = sb("hid", [128, 32 * 512], BF16, 32)
    ysel = sb("ysel", [128, 8, 512], F32, 8)
    xres = sb("xres", [128, 4, D], F32, 4)
    xtiles = Ring([sb("xt%d" % i, [128, D], F32) for i in range(2)])
    xns = Ring([sb("xn%d" % i, [128, D], BF16) for i in range(2)])
    kouts = Ring([sb("kout%d" % i, [128, D], F32) for i in range(1)])
    vouts = Ring([sb("vout%d" % i, [128, D], F32) for i in range(1)])
    vbs = Ring([sb("vb%d" % i, [128, D], BF16) for i in range(1)])
    wring = Ring([sb("wbuf%d" % i, [128, 8, 512], BF16) for i in range(3)])
    tmpF = Ring([sb("tF%d" % i, [128, 512], F32) for i in range(8)])
    tmpX = Ring([sb("tX%d" % i, [128, 520], F32) for i in range(2)])
    tmpB = Ring([sb("tB%d" % i, [128, 512], BF16) for i in range(6)])
    ksts = Ring([sb("kst%d" % i, [128, 1024], BF16) for i in range(3)])
    vsts = Ring([sb("vst%d" % i, [128, 8, 128], BF16) for i in range(3)])
    vnew = sb("vnew", [128, D], BF16)
    Pz = [sb("Pz%d" % i, [128, 64], BF16) for i in range(2)]
    stats = Ring([sb("st%d" % i, [128, 4], F32) for i in range(12)])
    hist_p = sb("hist_p", [128, 8, 1, 3], F32, 8)
    hprev_p = sb("hprev_p", [128, 8, 1], F32, 8)
    hist_s = sb("hist_s", [128, 8, 4, 3], F32, 8)
    hprev_s = sb("hprev_s", [128, 8, 4], F32, 8)
    mergedV = HidView(hid, 0)

    class AliasF:
        def __init__(self, k):
            self.h = hid.h[:, k * 1024:(k + 1) * 1024].bitcast(F32)
            self.rs = [hid.u[2 * k], hid.u[2 * k + 1]]

    lru_xc = Ring([AliasF(k) for k in range(0, 5)])
    lru_rt = Ring([AliasF(k) for k in range(5, 8)])
    lru_it = Ring([AliasF(k) for k in range(8, 11)])
    lru_at = Ring([AliasF(k) for k in range(11, 14)])
    onS = HidView(hid, 8)

    PS = [T(nc.alloc_psum_tensor("ps%d" % i, [128, 512], F32), 1, "ps%d" % i) for i in range(8)]

    C_GMIX, C_GMLP, C_CW, C_CB, C_BR, C_BI, C_LAM, C_GAIN, C_SEL = 0, 8, 16, 48, 56, 64, 72, 80, 81
    V_CNEG, V_EPS, V_ONE, V_ZERO, V_NEGLAM, V_GAINS, V_BIASB = 0, 8, 9, 10, 11, 12, 13

    def vcol(c, n=1):
        return vecs.h[:, c:c + n]

    def dcol(c, n=1):
        return drv.h[:, c:c + n]

    RW = {}
    R_kt = [Res("ktsc%d" % i) for i in range(NSB)]
    R_v = [Res("vsc%d" % i) for i in range(NKB)]
    R_vsm = [Res("vsm%d" % i) for i in range(4)]

    S.op("pool", lambda e: e.memset(ident.h[:], 1.0), W=[ident.r])
    S.op("pool", lambda e: e.affine_select(out=ident.h[:], in_=ident.h[:], pattern=[[-1, 128]],
                                           compare_op=ALU.is_equal, fill=0.0, base=0, channel_multiplier=1),
         R=[ident.r], W=[ident.r])
    S.op("pool", lambda e: e.memset(ones.h[:], 1.0), W=[ones.r])
    S.op("pool", lambda e: e.memset(drv.h[:], 0.0), W=[drv.r])
    S.op("pool", lambda e: e.memset(drv.h[:, V_EPS:V_EPS + 1], EPS), R=[drv.r], W=[drv.r])
    S.op("pool", lambda e: e.memset(drv.h[:, V_ONE:V_ONE + 1], 1.0), R=[drv.r], W=[drv.r])
    S.op("pool", lambda e: e.memset(QA.h[:], 0.0), W=QA.u)
    S.op("pool", lambda e: e.memset(QB.h[:], 0.0), W=QB.u)
    S.op("pool", lambda e: e.memset(hist_p.h[:], 0.0), W=hist_p.u)
    S.op("pool", lambda e: e.memset(vnew.h[:], 0.0), W=[vnew.r])
    for _pz in Pz:
        S.op("pool", (lambda _pz=_pz: lambda e: e.memset(_pz.h[:], 0.0))(), W=[_pz.r])
    S.op("pool", lambda e: e.memset(hprev_p.h[:], 0.0), W=hprev_p.u)
    S.op("sp", lambda e: e.dma_start(out=vecs.h[:], in_=vecs_d), W=[vecs.r], dma=True)
    S.op("sp", lambda e: e.dma_start(out=nfb.h[:], in_=nfb_d), W=[nfb.r], dma=True)
    S.op("sp", lambda e: e.dma_start(out=lamb.h[:], in_=lamb_d), W=[lamb.r], dma=True)
    S.op("sp", lambda e: e.dma_start(out=hist_s.h[:], in_=sconvT), W=hist_s.u, dma=True)
    S.op("sp", lambda e: e.dma_start(out=hprev_s.h[:], in_=slruT), W=hprev_s.u, dma=True)
    S.op("pool", lambda e: e.dma_start(out=dmask.h[:], in_=dmask_d.rearrange("r p q -> p r q")), W=[dmask.r], dma=True)
    S.op("pool", lambda e: e.dma_start(out=wr.h[:], in_=wr_bd.rearrange("c p n -> p c n")), W=[wr.r], dma=True)
    S.op("pool", lambda e: e.dma_start(out=wi.h[:], in_=wi_bd.rearrange("c p n -> p c n")), W=[wi.r], dma=True)
    w_in_v = w_in.rearrange("(c p) n -> p c n", p=128)

    def cast_w_in(key, c0, c1):
        RW[key] = []
        for c in range(0, 8, 2):
            r = Res("%s_%d" % (key, c))
            RW[key].append(r)
            S.op("pool", (lambda c=c: lambda e: e.dma_start(out=Wsc_in[:, c:c + 2, c0:c1], in_=w_in_v[:, c:c + 2, c0:c1]))(), W=[r], dma=True)

    def cast_w(key, src, dst, kc):
        RW[key] = []
        sv = src.rearrange("(c p) n -> p c n", p=128)
        for c in range(0, kc, 4):
            r = Res("%s_%d" % (key, c))
            RW[key].append(r)
            S.op("pool", (lambda c=c: lambda e: e.dma_start(out=dst[:, c:c + 4, :], in_=sv[:, c:c + 4, :]))(), W=[r], dma=True)

    cast_w_in("kvx", 1024, 4096)

    def prologue_rest():
        cast_w_in("q", 0, 1024)
        cast_w_in("gab", 4096, 7168)
        cast_w("ba", w_ba, Wsc_ba, 8)
        cast_w("bl", w_bl, Wsc_bl, 8)
        cast_w("o", w_o, Wsc_o, 8)
        cast_w("up", w_up, Wsc_up, 8)
        cast_w("dn", w_dn, Wsc_dn, 32)

    S.stage(1)
    tA = stats.next()
    S.op("act", lambda e: e.activation(out=drv.h[:, 16:24], in_=vcol(C_LAM, 8), func=AF.Exp, scale=-1.0), R=[vecs.r, drv.r], W=[drv.r])
    S.op("act", lambda e: e.activation(out=drv.h[:, 16:24], in_=drv.h[:, 16:24], func=AF.Ln, bias=dcol(V_ONE), scale=1.0), R=[drv.r], W=[drv.r])
    S.op("dve", lambda e: e.tensor_scalar(out=drv.h[:, V_CNEG:V_CNEG + 8], in0=drv.h[:, 16:24], scalar1=-8.0, scalar2=None, op0=ALU.mult),
         R=[drv.r], W=[drv.r])
    lp = tmpF.next()
    S.op("dve", lambda e: e.tensor_tensor(out=lp.h[:, 0:128], in0=lamb.h[:, 0:128], in1=lamb.h[:, 128:256], op=ALU.mult), R=[lamb.r], W=[lp.r])
    S.op("act", lambda e: e.activation(out=lp.h[:, 128:192], in_=lp.h[:, 0:64], func=AF.Copy, accum_out=tA.h[:, 0:1]), R=[lp.r], W=[tA.r, lp.r])
    S.op("act", lambda e: e.activation(out=lp.h[:, 192:256], in_=lp.h[:, 64:128], func=AF.Copy, accum_out=tA.h[:, 1:2]), R=[lp.r, tA.r], W=[tA.r, lp.r])
    S.op("act", lambda e: e.activation(out=tA.h[:, 2:4], in_=tA.h[:, 0:2], func=AF.Exp), R=[tA.r], W=[tA.r])
    S.op("dve", lambda e: e.tensor_tensor(out=tA.h[:, 0:1], in0=tA.h[:, 3:4], in1=tA.h[:, 2:3], op=ALU.subtract), R=[tA.r], W=[tA.r])
    S.op("dve", lambda e: e.tensor_scalar(out=drv.h[:, V_NEGLAM:V_NEGLAM + 1], in0=tA.h[:, 0:1], scalar1=-LAM_INIT, scalar2=None, op0=ALU.add),
         R=[tA.r, drv.r], W=[drv.r])
    S.op("dve", lambda e: e.tensor_scalar(out=drv.h[:, V_GAINS:V_GAINS + 1], in0=vcol(C_GAIN), scalar1=1.0 - LAM_INIT, scalar2=None, op0=ALU.mult),
         R=[vecs.r, drv.r], W=[drv.r])
    for par in range(2):
        S.op("dve", (lambda par=par: lambda e: e.tensor_scalar(out=drv.h[:, V_BIASB + par:V_BIASB + par + 1], in0=vcol(C_SEL + 2 * par),
                                                               scalar1=NEG, scalar2=None, op0=ALU.mult))(),
             R=[vecs.r, drv.r], W=[drv.r])
        for ab in range(2):
            S.op("dve", (lambda par=par, ab=ab: lambda e: e.tensor_scalar(out=identM.h[:, 2 * par + ab, :], in0=ident.h[:],
                                                                         scalar1=vcol(C_SEL + 2 * par + ab), scalar2=None, op0=ALU.mult))(),
                 R=[vecs.r, ident.r, identM.r], W=[identM.r])

    ps_rot = {"i": 0}

    def next_ps(lo=0, n=4):
        key = (lo, n)
        c = ps_rot.get(key, 0)
        ps_rot[key] = c + 1
        return PS[lo + (c % n)]

    def wres(wb):
        return wb.rs if hasattr(wb, "rs") else [wb.r]

    class WBuf:
        def __init__(self, h, rs):
            self.h, self.rs = h, rs

    def wload(scr_ap, res, wb=None):
        if wb is None:
            wb = wring.next()
        S.op("sp", lambda e: e.dma_start(out=wb.h[:, :, :], in_=scr_ap), R=res, W=wres(wb), dma=True)
        return wb

    wb_T8d = WBuf(T8d.h, T8d.u)
    wb_xr = [WBuf(xres.h[:, 2 * k:2 * k + 2, :].rearrange("p a b -> p (a b)").bitcast(BF16).rearrange("p (c n) -> p c n", c=8),
                  [xres.u[2 * k], xres.u[2 * k + 1]]) for k in range(2)]

    def chain_fm(ps, wb, ocl, xT, NT):
        def fn(e):
            for kc in range(8):
                ins = e.matmul(ps.h[:, 0:NT], lhsT=wb.h[:, kc, ocl * 128:(ocl + 1) * 128], rhs=xT.ap(kc, 0, NT),
                               start=(kc == 0), stop=(kc == 7))
            return ins
        S.op("pe", fn, R=wres(wb) + xT.allres(), W=[ps.r])

    def chain_tm(ps, wb, xT, t0, TT):
        def fn(e):
            for kc in range(8):
                ins = e.matmul(ps.h[0:TT, 0:512], lhsT=xT.ap(kc, t0, t0 + TT), rhs=wb.h[:, kc, :],
                               start=(kc == 0), stop=(kc == 7))
            return ins
        S.op("pe", fn, R=wres(wb) + xT.allres(), W=[ps.r])

    def rstd_of(src_ap, src_res, width, jk):
        st = stats.next()
        S.op("act", lambda e: e.activation(out=jk.h[:, 0:width], in_=src_ap, func=AF.Square, accum_out=st.h[:, 0:1]),
             R=src_res, W=[st.r, jk.r])
        S.op("act", lambda e: e.activation(out=st.h[:, 1:2], in_=st.h[:, 0:1], func=AF.Sqrt, scale=1.0 / width, bias=dcol(V_EPS)),
             R=[st.r, drv.r], W=[st.r])
        S.op("dve", lambda e: e.reciprocal(out=st.h[:, 2:3], in_=st.h[:, 1:2]), R=[st.r], W=[st.r])
        return st

    def norm_T(src_rows, NT, gcol0, dstT, keep):
        for t in range(NT // 128):
            if src_rows is not None and not keep:
                xtile = xtiles.next()
                xt_ap, xt_res = xtile.h[:], [xtile.r]
            else:
                xt_ap, xt_res = xres.h[:, t, :], [xres.u[t]]
            if src_rows is not None:
                S.op("sp", (lambda xt_ap=xt_ap, t=t: lambda e: e.dma_start(out=xt_ap, in_=src_rows[t * 128:(t + 1) * 128, :]))(),
                     W=xt_res, dma=True)
            xn = xns.next()
            st = rstd_of(xt_ap, xt_res, D, xn)
            S.op("dve", (lambda xn=xn, xt_ap=xt_ap, st=st: lambda e: e.tensor_scalar(out=xn.h[:], in0=xt_ap, scalar1=st.h[:, 2:3], scalar2=None, op0=ALU.mult))(),
                 R=xt_res + [st.r], W=[xn.r])
            ps = next_ps(4, 2)
            psb = ps.h[:].bitcast(BF16)

            def fn(e, xn=xn, psb=psb):
                for c in range(8):
                    ins = e.transpose(out=psb[:, c * 128:(c + 1) * 128], in_=xn.h[:, c * 128:(c + 1) * 128], identity=ident.h[:])
                return ins
            S.op("pe", fn, R=[xn.r, ident.r], W=[ps.r])
            S.op("dve", (lambda psb=psb, t=t: lambda e: e.tensor_tensor(
                out=dstT.h[:, :, t * 128:(t + 1) * 128], in0=psb[:, 0:1024].rearrange("p (c t) -> p c t", c=8),
                in1=vecs.h[:, gcol0:gcol0 + 8].unsqueeze(2).to_broadcast([128, 8, 128]), op=ALU.mult))(),
                 R=[ps.r, vecs.r], W=dstT.u)

    def phaseA(src_rows, NT, nseq, hist, hprev, selcol, first_slot, kdst, vdst, ktsc_dst, vsc_dst, TT):
        L = NT // nseq
        S.stage(2)
        norm_T(src_rows, NT, C_GMIX, T8a, keep=False)
        S.stage(3)
        wX = [wload(Wsc_in[:, :, 3072 + n * 512:3072 + (n + 1) * 512], RW["kvx"], wb_xr[n]) for n in range(2)]
        wK = [wload(Wsc_in[:, :, 1024 + n * 512:1024 + (n + 1) * 512], RW["kvx"]) for n in range(2)]
        wV = [wload(Wsc_in[:, :, 2048:2560], RW["kvx"]), wload(Wsc_in[:, :, 2560:3072], RW["kvx"], wb_T8d)]
        items = []

        def it_kT(oc):
            ps = next_ps(5, 3)
            chain_fm(ps, wK[oc // 4], oc % 4, T8a, NT)

            def ev():
                S.op("dve", lambda e: e.tensor_copy(out=T8b.h[:, oc, 0:NT], in_=ps.h[:, 0:NT]), R=[ps.r], W=[T8b.u[oc]])
                if oc == 7 and ktsc_dst is not None:
                    S.op("pool", lambda e: e.dma_start(out=ktsc_dst[0], in_=T8b.h[:, :, 0:NT]), R=T8b.u, W=[ktsc_dst[1]], dma=True)
            return ev

        def it_tok(t, wW, ring, dst, isv):
            pss = []
            for n in range(2):
                ps = next_ps(5, 3)
                chain_tm(ps, wW[n], T8a, t * TT, TT)
                pss.append(ps)

            def ev():
                o_ = ring.next()
                S.op("act", lambda e: e.activation(out=o_.h[0:TT, 0:512], in_=pss[0].h[0:TT, :], func=AF.Copy), R=[pss[0].r], W=[o_.r])
                S.op("dve", lambda e: e.tensor_copy(out=o_.h[0:TT, 512:1024], in_=pss[1].h[0:TT, :]), R=[pss[1].r, o_.r], W=[o_.r])
                S.op("pool", lambda e: e.dma_start(out=dst(t), in_=o_.h[0:TT, :]), R=[o_.r], dma=True)
                if isv:
                    vb = vbs.next()
                    S.op("dve", lambda e: e.tensor_copy(out=vb.h[0:TT, :], in_=o_.h[0:TT, :]), R=[o_.r], W=[vb.r])
                    dst_ap, dst_res = vsc_dst(t)
                    S.op("pool", lambda e: e.dma_start(out=dst_ap, in_=vb.h[0:TT, :]), R=[vb.r], W=[dst_res], dma=True)
            return ev

        ntok = NT // TT
        tok_items = []
        for t in range(ntok):
            tok_items.append((2, (lambda t=t: it_tok(t, wK, kouts, kdst, False))))
            tok_items.append((2, (lambda t=t: it_tok(t, wV, vouts, vdst, True))))
        kt_items = [(1, (lambda oc=oc: it_kT(oc))) for oc in range(8)]
        while kt_items or tok_items:
            if kt_items:
                items.append(kt_items.pop(0))
            if tok_items:
                items.append(tok_items.pop(0))

        st_ = {}

        def p0(cc):
            ps = next_ps(0, 5)
            chain_fm(ps, wX[cc // 4], cc % 4, T8a, NT)
            st_[cc] = dict(ps=ps)

        def p1(cc):
            d_ = st_[cc]
            ps = d_["ps"]
            xp = tmpX.next()
            xp3 = xp.h[:, 0:nseq * (L + 3)].rearrange("p (s l) -> p s l", s=nseq)
            S.op("dve", lambda e: e.tensor_copy(out=xp3[:, :, 0:3], in_=hist.h[:, cc, :, :]), R=[hist.u[cc]], W=[xp.r])
            S.op("act", lambda e: e.activation(out=xp3[:, :, 3:3 + L], in_=ps.h[:, 0:NT].rearrange("p (s l) -> p s l", s=nseq), func=AF.Copy),
                 R=[ps.r, xp.r], W=[xp.r])
            xc = lru_xc.next()
            xc3 = xc.h[:, 0:NT].rearrange("p (s l) -> p s l", s=nseq)
            d_.update(xp=xp, xp3=xp3, xc=xc, xc3=xc3)

        def p2(cc):
            d_ = st_[cc]
            xp, xp3, xc, xc3 = d_["xp"], d_["xp3"], d_["xc"], d_["xc3"]
            S.op("dve", lambda e: e.tensor_copy(out=hist.h[:, cc, :, :], in_=xp3[:, :, L:L + 3]), R=[xp.r], W=[hist.u[cc]])
            S.op("dve", lambda e: e.tensor_scalar(out=xc3, in0=xp3[:, :, 0:L], scalar1=vcol(C_CW + cc * 4 + 0), scalar2=vcol(C_CB + cc), op0=ALU.mult, op1=ALU.add),
                 R=[xp.r, vecs.r], W=xc.rs)
            for jj in range(1, 4):
                S.op("dve", (lambda jj=jj: lambda e: e.scalar_tensor_tensor(
                    out=xc3, in0=xp3[:, :, jj:jj + L], scalar=vcol(C_CW + cc * 4 + jj), in1=xc3, op0=ALU.mult, op1=ALU.add))(),
                     R=[xp.r, vecs.r] + xc.rs, W=xc.rs)
            xcb = tmpB.next()
            S.op("dve", lambda e: e.tensor_copy(out=xcb.h[:, 0:NT], in_=xc.h[:, 0:NT]), R=xc.rs, W=[xcb.r])
            psr = next_ps(0, 5)
            psi = next_ps(0, 5)
            S.op("pe", lambda e: e.matmul(psr.h[:, 0:NT], lhsT=wr.h[:, cc, :], rhs=xcb.h[:, 0:NT], start=True, stop=True), R=[wr.r, xcb.r], W=[psr.r])
            S.op("pe", lambda e: e.matmul(psi.h[:, 0:NT], lhsT=wi.h[:, cc, :], rhs=xcb.h[:, 0:NT], start=True, stop=True), R=[wi.r, xcb.r], W=[psi.r])
            d_["psr"], d_["psi"] = psr, psi

        def p3(cc):
            d_ = st_[cc]
            psr, psi = d_["psr"], d_["psi"]
            rt_, it_, at_ = lru_rt.next(), lru_it.next(), lru_at.next()
            S.op("act", lambda e: e.activation(out=rt_.h[:, 0:NT], in_=psr.h[:, 0:NT], func=AF.Sigmoid, bias=vcol(C_BR + cc), scale=1.0),
                 R=[psr.r, vecs.r], W=rt_.rs)
            S.op("act", lambda e: e.activation(out=it_.h[:, 0:NT], in_=psi.h[:, 0:NT], func=AF.Sigmoid, bias=vcol(C_BI + cc), scale=1.0),
                 R=[psi.r, vecs.r], W=it_.rs)
            S.op("act", lambda e: e.activation(out=at_.h[:, 0:NT], in_=rt_.h[:, 0:NT], func=AF.Exp, scale=dcol(V_CNEG + cc)),
                 R=rt_.rs + [drv.r], W=at_.rs)
            S.op("dve", lambda e: e.tensor_tensor(out=rt_.h[:, 0:NT], in0=at_.h[:, 0:NT], in1=at_.h[:, 0:NT], op=ALU.mult), R=at_.rs + rt_.rs, W=rt_.rs)
            d_["rt"], d_["it"], d_["at"] = rt_, it_, at_

        def p4(cc):
            rt_ = st_[cc]["rt"]
            S.op("act", lambda e: e.activation(out=rt_.h[:, 0:NT], in_=rt_.h[:, 0:NT], func=AF.Ln, scale=-1.0, bias=dcol(V_ONE)),
                 R=rt_.rs + [drv.r], W=rt_.rs)
            S.op("act", lambda e: e.activation(out=rt_.h[:, 0:NT], in_=rt_.h[:, 0:NT], func=AF.Exp, scale=0.5), R=rt_.rs, W=rt_.rs)

        def p5(cc):
            d_ = st_[cc]
            xc, rt_, it_, at_ = d_["xc"], d_["rt"], d_["it"], d_["at"]
            S.op("dve", lambda e: e.tensor_tensor(out=it_.h[:, 0:NT], in0=it_.h[:, 0:NT], in1=xc.h[:, 0:NT], op=ALU.mult), R=it_.rs + xc.rs, W=it_.rs)
            S.op("dve", lambda e: e.tensor_tensor(out=it_.h[:, 0:NT], in0=it_.h[:, 0:NT], in1=rt_.h[:, 0:NT], op=ALU.mult), R=it_.rs + rt_.rs, W=it_.rs)
            ht = rt_
            for s in range(nseq):
                S.op("dve", (lambda s=s: lambda e: e.tensor_tensor_scan(
                    out=ht.h[:, s * L:(s + 1) * L], data0=at_.h[:, s * L:(s + 1) * L], data1=it_.h[:, s * L:(s + 1) * L],
                    initial=hprev.h[:, cc, s:s + 1], op0=ALU.mult, op1=ALU.add))(),
                     R=at_.rs + it_.rs + [hprev.u[cc]] + ht.rs, W=ht.rs)
            S.op("dve", lambda e: e.tensor_copy(out=hprev.h[:, cc, :], in_=ht.h[:, 0:NT].rearrange("p (s l) -> p s l", s=nseq)[:, :, L - 1]),
                 R=ht.rs, W=[hprev.u[cc]])
            if first_slot:
                S.op("dve", lambda e: e.tensor_scalar(out=ysel.h[:, cc, 0:NT], in0=ht.h[:, 0:NT], scalar1=vcol(selcol), scalar2=None, op0=ALU.mult),
                     R=ht.rs + [vecs.r], W=[ysel.u[cc]])
            else:
                S.op("dve", lambda e: e.scalar_tensor_tensor(out=ysel.h[:, cc, 0:NT], in0=ht.h[:, 0:NT], scalar=vcol(selcol),
                                                             in1=ysel.h[:, cc, 0:NT], op0=ALU.mult, op1=ALU.add),
                     R=ht.rs + [vecs.r, ysel.u[cc]], W=[ysel.u[cc]])

        stages = [p0, p1, p2, p3, p4, p5]
        niter = 8 + len(stages) - 1
        pend_ev = []
        for t in range(niter):
            for k in range(len(stages)):
                cc = t - k
                if 0 <= cc < 8:
                    stages[k](cc)
            for ev in pend_ev:
                ev()
            pend_ev = []
            budget = 3
            while items and items[0][0] <= budget:
                n_, f_ = items.pop(0)
                budget -= n_
                pend_ev.append(f_())
        while items or pend_ev:
            for ev in pend_ev:
                ev()
            pend_ev = []
            budget = 3
            while items and items[0][0] <= budget:
                n_, f_ = items.pop(0)
                budget -= n_
                pend_ev.append(f_())

    class Attn:
        LOOK = 3

        def __init__(self, h, NQ, qoff, nblocks):
            self.h, self.NQ, self.qoff, self.nb, self.i = h, NQ, qoff, nblocks, 0
            self.si = 0
            self.pend = []

        def _flush(self, keep):
            while len(self.pend) > keep:
                self.pend.pop(0)()

        def block(self, kt_ap, v_ap, nk, Rk, Rv, mask=None):
            h, NQ, qoff = self.h, self.NQ, self.qoff
            first, last = (self.i == 0), (self.i == self.nb - 1)
            self.i += 1
            for c in range(2):
                Sb = PS[self.si % 4]
                self.si += 1
                Q = QA if c == 0 else QB

                def fn(e, Sb=Sb, Q=Q):
                    ins = e.matmul(Sb.h[0:nk, 0:NQ], lhsT=kt_ap, rhs=Q.h[:, h, qoff:qoff + NQ], start=True, stop=(mask is None))
                    if mask is not None:
                        ins = e.matmul(Sb.h[0:nk, 0:NQ], lhsT=mask[0], rhs=mask[1], start=False, stop=True)
                    return ins
                S.op("pe", fn, R=Rk + [Q.u[h]] + ([identM.r, dmask.r] if mask is not None else []), W=[Sb.r])
                pad = nk < 128
                Pc = Pz[c] if pad else tmpB.next()
                bias_ap = mask[2] if mask is not None else dcol(V_ZERO)
                S.op("act", (lambda Pc=Pc, Sb=Sb, bias_ap=bias_ap: lambda e: e.activation(out=Pc.h[0:nk, 0:NQ], in_=Sb.h[0:nk, 0:NQ], func=AF.Exp,
                                                                                       scale=0.125, bias=bias_ap[0:nk, :]))(),
                     R=[Sb.r, drv.r] + ([Pc.r] if pad else []), W=[Pc.r])
                Ob, Lb = PS[4 + c], PS[6 + c]

                def pv(Pc=Pc, Ob=Ob, Lb=Lb, first=first, last=last):
                    def fn2(e):
                        e.matmul(Ob.h[:, 0:NQ], lhsT=v_ap, rhs=Pc.h[:, 0:NQ], start=first, stop=last)
                        return e.matmul(Lb.h[:, 0:NQ], lhsT=ones.h[:], rhs=Pc.h[:, 0:NQ], start=first, stop=last)
                    S.op("pe", fn2, R=Rv + [Pc.r, ones.r], W=[Ob.r, Lb.r])
                self.pend.append(pv)
                self._flush(self.LOOK)

        def finish(self, dstV):
            self._flush(0)
            h, NQ, qoff = self.h, self.NQ, self.qoff
            o1, o2, l1, l2 = tmpF.next(), tmpF.next(), tmpF.next(), tmpF.next()
            S.op("dve", lambda e: e.tensor_copy(out=l1.h[:, 0:NQ], in_=PS[6].h[:, 0:NQ]), R=[PS[6].r], W=[l1.r])
            S.op("dve", lambda e: e.tensor_copy(out=o1.h[:, 0:NQ], in_=PS[4].h[:, 0:NQ]), R=[PS[4].r], W=[o1.r])
            S.op("dve", lambda e: e.tensor_copy(out=l2.h[:, 0:NQ], in_=PS[7].h[:, 0:NQ]), R=[PS[7].r], W=[l2.r])
            S.op("dve", lambda e: e.tensor_copy(out=o2.h[:, 0:NQ], in_=PS[5].h[:, 0:NQ]), R=[PS[5].r], W=[o2.r])
            S.op("dve", lambda e: e.reciprocal(out=l1.h[:, 0:NQ], in_=l1.h[:, 0:NQ]), R=[l1.r], W=[l1.r])
            S.op("dve", lambda e: e.reciprocal(out=l2.h[:, 0:NQ], in_=l2.h[:, 0:NQ]), R=[l2.r], W=[l2.r])
            S.op("dve", lambda e: e.tensor_tensor(out=o1.h[:, 0:NQ], in0=o1.h[:, 0:NQ], in1=l1.h[:, 0:NQ], op=ALU.mult), R=[o1.r, l1.r], W=[o1.r])
            S.op("dve", lambda e: e.tensor_tensor(out=o2.h[:, 0:NQ], in0=o2.h[:, 0:NQ], in1=l2.h[:, 0:NQ], op=ALU.mult), R=[o2.r, l2.r], W=[o2.r])
            t1 = o1
            S.op("dve", lambda e: e.scalar_tensor_tensor(out=t1.h[:, 0:NQ], in0=o2.h[:, 0:NQ], scalar=dcol(V_NEGLAM), in1=o1.h[:, 0:NQ], op0=ALU.mult, op1=ALU.add),
                 R=[o1.r, o2.r, drv.r], W=[t1.r])
            sq = xns.next()
            S.op("dve", lambda e: e.tensor_tensor(out=sq.h[:, 0:NQ], in0=t1.h[:, 0:NQ], in1=t1.h[:, 0:NQ], op=ALU.mult), R=[t1.r], W=[sq.r])
            Mb = PS[self.si % 4]
            self.si += 1

            def tail():
                S.op("pe", lambda e: e.matmul(Mb.h[:, 0:NQ], lhsT=ones.h[:], rhs=sq.h[:, 0:NQ], start=True, stop=True), R=[sq.r, ones.r], W=[Mb.r])
                r1 = l1
                S.op("dve", lambda e: e.tensor_scalar(out=r1.h[:, 0:NQ], in0=Mb.h[:, 0:NQ], scalar1=1.0 / 128, scalar2=EPS, op0=ALU.mult, op1=ALU.add), R=[Mb.r, r1.r], W=[r1.r])
                S.op("act", lambda e: e.activation(out=r1.h[:, 0:NQ], in_=r1.h[:, 0:NQ], func=AF.Ln), R=[r1.r], W=[r1.r])
                S.op("act", lambda e: e.activation(out=r1.h[:, 0:NQ], in_=r1.h[:, 0:NQ], func=AF.Exp, scale=-0.5), R=[r1.r], W=[r1.r])
                S.op("dve", lambda e: e.scalar_tensor_tensor(out=dstV.ap(h, qoff, qoff + NQ), in0=t1.h[:, 0:NQ], scalar=dcol(V_GAINS), in1=r1.h[:, 0:NQ],
                                                             op0=ALU.mult, op1=ALU.mult),
                     R=[t1.r, r1.r, drv.r], W=[dstV.res(h)])
            return tail

    def attend_prompt(j):
        par = j % 2
        tail = None
        for h in range(8):
            at = Attn(h, 512, 0, (j + 1) * 8)
            nblk = 0
            for jj in range(j + 1):
                kst, vst = ksts.next(), vsts.next()
                S.op("sp", (lambda kst=kst, jj=jj, h=h: lambda e: e.dma_start(out=kst.h[:], in_=KTsc[:, h, jj * 1024:(jj + 1) * 1024]))(),
                     R=[R_kt[2 * jj], R_kt[2 * jj + 1]], W=[kst.r], dma=True)
                S.op("sp", (lambda vst=vst, jj=jj, h=h: lambda e: e.dma_start(out=vst.h[:], in_=Vsc[:, jj * 8:(jj + 1) * 8, h * 128:(h + 1) * 128]))(),
                     R=[R_v[jj * 8 + k] for k in range(8)], W=[vst.r], dma=True)
                for kb in range(8):
                    mask = None
                    if jj == j:
                        ab = kb // 4
                        bias_ap = dcol(V_ZERO) if ab == 0 else dcol(V_BIASB + par)
                        mask = (identM.h[:, 2 * par + ab, :], dmask.h[:, kb % 4, :], bias_ap)
                    at.block(kst.h[:, kb * 128:(kb + 1) * 128], vst.h[:, kb, :], 128, [kst.r], [vst.r], mask)
                    nblk += 1
                    if nblk == 8 and tail is not None:
                        tail()
                        tail = None
            tail = at.finish(T8b)
        tail()

    def attend_sample():
        stail = [None]
        for s in range(4):
            S.op("sp", (lambda s=s: lambda e: e.dma_start(out=vnew.h[0:64, :], in_=Vsm_sc[s]))(), R=[R_vsm[s], vnew.r], W=[vnew.r], dma=True)
            for h in range(8):
                stg, vst, kst = vsts.next(), vsts.next(), ksts.next()
                S.op("pool", (lambda stg=stg, s=s, h=h: lambda e: e.dma_start(out=stg.h[:], in_=ck[s, :, h * 128:(h + 1) * 128].rearrange("(kb p) c -> p kb c", p=128)))(),
                     W=[stg.r], dma=True)
                S.op("pool", (lambda vst=vst, s=s, h=h: lambda e: e.dma_start(out=vst.h[:], in_=cv[s, :, h * 128:(h + 1) * 128].rearrange("(kb p) c -> p kb c", p=128)))(),
                     W=[vst.r], dma=True)
                ps = next_ps(0, 4)
                psb = ps.h[:].bitcast(BF16)

                def fn(e, stg=stg, psb=psb):
                    for kb in range(8):
                        ins = e.transpose(out=psb[:, kb * 128:(kb + 1) * 128], in_=stg.h[:, kb, :], identity=ident.h[:])
                    return ins
                S.op("pe", fn, R=[stg.r, ident.r], W=[ps.r])
                S.op("dve", (lambda kst=kst, psb=psb: lambda e: e.tensor_copy(out=kst.h[:], in_=psb[:, 0:1024]))(), R=[ps.r], W=[kst.r])
                at = Attn(h, 64, s * 64, 9)
                for kb in range(8):
                    at.block(kst.h[:, kb * 128:(kb + 1) * 128], vst.h[:, kb, :], 128, [kst.r], [vst.r], None)
                    if kb == 7 and stail[0] is not None:
                        stail[0]()
                        stail[0] = None
                at.block(T8b.h[:, h, s * 64:(s + 1) * 64], vnew.h[:, h * 128:(h + 1) * 128], 64, [T8b.u[h]], [vnew.r], None)
                stail[0] = at.finish(onS)
        stail[0]()

    def phaseB(src_rows, NT, attend, onV, ydst):
        S.stage(6)
        norm_T(src_rows, NT, C_GMIX, T8a, keep=True)
        wQ = [wload(Wsc_in[:, :, n * 512:(n + 1) * 512], RW["q"]) for n in range(2)]
        for oc in range(8):
            ps = next_ps(0, 4)
            chain_fm(ps, wQ[oc // 4], oc % 4, T8a, NT)
            S.op("dve", (lambda ps=ps, oc=oc: lambda e: e.tensor_copy(out=QA.h[0:64, oc, 0:NT], in_=ps.h[0:64, 0:NT]))(), R=[ps.r], W=[QA.u[oc]])
            S.op("act", (lambda ps=ps, oc=oc: lambda e: e.activation(out=QB.h[64:128, oc, 0:NT], in_=ps.h[64:128, 0:NT], func=AF.Copy))(), R=[ps.r], W=[QB.u[oc]])
        wG = [wload(Wsc_in[:, :, 4096 + n * 512:4096 + (n + 1) * 512], RW["gab"]) for n in range(2)]
        for oc in range(8):
            ps = next_ps(0, 4)
            chain_fm(ps, wG[oc // 4], oc % 4, T8a, NT)
            gt = tmpF.next()
            S.op("act", (lambda ps=ps, gt=gt: lambda e: e.activation(out=gt.h[:, 0:NT], in_=ps.h[:, 0:NT], func=AF.Gelu_apprx_tanh))(), R=[ps.r], W=[gt.r])
            S.op("dve", (lambda gt=gt, oc=oc: lambda e: e.tensor_tensor(out=T8d.h[:, oc, 0:NT], in0=gt.h[:, 0:NT], in1=ysel.h[:, oc, 0:NT], op=ALU.mult))(),
                 R=[gt.r, ysel.u[oc]], W=[T8d.u[oc]])
        S.stage(7)
        attend()
        S.stage(8)
        for n in range(2):
            wGA = wload(Wsc_in[:, :, 5120 + n * 512:5120 + (n + 1) * 512], RW["gab"])
            wBA = wload(Wsc_ba[:, :, n * 512:(n + 1) * 512], RW["ba"])
            parts = []
            for ocl in range(4):
                psa = next_ps(0, 4)
                chain_fm(psa, wGA, ocl, T8a, NT)
                sa = tmpF.next()
                S.op("act", (lambda psa=psa, sa=sa: lambda e: e.activation(out=sa.h[:, 0:NT], in_=psa.h[:, 0:NT], func=AF.Sigmoid))(), R=[psa.r], W=[sa.r])
                psA = next_ps(0, 4)
                chain_fm(psA, wBA, ocl, onV, NT)
                S.op("dve", (lambda psA=psA, sa=sa: lambda e: e.tensor_tensor(out=sa.h[:, 0:NT], in0=psA.h[:, 0:NT], in1=sa.h[:, 0:NT], op=ALU.mult))(),
                     R=[psA.r, sa.r], W=[sa.r])
                parts.append(sa)
            wGB = wload(Wsc_in[:, :, 6144 + n * 512:6144 + (n + 1) * 512], RW["gab"])
            wBL = wload(Wsc_bl[:, :, n * 512:(n + 1) * 512], RW["bl"])
            for ocl in range(4):
                oc = n * 4 + ocl
                psg = next_ps(0, 4)
                chain_fm(psg, wGB, ocl, T8a, NT)
                sbt = tmpF.next()
                S.op("act", (lambda psg=psg, sbt=sbt: lambda e: e.activation(out=sbt.h[:, 0:NT], in_=psg.h[:, 0:NT], func=AF.Sigmoid))(), R=[psg.r], W=[sbt.r])
                psL = next_ps(0, 4)
                chain_fm(psL, wBL, ocl, T8d, NT)
                S.op("dve", (lambda psL=psL, sbt=sbt: lambda e: e.tensor_tensor(out=sbt.h[:, 0:NT], in0=psL.h[:, 0:NT], in1=sbt.h[:, 0:NT], op=ALU.mult))(),
                     R=[psL.r, sbt.r], W=[sbt.r])
                sa = parts[ocl]
                S.op("dve", (lambda sa=sa, sbt=sbt, oc=oc: lambda e: e.tensor_tensor(out=mergedV.ap(oc, 0, NT), in0=sa.h[:, 0:NT], in1=sbt.h[:, 0:NT], op=ALU.add))(),
                     R=[sa.r, sbt.r], W=[mergedV.res(oc)])
        S.stage(9)
        wO = [wload(Wsc_o[:, :, n * 512:(n + 1) * 512], RW["o"]) for n in range(2)]
        for t in range(NT // 128):
            for n in range(2):
                ps = next_ps(6, 2)
                chain_tm(ps, wO[n], mergedV, t * 128, 128)
                S.op("dve", (lambda ps=ps, t=t, n=n: lambda e: e.tensor_tensor(out=xres.h[:, t, n * 512:(n + 1) * 512], in0=ps.h[:, 0:512],
                                                                            in1=xres.h[:, t, n * 512:(n + 1) * 512], op=ALU.add))(),
                     R=[ps.r, xres.u[t]], W=[xres.u[t]])
        S.stage(10)
        norm_T(None, NT, C_GMLP, T8a, keep=True)
        for ob in range(8):
            wU = wload(Wsc_up[:, :, ob * 512:(ob + 1) * 512], RW["up"])
            for ocl in range(4):
                oc = ob * 4 + ocl
                ps = next_ps(0, 4)
                chain_fm(ps, wU, ocl, T8a, NT)
                rl = tmpF.next()
                S.op("act", (lambda ps=ps, rl=rl: lambda e: e.activation(out=rl.h[:, 0:NT], in_=ps.h[:, 0:NT], func=AF.Relu))(), R=[ps.r], W=[rl.r])
                S.op("pool", (lambda rl=rl, oc=oc: lambda e: e.tensor_tensor(out=hid.h[:, oc * 512:oc * 512 + NT], in0=rl.h[:, 0:NT], in1=rl.h[:, 0:NT], op=ALU.mult))(),
                     R=[rl.r], W=[hid.u[oc]])
        ntt = NT // 128
        for n in range(2):
            for kg in range(4):
                wD = wload(Wsc_dn[:, kg * 8:(kg + 1) * 8, n * 512:(n + 1) * 512], RW["dn"])
                for t in range(ntt):
                    ps = PS[4 + t]

                    def fn(e, ps=ps, t=t, kg=kg, wD=wD):
                        for k8 in range(8):
                            kc = kg * 8 + k8
                            ins = e.matmul(ps.h[:, 0:512], lhsT=hid.h[:, kc * 512 + t * 128:kc * 512 + (t + 1) * 128], rhs=wD.h[:, k8, :],
                                           start=(kc == 0), stop=(kc == 31))
                        return ins
                    S.op("pe", fn, R=[wD.r] + hid.u[kg * 8:(kg + 1) * 8], W=[ps.r])
            for t in range(ntt):
                ps = PS[4 + t]
                S.op("dve", (lambda ps=ps, t=t, n=n: lambda e: e.tensor_tensor(out=xres.h[:, t, n * 512:(n + 1) * 512], in0=ps.h[:, 0:512],
                                                                            in1=xres.h[:, t, n * 512:(n + 1) * 512], op=ALU.add))(),
                     R=[ps.r, xres.u[t]], W=[xres.u[t]])
        S.stage(11)
        for t in range(ntt):
            jk = xns.next()
            st = rstd_of(xres.h[:, t, :], [xres.u[t]], D, jk)
            S.op("dve", (lambda st=st, t=t: lambda e: e.scalar_tensor_tensor(out=xres.h[:, t, :], in0=xres.h[:, t, :], scalar=st.h[:, 2:3], in1=nfb.h[:],
                                                                            op0=ALU.mult, op1=ALU.mult))(),
                 R=[st.r, nfb.r, xres.u[t]], W=[xres.u[t]])
            S.op("pool", (lambda t=t: lambda e: e.dma_start(out=ydst[t * 128:(t + 1) * 128, :], in_=xres.h[:, t, :]))(), R=[xres.u[t]], dma=True)

    rest_done = False
    if do_prompt:
        for j in range(NP):
            for slot in range(2):
                i = 2 * j + slot
                phaseA(xs[i * 512:(i + 1) * 512, :], 512, 1, hist_p, hprev_p, C_SEL + 2 * (j % 2) + slot, slot == 0,
                       kdst=lambda t, i=i: k_all[i * 512 + t * 128:i * 512 + (t + 1) * 128, :],
                       vdst=lambda t, i=i: v_all[i * 512 + t * 128:i * 512 + (t + 1) * 128, :],
                       ktsc_dst=(KTsc[:, :, i * 512:(i + 1) * 512], R_kt[i]),
                       vsc_dst=lambda t, i=i: (Vsc[:, i * 4 + t, :], R_v[i * 4 + t]), TT=128)
                if not rest_done:
                    S.stage(5.5)
                    prologue_rest()
                    rest_done = True
            phaseB(xo[j * 512:(j + 1) * 512, :], 512, (lambda j=j: attend_prompt(j)), T8b, y_own[j * 512:(j + 1) * 512, :])
        S.op("pool", lambda e: e.dma_start(out=conv_p, in_=hist_p.h[:, :, 0, :]), R=hist_p.u, dma=True)
        S.op("pool", lambda e: e.dma_start(out=lru_p, in_=hprev_p.h[:, :, 0]), R=hprev_p.u, dma=True)
    if do_sample:
        phaseA(xsm, 256, 4, hist_s, hprev_s, C_SEL + 4, True,
               kdst=lambda t: k_smp[t * 64:(t + 1) * 64, :], vdst=lambda t: v_smp[t * 64:(t + 1) * 64, :],
               ktsc_dst=None, vsc_dst=lambda t: (Vsm_sc[t], R_vsm[t]), TT=64)
        if not rest_done:
            prologue_rest()
            rest_done = True
        phaseB(xsm, 256, attend_sample, onS, y_smp)
        S.op("pool", lambda e: e.dma_start(out=conv_s, in_=hist_s.h[:]), R=hist_s.u, dma=True)
        S.op("pool", lambda e: e.dma_start(out=lru_s, in_=hprev_s.h[:]), R=hprev_s.u, dma=True)

    S.finish()
    from contextlib import ExitStack
    with ExitStack() as es:
        for i in range(len(S.sems)):
            S.sems[i] = es.enter_context(nc.semaphore("s%d" % i))
        block = es.enter_context(nc.Block())

        @block.tensor
        def _(e):
            S.replay("pe", e)

        @block.scalar
        def _(e):
            S.replay("act", e)

        @block.vector
        def _(e):
            S.replay("dve", e)

        @block.gpsimd
        def _(e):
            S.replay("pool", e)

        @block.sync
        def _(e):
            S.replay("sp", e)
    return nc, S


def _colvec(v):
    return np.ascontiguousarray(np.asarray(v, np.float32).reshape(8, 128).T)


def _own_index(half, j):
    return 2 * j + ((half + j) % 2)


def make_in_maps(inp, NP, n_cores=8):
    f = lambda a: np.ascontiguousarray(np.asarray(a, np.float32))
    x_prompt, x_sample = f(inp["x_prompt"]), f(inp["x_sample"])
    ck_all = f(inp["cache_k"])[0].reshape(32, 1024, D)
    cv_all = f(inp["cache_v"])[0].reshape(32, 1024, D)
    sconv, slru = f(inp["state_conv"])[0], f(inp["state_lru"])[0]
    w_r, w_i = f(inp["w_rgate"])[0], f(inp["w_igate"])[0]
    wr_bd = np.zeros((8, 128, 128), np.float32)
    wi_bd = np.zeros((8, 128, 128), np.float32)
    for cc in range(8):
        for k in range(2):
            wr_bd[cc, k * 64:(k + 1) * 64, k * 64:(k + 1) * 64] = w_r[2 * cc + k]
            wi_bd[cc, k * 64:(k + 1) * 64, k * 64:(k + 1) * 64] = w_i[2 * cc + k]
    nfb = np.ascontiguousarray(np.broadcast_to(f(inp["norm_final"])[None, :], (128, D)))
    lamb = np.ascontiguousarray(np.broadcast_to(
        np.concatenate([f(inp["lambda_q"])[0].reshape(-1), f(inp["lambda_k"])[0].reshape(-1)])[None, :], (128, 256)))
    dmask = np.zeros((4, 128, 512), np.float32)
    kk = np.arange(128)[:, None]
    qq = np.arange(512)[None, :]
    for r in range(4):
        dmask[r] = np.where((2 * r + kk // 64) <= (qq // 64), 0.0, NEG)
    shared = dict(
        w_in=f(inp["w_in"])[0], w_ba=f(inp["w_branch_attn"])[0], w_bl=f(inp["w_branch_lru"])[0], w_o=f(inp["w_out"])[0],
        w_up=f(inp["w_mlp_up"])[0], w_dn=f(inp["w_mlp_down"])[0], wr_bd=wr_bd, wi_bd=wi_bd, nfb=nfb, lamb=lamb, dmask=dmask)
    vbase = np.zeros((128, 93), np.float32)
    vbase[:, 0:8] = _colvec(inp["norm_mix"][0])
    vbase[:, 8:16] = _colvec(inp["norm_mlp"][0])
    cw = f(inp["conv_w"])[0]
    for jj in range(4):
        vbase[:, 16 + jj:48:4] = _colvec(cw[jj])
    vbase[:, 48:56] = _colvec(inp["conv_b"][0])
    vbase[:, 56:64] = _colvec(inp["b_rgate"][0])
    vbase[:, 64:72] = _colvec(inp["b_igate"][0])
    vbase[:, 72:80] = _colvec(inp["lru_lambda"][0])
    vbase[:, 80] = f(inp["head_gain"])[0]
    vbase[:, 85] = 1.0
    maps = []
    for c in range(n_cores):
        b, half = c // 2, c % 2
        v = vbase.copy()
        for par in range(2):
            o = (half + par) % 2
            v[:, 81 + 2 * par] = 1.0 if o == 0 else 0.0
            v[:, 82 + 2 * par] = 1.0 if o == 1 else 0.0
        xs = x_prompt[b, :NP * 1024]
        xo = np.concatenate([xs[_own_index(half, j) * 512:(_own_index(half, j) + 1) * 512] for j in range(NP)], axis=0)
        sq = slice(4 * c, 4 * c + 4)
        m = dict(shared)
        m.update(xs=np.ascontiguousarray(xs), xo=np.ascontiguousarray(xo),
                 xsm=np.ascontiguousarray(x_sample[sq].reshape(256, D)),
                 ck=np.ascontiguousarray(ck_all[sq]), cv=np.ascontiguousarray(cv_all[sq]),
                 sconvT=np.ascontiguousarray(sconv[sq].reshape(4, 3, 8, 128).transpose(3, 2, 0, 1)),
                 slruT=np.ascontiguousarray(slru[sq].reshape(4, 8, 128).transpose(2, 1, 0)),
                 vecs=v)
        maps.append(m)
    return maps


def assemble(results, NP, n_cores=8):
    SEQ = NP * 1024
    B = n_cores // 2
    y_prompt = np.zeros((B, SEQ, D), np.float32)
    y_sample = np.zeros((4 * n_cores, 64, D), np.float32)
    k_prompt = np.zeros((1, B, SEQ, 8, 2, 64), np.float32)
    v_prompt = np.zeros((1, B, SEQ, 8, 128), np.float32)
    conv_prompt = np.zeros((1, B, 3, D), np.float32)
    lru_prompt = np.zeros((1, B, D), np.float32)
    k_sample = np.zeros((1, 4 * n_cores, 64, 8, 2, 64), np.float32)
    v_sample = np.zeros((1, 4 * n_cores, 64, 8, 128), np.float32)
    conv_sample = np.zeros((1, 4 * n_cores, 3, D), np.float32)
    lru_sample = np.zeros((1, 4 * n_cores, D), np.float32)
    for c in range(n_cores):
        r = results[c]
        b, half = c // 2, c % 2
        for j in range(NP):
            i = _own_index(half, j)
            y_prompt[b, i * 512:(i + 1) * 512] = r["y_own"][j * 512:(j + 1) * 512]
        if half == 0:
            k_prompt[0, b] = r["k_all"].reshape(SEQ, 8, 2, 64)
            v_prompt[0, b] = r["v_all"].reshape(SEQ, 8, 128)
            conv_prompt[0, b] = r["conv_p"].transpose(2, 1, 0).reshape(3, D)
            lru_prompt[0, b] = r["lru_p"].T.reshape(D)
        sq = slice(4 * c, 4 * c + 4)
        y_sample[sq] = r["y_smp"].reshape(4, 64, D)
        k_sample[0, sq] = r["k_smp"].reshape(4, 64, 8, 2, 64)
        v_sample[0, sq] = r["v_smp"].reshape(4, 64, 8, 128)
        conv_sample[0, sq] = r["conv_s"].transpose(2, 3, 1, 0).reshape(4, 3, D)
        lru_sample[0, sq] = r["lru_s"].transpose(2, 1, 0).reshape(4, D)
    return (y_prompt, y_sample, k_prompt, v_prompt, conv_prompt, lru_prompt, k_sample, v_sample, conv_sample, lru_sample)


def kernel(**inputs):
    NP = inputs["x_prompt"].shape[1] // 1024
    nc, _ = build_program(NP)
    maps = make_in_maps(inputs, NP)
    res = run_bass_kernel_spmd(nc, maps, core_ids=list(range(8)))
    return assemble(res.results, NP)
```

```python
import numpy as np
import concourse.bass as bass
import concourse.mybir as mybir
from concourse.bass_utils import run_bass_kernel_spmd

F32 = mybir.dt.float32
BF16 = mybir.dt.bfloat16
AF = mybir.ActivationFunctionType
ALU = mybir.AluOpType

D = 1024
NEG = -30000.0
LAM_INIT = 0.2
EPS = 1e-6
SEM_LIMIT = 28000
NDMA_SLOTS = 20


class Res:
    __slots__ = ("name", "w", "r")

    def __init__(self, name):
        self.name = name
        self.w = None
        self.r = {}


class Sched:
    def __init__(self, nc):
        self.nc = nc
        self.sems = []
        self.prog = {e: [] for e in ("pe", "act", "dve", "pool", "sp")}
        self.known = {e: {} for e in self.prog}
        self.cur = {}
        self.cnt = {}
        self.slots = {}
        self.slot_next = {}
        for e in ("pe", "act", "dve", "pool"):
            self.cur[e] = self._newsem()
            self.cnt[e] = 0
        for q in ("sp", "pool"):
            self.slots[q] = [[self._newsem(), 0] for _ in range(NDMA_SLOTS)]
            self.slot_next[q] = 0
        self.nops = 0
        self.enabled = True
        self.limit = 10 ** 9

    def _newsem(self):
        self.sems.append(None)
        return len(self.sems) - 1

    def _need(self, eng, deps, tok, war):
        if tok is None:
            return
        teng, si, val, isdma = tok
        if not isdma and teng == eng:
            if eng == "pe":
                return
        if self.known[eng].get(si, 0) >= val:
            return
        if deps.get(si, 0) < val:
            deps[si] = val

    def stage(self, n):
        if n > self.limit:
            self.enabled = False

    def op(self, eng, fn, R=(), W=(), dma=False):
        if not self.enabled:
            return None
        deps = {}
        for r in R:
            self._need(eng, deps, r.w, False)
        for w in W:
            self._need(eng, deps, w.w, False)
            for t in w.r.values():
                self._need(eng, deps, t, True)
        if dma:
            k = self.slot_next[eng]
            self.slot_next[eng] = (k + 1) % NDMA_SLOTS
            slot = self.slots[eng][k]
            if slot[1] > 0 and self.known[eng].get(slot[0], 0) < slot[1]:
                if deps.get(slot[0], 0) < slot[1]:
                    deps[slot[0]] = slot[1]
            slot[1] += 16
            tok = (eng, slot[0], slot[1], True)
            inc = 16
        else:
            if self.cnt[eng] >= SEM_LIMIT:
                self.cur[eng] = self._newsem()
                self.cnt[eng] = 0
            self.cnt[eng] += 1
            tok = (eng, self.cur[eng], self.cnt[eng], False)
            inc = 1
        P = self.prog[eng]
        for si, val in deps.items():
            P.append(("w", si, val))
            self.known[eng][si] = val
        P.append(("o", fn, tok[1], inc))
        self.nops += 1
        for r in R:
            r.r[(eng, tok[1])] = tok
        for w in W:
            w.w = tok
            w.r = {}
        return tok

    def finish(self):
        P = self.prog["sp"]
        for q in ("sp", "pool"):
            for si, val in self.slots[q]:
                if val > 0 and self.known["sp"].get(si, 0) < val:
                    P.append(("w", si, val))
        for e in ("pe", "act", "dve", "pool"):
            if self.cnt[e] > 0:
                P.append(("w", self.cur[e], self.cnt[e]))

    def replay(self, eng, e):
        sems = self.sems
        for it in self.prog[eng]:
            if it[0] == "w":
                e.wait_ge(sems[it[1]], it[2])
            else:
                ins = it[1](e)
                ins.then_inc(sems[it[2]], it[3])


class Ring:
    def __init__(self, tiles):
        self.tiles = tiles
        self.i = 0

    def next(self):
        t = self.tiles[self.i]
        self.i = (self.i + 1) % len(self.tiles)
        return t


class T:
    def __init__(self, h, nunits=1, name=""):
        self.h = h
        self.u = [Res("%s.%d" % (name, i)) for i in range(nunits)]

    @property
    def r(self):
        return self.u[0]

    def ap(self, c, lo, hi):
        return self.h[:, c, lo:hi]

    def res(self, c):
        return self.u[c]

    def allres(self):
        return self.u[0:8]


class HidView:
    def __init__(self, hid, base):
        self.hid, self.base = hid, base

    def ap(self, c, lo, hi):
        o = (self.base + c) * 512
        return self.hid.h[:, o + lo:o + hi]

    def res(self, c):
        return self.hid.u[self.base + c]

    def allres(self):
        return self.hid.u[self.base:self.base + 8]


def build_program(NP, do_prompt=True, do_sample=True, limit=10 ** 9):
    nc = bass.Bass("TRN2", target_bir_lowering=False)
    NSB = 2 * NP
    NTOK = NSB * 512
    NKB = NTOK // 128

    def din(name, shape, dt=F32):
        return nc.dram_tensor(name, list(shape), dt, kind="ExternalInput").ap()

    def dout(name, shape, dt=F32):
        return nc.dram_tensor(name, list(shape), dt, kind="ExternalOutput").ap()

    def dscr(name, shape, dt=BF16):
        return nc.dram_tensor(name, list(shape), dt).ap()

    xs = din("xs", [NTOK, D])
    xo = din("xo", [NP * 512, D])
    xsm = din("xsm", [256, D])
    ck = din("ck", [4, 1024, D])
    cv = din("cv", [4, 1024, D])
    sconvT = din("sconvT", [128, 8, 4, 3])
    slruT = din("slruT", [128, 8, 4])
    w_in = din("w_in", [D, 7168])
    w_ba = din("w_ba", [D, D])
    w_bl = din("w_bl", [D, D])
    w_o = din("w_o", [D, D])
    w_up = din("w_up", [D, 4096])
    w_dn = din("w_dn", [4096, D])
    wr_bd = din("wr_bd", [8, 128, 128])
    wi_bd = din("wi_bd", [8, 128, 128])
    NV = 93
    vecs_d = din("vecs", [128, NV])
    nfb_d = din("nfb", [128, D])
    lamb_d = din("lamb", [128, 256])
    dmask_d = din("dmask", [4, 128, 512])

    y_own = dout("y_own", [NP * 512, D])
    y_smp = dout("y_smp", [256, D])
    k_all = dout("k_all", [NTOK, D])
    v_all = dout("v_all", [NTOK, D])
    conv_p = dout("conv_p", [128, 8, 3])
    lru_p = dout("lru_p", [128, 8])
    k_smp = dout("k_smp", [256, D])
    v_smp = dout("v_smp", [256, D])
    conv_s = dout("conv_s", [128, 8, 4, 3])
    lru_s = dout("lru_s", [128, 8, 4])

    Wsc_in = dscr("Wsc_in", [128, 8, 7168])
    Wsc_ba = dscr("Wsc_ba", [128, 8, D])
    Wsc_bl = dscr("Wsc_bl", [128, 8, D])
    Wsc_o = dscr("Wsc_o", [128, 8, D])
    Wsc_up = dscr("Wsc_up", [128, 8, 4096])
    Wsc_dn = dscr("Wsc_dn", [128, 32, D])
    KTsc = dscr("KTsc", [128, 8, NTOK])
    Vsc = dscr("Vsc", [128, NKB, D])
    Vsm_sc = dscr("Vsm_sc", [4, 64, D])

    S = Sched(nc)
    S.limit = limit
    A = nc.alloc_sbuf_tensor

    def sb(name, shape, dt, nunits=1):
        return T(A("sb_" + name, list(shape), dt), nunits, name)

    ident = sb("ident", [128, 128], BF16)
    ones = sb("ones", [128, 128], BF16)
    vecs = sb("vecs", [128, NV], F32)
    drv = sb("drv", [128, 32], F32)
    nfb = sb("nfb", [128, D], F32)
    lamb = sb("lamb", [128, 256], F32)
    dmask = sb("dmask", [128, 4, 512], BF16)
    identM = sb("identM", [128, 4, 128], BF16)
    wr = sb("wr", [128, 8, 128], BF16)
    wi = sb("wi", [128, 8, 128], BF16)
    T8a = sb("T8a", [128, 8, 512], BF16, 8)
    T8b = sb("T8b", [128, 8, 512], BF16, 8)
    T8d = sb("T8d", [128, 8, 512], BF16, 8)
    QA = sb("QA", [128, 8, 512], BF16, 8)
    QB = sb("QB", [128, 8, 512], BF16, 8)
    hid = sb("hid", [128, 32 * 512], BF16, 32)
    ysel = sb("ysel", [128, 8, 512], F32, 8)
    xres = sb("xres", [128, 4, D], F32, 4)
    xtiles = Ring([sb("xt%d" % i, [128, D], F32) for i in range(2)])
    xns = Ring([sb("xn%d" % i, [128, D], BF16) for i in range(2)])
    kouts = Ring([sb("kout%d" % i, [128, D], F32) for i in range(1)])
    vouts = Ring([sb("vout%d" % i, [128, D], F32) for i in range(1)])
    vbs = Ring([sb("vb%d" % i, [128, D], BF16) for i in range(1)])
    wring = Ring([sb("wbuf%d" % i, [128, 8, 512], BF16) for i in range(3)])
    tmpF = Ring([sb("tF%d" % i, [128, 512], F32) for i in range(8)])
    tmpX = Ring([sb("tX%d" % i, [128, 520], F32) for i in range(2)])
    tmpB = Ring([sb("tB%d" % i, [128, 512], BF16) for i in range(6)])
    ksts = Ring([sb("kst%d" % i, [128, 1024], BF16) for i in range(3)])
    vsts = Ring([sb("vst%d" % i, [128, 8, 128], BF16) for i in range(3)])
    vnew = sb("vnew", [128, D], BF16)
    Pz = [sb("Pz%d" % i, [128, 64], BF16) for i in range(2)]
    stats = Ring([sb("st%d" % i, [128, 4], F32) for i in range(12)])
    hist_p = sb("hist_p", [128, 8, 1, 3], F32, 8)
    hprev_p = sb("hprev_p", [128, 8, 1], F32, 8)
    hist_s = sb("hist_s", [128, 8, 4, 3], F32, 8)
    hprev_s = sb("hprev_s", [128, 8, 4], F32, 8)
    mergedV = HidView(hid, 0)

    class AliasF:
        def __init__(self, k):
            self.h = hid.h[:, k * 1024:(k + 1) * 1024].bitcast(F32)
            self.rs = [hid.u[2 * k], hid.u[2 * k + 1]]

    lru_xc = Ring([AliasF(k) for k in range(0, 5)])
    lru_rt = Ring([AliasF(k) for k in range(5, 8)])
    lru_it = Ring([AliasF(k) for k in range(8, 11)])
    lru_at = Ring([AliasF(k) for k in range(11, 14)])
    onS = HidView(hid, 8)

    PS = [T(nc.alloc_psum_tensor("ps%d" % i, [128, 512], F32), 1, "ps%d" % i) for i in range(8)]

    C_GMIX, C_GMLP, C_CW, C_CB, C_BR, C_BI, C_LAM, C_GAIN, C_SEL = 0, 8, 16, 48, 56, 64, 72, 80, 81
    V_CNEG, V_EPS, V_ONE, V_ZERO, V_NEGLAM, V_GAINS, V_BIASB = 0, 8, 9, 10, 11, 12, 13

    def vcol(c, n=1):
        return vecs.h[:, c:c + n]

    def dcol(c, n=1):
        return drv.h[:, c:c + n]

    RW = {}
    R_kt = [Res("ktsc%d" % i) for i in range(NSB)]
    R_v = [Res("vsc%d" % i) for i in range(NKB)]
    R_vsm = [Res("vsm%d" % i) for i in range(4)]

    S.op("pool", lambda e: e.memset(ident.h[:], 1.0), W=[ident.r])
    S.op("pool", lambda e: e.affine_select(out=ident.h[:], in_=ident.h[:], pattern=[[-1, 128]],
                                           compare_op=ALU.is_equal, fill=0.0, base=0, channel_multiplier=1),
         R=[ident.r], W=[ident.r])
    S.op("pool", lambda e: e.memset(ones.h[:], 1.0), W=[ones.r])
    S.op("pool", lambda e: e.memset(drv.h[:], 0.0), W=[drv.r])
    S.op("pool", lambda e: e.memset(drv.h[:, V_EPS:V_EPS + 1], EPS), R=[drv.r], W=[drv.r])
    S.op("pool", lambda e: e.memset(drv.h[:, V_ONE:V_ONE + 1], 1.0), R=[drv.r], W=[drv.r])
    S.op("pool", lambda e: e.memset(QA.h[:], 0.0), W=QA.u)
    S.op("pool", lambda e: e.memset(QB.h[:], 0.0), W=QB.u)
    S.op("pool", lambda e: e.memset(hist_p.h[:], 0.0), W=hist_p.u)
    S.op("pool", lambda e: e.memset(vnew.h[:], 0.0), W=[vnew.r])
    for _pz in Pz:
        S.op("pool", (lambda _pz=_pz: lambda e: e.memset(_pz.h[:], 0.0))(), W=[_pz.r])
    S.op("pool", lambda e: e.memset(hprev_p.h[:], 0.0), W=hprev_p.u)
    S.op("sp", lambda e: e.dma_start(out=vecs.h[:], in_=vecs_d), W=[vecs.r], dma=True)
    S.op("sp", lambda e: e.dma_start(out=nfb.h[:], in_=nfb_d), W=[nfb.r], dma=True)
    S.op("sp", lambda e: e.dma_start(out=lamb.h[:], in_=lamb_d), W=[lamb.r], dma=True)
    S.op("sp", lambda e: e.dma_start(out=hist_s.h[:], in_=sconvT), W=hist_s.u, dma=True)
    S.op("sp", lambda e: e.dma_start(out=hprev_s.h[:], in_=slruT), W=hprev_s.u, dma=True)
    S.op("pool", lambda e: e.dma_start(out=dmask.h[:], in_=dmask_d.rearrange("r p q -> p r q")), W=[dmask.r], dma=True)
    S.op("pool", lambda e: e.dma_start(out=wr.h[:], in_=wr_bd.rearrange("c p n -> p c n")), W=[wr.r], dma=True)
    S.op("pool", lambda e: e.dma_start(out=wi.h[:], in_=wi_bd.rearrange("c p n -> p c n")), W=[wi.r], dma=True)
    w_in_v = w_in.rearrange("(c p) n -> p c n", p=128)

    def cast_w_in(key, c0, c1):
        RW[key] = []
        for c in range(0, 8, 2):
            r = Res("%s_%d" % (key, c))
            RW[key].append(r)
            S.op("pool", (lambda c=c: lambda e: e.dma_start(out=Wsc_in[:, c:c + 2, c0:c1], in_=w_in_v[:, c:c + 2, c0:c1]))(), W=[r], dma=True)

    def cast_w(key, src, dst, kc):
        RW[key] = []
        sv = src.rearrange("(c p) n -> p c n", p=128)
        for c in range(0, kc, 4):
            r = Res("%s_%d" % (key, c))
            RW[key].append(r)
            S.op("pool", (lambda c=c: lambda e: e.dma_start(out=dst[:, c:c + 4, :], in_=sv[:, c:c + 4, :]))(), W=[r], dma=True)

    cast_w_in("kvx", 1024, 4096)

    def prologue_rest():
        cast_w_in("q", 0, 1024)
        cast_w_in("gab", 4096, 7168)
        cast_w("ba", w_ba, Wsc_ba, 8)
        cast_w("bl", w_bl, Wsc_bl, 8)
        cast_w("o", w_o, Wsc_o, 8)
        cast_w("up", w_up, Wsc_up, 8)
        cast_w("dn", w_dn, Wsc_dn, 32)

    S.stage(1)
    tA = stats.next()
    S.op("act", lambda e: e.activation(out=drv.h[:, 16:24], in_=vcol(C_LAM, 8), func=AF.Exp, scale=-1.0), R=[vecs.r, drv.r], W=[drv.r])
    S.op("act", lambda e: e.activation(out=drv.h[:, 16:24], in_=drv.h[:, 16:24], func=AF.Ln, bias=dcol(V_ONE), scale=1.0), R=[drv.r], W=[drv.r])
    S.op("dve", lambda e: e.tensor_scalar(out=drv.h[:, V_CNEG:V_CNEG + 8], in0=drv.h[:, 16:24], scalar1=-8.0, scalar2=None, op0=ALU.mult),
         R=[drv.r], W=[drv.r])
    lp = tmpF.next()
    S.op("dve", lambda e: e.tensor_tensor(out=lp.h[:, 0:128], in0=lamb.h[:, 0:128], in1=lamb.h[:, 128:256], op=ALU.mult), R=[lamb.r], W=[lp.r])
    S.op("act", lambda e: e.activation(out=lp.h[:, 128:192], in_=lp.h[:, 0:64], func=AF.Copy, accum_out=tA.h[:, 0:1]), R=[lp.r], W=[tA.r, lp.r])
    S.op("act", lambda e: e.activation(out=lp.h[:, 192:256], in_=lp.h[:, 64:128], func=AF.Copy, accum_out=tA.h[:, 1:2]), R=[lp.r, tA.r], W=[tA.r, lp.r])
    S.op("act", lambda e: e.activation(out=tA.h[:, 2:4], in_=tA.h[:, 0:2], func=AF.Exp), R=[tA.r], W=[tA.r])
    S.op("dve", lambda e: e.tensor_tensor(out=tA.h[:, 0:1], in0=tA.h[:, 3:4], in1=tA.h[:, 2:3], op=ALU.subtract), R=[tA.r], W=[tA.r])
    S.op("dve", lambda e: e.tensor_scalar(out=drv.h[:, V_NEGLAM:V_NEGLAM + 1], in0=tA.h[:, 0:1], scalar1=-LAM_INIT, scalar2=None, op0=ALU.add),
         R=[tA.r, drv.r], W=[drv.r])
    S.op("dve", lambda e: e.tensor_scalar(out=drv.h[:, V_GAINS:V_GAINS + 1], in0=vcol(C_GAIN), scalar1=1.0 - LAM_INIT, scalar2=None, op0=ALU.mult),
         R=[vecs.r, drv.r], W=[drv.r])
    for par in range(2):
        S.op("dve", (lambda par=par: lambda e: e.tensor_scalar(out=drv.h[:, V_BIASB + par:V_BIASB + par + 1], in0=vcol(C_SEL + 2 * par),
                                                               scalar1=NEG, scalar2=None, op0=ALU.mult))(),
             R=[vecs.r, drv.r], W=[drv.r])
        for ab in range(2):
            S.op("dve", (lambda par=par, ab=ab: lambda e: e.tensor_scalar(out=identM.h[:, 2 * par + ab, :], in0=ident.h[:],
                                                                         scalar1=vcol(C_SEL + 2 * par + ab), scalar2=None, op0=ALU.mult))(),
                 R=[vecs.r, ident.r, identM.r], W=[identM.r])

    ps_rot = {"i": 0}

    def next_ps(lo=0, n=4):
        key = (lo, n)
        c = ps_rot.get(key, 0)
        ps_rot[key] = c + 1
        return PS[lo + (c % n)]

    def wres(wb):
        return wb.rs if hasattr(wb, "rs") else [wb.r]

    class WBuf:
        def __init__(self, h, rs):
            self.h, self.rs = h, rs

    def wload(scr_ap, res, wb=None):
        if wb is None:
            wb = wring.next()
        S.op("sp", lambda e: e.dma_start(out=wb.h[:, :, :], in_=scr_ap), R=res, W=wres(wb), dma=True)
        return wb

    wb_T8d = WBuf(T8d.h, T8d.u)
    wb_xr = [WBuf(xres.h[:, 2 * k:2 * k + 2, :].rearrange("p a b -> p (a b)").bitcast(BF16).rearrange("p (c n) -> p c n", c=8),
                  [xres.u[2 * k], xres.u[2 * k + 1]]) for k in range(2)]

    def chain_fm(ps, wb, ocl, xT, NT):
        def fn(e):
            for kc in range(8):
                ins = e.matmul(ps.h[:, 0:NT], lhsT=wb.h[:, kc, ocl * 128:(ocl + 1) * 128], rhs=xT.ap(kc, 0, NT),
                               start=(kc == 0), stop=(kc == 7))
            return ins
        S.op("pe", fn, R=wres(wb) + xT.allres(), W=[ps.r])

    def chain_tm(ps, wb, xT, t0, TT):
        def fn(e):
            for kc in range(8):
                ins = e.matmul(ps.h[0:TT, 0:512], lhsT=xT.ap(kc, t0, t0 + TT), rhs=wb.h[:, kc, :],
                               start=(kc == 0), stop=(kc == 7))
            return ins
        S.op("pe", fn, R=wres(wb) + xT.allres(), W=[ps.r])

    def rstd_of(src_ap, src_res, width, jk):
        st = stats.next()
        S.op("act", lambda e: e.activation(out=jk.h[:, 0:width], in_=src_ap, func=AF.Square, accum_out=st.h[:, 0:1]),
             R=src_res, W=[st.r, jk.r])
        S.op("act", lambda e: e.activation(out=st.h[:, 1:2], in_=st.h[:, 0:1], func=AF.Sqrt, scale=1.0 / width, bias=dcol(V_EPS)),
             R=[st.r, drv.r], W=[st.r])
        S.op("dve", lambda e: e.reciprocal(out=st.h[:, 2:3], in_=st.h[:, 1:2]), R=[st.r], W=[st.r])
        return st

    def norm_T(src_rows, NT, gcol0, dstT, keep):
        for t in range(NT // 128):
            if src_rows is not None and not keep:
                xtile = xtiles.next()
                xt_ap, xt_res = xtile.h[:], [xtile.r]
            else:
                xt_ap, xt_res = xres.h[:, t, :], [xres.u[t]]
            if src_rows is not None:
                S.op("sp", (lambda xt_ap=xt_ap, t=t: lambda e: e.dma_start(out=xt_ap, in_=src_rows[t * 128:(t + 1) * 128, :]))(),
                     W=xt_res, dma=True)
            xn = xns.next()
            st = rstd_of(xt_ap, xt_res, D, xn)
            S.op("dve", (lambda xn=xn, xt_ap=xt_ap, st=st: lambda e: e.tensor_scalar(out=xn.h[:], in0=xt_ap, scalar1=st.h[:, 2:3], scalar2=None, op0=ALU.mult))(),
                 R=xt_res + [st.r], W=[xn.r])
            ps = next_ps(4, 2)
            psb = ps.h[:].bitcast(BF16)

            def fn(e, xn=xn, psb=psb):
                for c in range(8):
                    ins = e.transpose(out=psb[:, c * 128:(c + 1) * 128], in_=xn.h[:, c * 128:(c + 1) * 128], identity=ident.h[:])
                return ins
            S.op("pe", fn, R=[xn.r, ident.r], W=[ps.r])
            S.op("dve", (lambda psb=psb, t=t: lambda e: e.tensor_tensor(
                out=dstT.h[:, :, t * 128:(t + 1) * 128], in0=psb[:, 0:1024].rearrange("p (c t) -> p c t", c=8),
                in1=vecs.h[:, gcol0:gcol0 + 8].unsqueeze(2).to_broadcast([128, 8, 128]), op=ALU.mult))(),
                 R=[ps.r, vecs.r], W=dstT.u)

    def phaseA(src_rows, NT, nseq, hist, hprev, selcol, first_slot, kdst, vdst, ktsc_dst, vsc_dst, TT):
        L = NT // nseq
        S.stage(2)
        norm_T(src_rows, NT, C_GMIX, T8a, keep=False)
        S.stage(3)
        wX = [wload(Wsc_in[:, :, 3072 + n * 512:3072 + (n + 1) * 512], RW["kvx"], wb_xr[n]) for n in range(2)]
        wK = [wload(Wsc_in[:, :, 1024 + n * 512:1024 + (n + 1) * 512], RW["kvx"]) for n in range(2)]
        wV = [wload(Wsc_in[:, :, 2048:2560], RW["kvx"]), wload(Wsc_in[:, :, 2560:3072], RW["kvx"], wb_T8d)]
        items = []

        def it_kT(oc):
            ps = next_ps(5, 3)
            chain_fm(ps, wK[oc // 4], oc % 4, T8a, NT)

            def ev():
                S.op("dve", lambda e: e.tensor_copy(out=T8b.h[:, oc, 0:NT], in_=ps.h[:, 0:NT]), R=[ps.r], W=[T8b.u[oc]])
                if oc == 7 and ktsc_dst is not None:
                    S.op("pool", lambda e: e.dma_start(out=ktsc_dst[0], in_=T8b.h[:, :, 0:NT]), R=T8b.u, W=[ktsc_dst[1]], dma=True)
            return ev

        def it_tok(t, wW, ring, dst, isv):
            pss = []
            for n in range(2):
                ps = next_ps(5, 3)
                chain_tm(ps, wW[n], T8a, t * TT, TT)
                pss.append(ps)

            def ev():
                o_ = ring.next()
                S.op("act", lambda e: e.activation(out=o_.h[0:TT, 0:512], in_=pss[0].h[0:TT, :], func=AF.Copy), R=[pss[0].r], W=[o_.r])
                S.op("dve", lambda e: e.tensor_copy(out=o_.h[0:TT, 512:1024], in_=pss[1].h[0:TT, :]), R=[pss[1].r, o_.r], W=[o_.r])
                S.op("pool", lambda e: e.dma_start(out=dst(t), in_=o_.h[0:TT, :]), R=[o_.r], dma=True)
                if isv:
                    vb = vbs.next()
                    S.op("dve", lambda e: e.tensor_copy(out=vb.h[0:TT, :], in_=o_.h[0:TT, :]), R=[o_.r], W=[vb.r])
                    dst_ap, dst_res = vsc_dst(t)
                    S.op("pool", lambda e: e.dma_start(out=dst_ap, in_=vb.h[0:TT, :]), R=[vb.r], W=[dst_res], dma=True)
            return ev

        ntok = NT // TT
        tok_items = []
        for t in range(ntok):
            tok_items.append((2, (lambda t=t: it_tok(t, wK, kouts, kdst, False))))
            tok_items.append((2, (lambda t=t: it_tok(t, wV, vouts, vdst, True))))
        kt_items = [(1, (lambda oc=oc: it_kT(oc))) for oc in range(8)]
        while kt_items or tok_items:
            if kt_items:
                items.append(kt_items.pop(0))
            if tok_items:
                items.append(tok_items.pop(0))

        st_ = {}

        def p0(cc):
            ps = next_ps(0, 5)
            chain_fm(ps, wX[cc // 4], cc % 4, T8a, NT)
            st_[cc] = dict(ps=ps)

        def p1(cc):
            d_ = st_[cc]
            ps = d_["ps"]
            xp = tmpX.next()
            xp3 = xp.h[:, 0:nseq * (L + 3)].rearrange("p (s l) -> p s l", s=nseq)
            S.op("dve", lambda e: e.tensor_copy(out=xp3[:, :, 0:3], in_=hist.h[:, cc, :, :]), R=[hist.u[cc]], W=[xp.r])
            S.op("act", lambda e: e.activation(out=xp3[:, :, 3:3 + L], in_=ps.h[:, 0:NT].rearrange("p (s l) -> p s l", s=nseq), func=AF.Copy),
                 R=[ps.r, xp.r], W=[xp.r])
            xc = lru_xc.next()
            xc3 = xc.h[:, 0:NT].rearrange("p (s l) -> p s l", s=nseq)
            d_.update(xp=xp, xp3=xp3, xc=xc, xc3=xc3)

        def p2(cc):
            d_ = st_[cc]
            xp, xp3, xc, xc3 = d_["xp"], d_["xp3"], d_["xc"], d_["xc3"]
            S.op("dve", lambda e: e.tensor_copy(out=hist.h[:, cc, :, :], in_=xp3[:, :, L:L + 3]), R=[xp.r], W=[hist.u[cc]])
            S.op("dve", lambda e: e.tensor_scalar(out=xc3, in0=xp3[:, :, 0:L], scalar1=vcol(C_CW + cc * 4 + 0), scalar2=vcol(C_CB + cc), op0=ALU.mult, op1=ALU.add),
                 R=[xp.r, vecs.r], W=xc.rs)
            for jj in range(1, 4):
                S.op("dve", (lambda jj=jj: lambda e: e.scalar_tensor_tensor(
                    out=xc3, in0=xp3[:, :, jj:jj + L], scalar=vcol(C_CW + cc * 4 + jj), in1=xc3, op0=ALU.mult, op1=ALU.add))(),
                     R=[xp.r, vecs.r] + xc.rs, W=xc.rs)
            xcb = tmpB.next()
            S.op("dve", lambda e: e.tensor_copy(out=xcb.h[:, 0:NT], in_=xc.h[:, 0:NT]), R=xc.rs, W=[xcb.r])
            psr = next_ps(0, 5)
            psi = next_ps(0, 5)
            S.op("pe", lambda e: e.matmul(psr.h[:, 0:NT], lhsT=wr.h[:, cc, :], rhs=xcb.h[:, 0:NT], start=True, stop=True), R=[wr.r, xcb.r], W=[psr.r])
            S.op("pe", lambda e: e.matmul(psi.h[:, 0:NT], lhsT=wi.h[:, cc, :], rhs=xcb.h[:, 0:NT], start=True, stop=True), R=[wi.r, xcb.r], W=[psi.r])
            d_["psr"], d_["psi"] = psr, psi

        def p3(cc):
            d_ = st_[cc]
            psr, psi = d_["psr"], d_["psi"]
            rt_, it_, at_ = lru_rt.next(), lru_it.next(), lru_at.next()
            S.op("act", lambda e: e.activation(out=rt_.h[:, 0:NT], in_=psr.h[:, 0:NT], func=AF.Sigmoid, bias=vcol(C_BR + cc), scale=1.0),
                 R=[psr.r, vecs.r], W=rt_.rs)
            S.op("act", lambda e: e.activation(out=it_.h[:, 0:NT], in_=psi.h[:, 0:NT], func=AF.Sigmoid, bias=vcol(C_BI + cc), scale=1.0),
                 R=[psi.r, vecs.r], W=it_.rs)
            S.op("act", lambda e: e.activation(out=at_.h[:, 0:NT], in_=rt_.h[:, 0:NT], func=AF.Exp, scale=dcol(V_CNEG + cc)),
                 R=rt_.rs + [drv.r], W=at_.rs)
            S.op("dve", lambda e: e.tensor_tensor(out=rt_.h[:, 0:NT], in0=at_.h[:, 0:NT], in1=at_.h[:, 0:NT], op=ALU.mult), R=at_.rs + rt_.rs, W=rt_.rs)
            d_["rt"], d_["it"], d_["at"] = rt_, it_, at_

        def p4(cc):
            rt_ = st_[cc]["rt"]
            S.op("act", lambda e: e.activation(out=rt_.h[:, 0:NT], in_=rt_.h[:, 0:NT], func=AF.Ln, scale=-1.0, bias=dcol(V_ONE)),
                 R=rt_.rs + [drv.r], W=rt_.rs)
            S.op("act", lambda e: e.activation(out=rt_.h[:, 0:NT], in_=rt_.h[:, 0:NT], func=AF.Exp, scale=0.5), R=rt_.rs, W=rt_.rs)

        def p5(cc):
            d_ = st_[cc]
            xc, rt_, it_, at_ = d_["xc"], d_["rt"], d_["it"], d_["at"]
            S.op("dve", lambda e: e.tensor_tensor(out=it_.h[:, 0:NT], in0=it_.h[:, 0:NT], in1=xc.h[:, 0:NT], op=ALU.mult), R=it_.rs + xc.rs, W=it_.rs)
            S.op("dve", lambda e: e.tensor_tensor(out=it_.h[:, 0:NT], in0=it_.h[:, 0:NT], in1=rt_.h[:, 0:NT], op=ALU.mult), R=it_.rs + rt_.rs, W=it_.rs)
            ht = rt_
            for s in range(nseq):
                S.op("dve", (lambda s=s: lambda e: e.tensor_tensor_scan(
                    out=ht.h[:, s * L:(s + 1) * L], data0=at_.h[:, s * L:(s + 1) * L], data1=it_.h[:, s * L:(s + 1) * L],
                    initial=hprev.h[:, cc, s:s + 1], op0=ALU.mult, op1=ALU.add))(),
                     R=at_.rs + it_.rs + [hprev.u[cc]] + ht.rs, W=ht.rs)
            S.op("dve", lambda e: e.tensor_copy(out=hprev.h[:, cc, :], in_=ht.h[:, 0:NT].rearrange("p (s l) -> p s l", s=nseq)[:, :, L - 1]),
                 R=ht.rs, W=[hprev.u[cc]])
            if first_slot:
                S.op("dve", lambda e: e.tensor_scalar(out=ysel.h[:, cc, 0:NT], in0=ht.h[:, 0:NT], scalar1=vcol(selcol), scalar2=None, op0=ALU.mult),
                     R=ht.rs + [vecs.r], W=[ysel.u[cc]])
            else:
                S.op("dve", lambda e: e.scalar_tensor_tensor(out=ysel.h[:, cc, 0:NT], in0=ht.h[:, 0:NT], scalar=vcol(selcol),
                                                             in1=ysel.h[:, cc, 0:NT], op0=ALU.mult, op1=ALU.add),
                     R=ht.rs + [vecs.r, ysel.u[cc]], W=[ysel.u[cc]])

        stages = [p0, p1, p2, p3, p4, p5]
        niter = 8 + len(stages) - 1
        pend_ev = []
        for t in range(niter):
            for k in range(len(stages)):
                cc = t - k
                if 0 <= cc < 8:
                    stages[k](cc)
            for ev in pend_ev:
                ev()
            pend_ev = []
            budget = 3
            while items and items[0][0] <= budget:
                n_, f_ = items.pop(0)
                budget -= n_
                pend_ev.append(f_())
        while items or pend_ev:
            for ev in pend_ev:
                ev()
            pend_ev = []
            budget = 3
            while items and items[0][0] <= budget:
                n_, f_ = items.pop(0)
                budget -= n_
                pend_ev.append(f_())

    class Attn:
        LOOK = 3

        def __init__(self, h, NQ, qoff, nblocks):
            self.h, self.NQ, self.qoff, self.nb, self.i = h, NQ, qoff, nblocks, 0
            self.si = 0
            self.pend = []

        def _flush(self, keep):
            while len(self.pend) > keep:
                self.pend.pop(0)()

        def block(self, kt_ap, v_ap, nk, Rk, Rv, mask=None):
            h, NQ, qoff = self.h, self.NQ, self.qoff
            first, last = (self.i == 0), (self.i == self.nb - 1)
            self.i += 1
            for c in range(2):
                Sb = PS[self.si % 4]
                self.si += 1
                Q = QA if c == 0 else QB

                def fn(e, Sb=Sb, Q=Q):
                    ins = e.matmul(Sb.h[0:nk, 0:NQ], lhsT=kt_ap, rhs=Q.h[:, h, qoff:qoff + NQ], start=True, stop=(mask is None))
                    if mask is not None:
                        ins = e.matmul(Sb.h[0:nk, 0:NQ], lhsT=mask[0], rhs=mask[1], start=False, stop=True)
                    return ins
                S.op("pe", fn, R=Rk + [Q.u[h]] + ([identM.r, dmask.r] if mask is not None else []), W=[Sb.r])
                pad = nk < 128
                Pc = Pz[c] if pad else tmpB.next()
                bias_ap = mask[2] if mask is not None else dcol(V_ZERO)
                S.op("act", (lambda Pc=Pc, Sb=Sb, bias_ap=bias_ap: lambda e: e.activation(out=Pc.h[0:nk, 0:NQ], in_=Sb.h[0:nk, 0:NQ], func=AF.Exp,
                                                                                       scale=0.125, bias=bias_ap[0:nk, :]))(),
                     R=[Sb.r, drv.r] + ([Pc.r] if pad else []), W=[Pc.r])
                Ob, Lb = PS[4 + c], PS[6 + c]

                def pv(Pc=Pc, Ob=Ob, Lb=Lb, first=first, last=last):
                    def fn2(e):
                        e.matmul(Ob.h[:, 0:NQ], lhsT=v_ap, rhs=Pc.h[:, 0:NQ], start=first, stop=last)
                        return e.matmul(Lb.h[:, 0:NQ], lhsT=ones.h[:], rhs=Pc.h[:, 0:NQ], start=first, stop=last)
                    S.op("pe", fn2, R=Rv + [Pc.r, ones.r], W=[Ob.r, Lb.r])
                self.pend.append(pv)
                self._flush(self.LOOK)

        def finish(self, dstV):
            self._flush(0)
            h, NQ, qoff = self.h, self.NQ, self.qoff
            o1, o2, l1, l2 = tmpF.next(), tmpF.next(), tmpF.next(), tmpF.next()
            S.op("dve", lambda e: e.tensor_copy(out=l1.h[:, 0:NQ], in_=PS[6].h[:, 0:NQ]), R=[PS[6].r], W=[l1.r])
            S.op("dve", lambda e: e.tensor_copy(out=o1.h[:, 0:NQ], in_=PS[4].h[:, 0:NQ]), R=[PS[4].r], W=[o1.r])
            S.op("dve", lambda e: e.tensor_copy(out=l2.h[:, 0:NQ], in_=PS[7].h[:, 0:NQ]), R=[PS[7].r], W=[l2.r])
            S.op("dve", lambda e: e.tensor_copy(out=o2.h[:, 0:NQ], in_=PS[5].h[:, 0:NQ]), R=[PS[5].r], W=[o2.r])
            S.op("dve", lambda e: e.reciprocal(out=l1.h[:, 0:NQ], in_=l1.h[:, 0:NQ]), R=[l1.r], W=[l1.r])
            S.op("dve", lambda e: e.reciprocal(out=l2.h[:, 0:NQ], in_=l2.h[:, 0:NQ]), R=[l2.r], W=[l2.r])
            S.op("dve", lambda e: e.tensor_tensor(out=o1.h[:, 0:NQ], in0=o1.h[:, 0:NQ], in1=l1.h[:, 0:NQ], op=ALU.mult), R=[o1.r, l1.r], W=[o1.r])
            S.op("dve", lambda e: e.tensor_tensor(out=o2.h[:, 0:NQ], in0=o2.h[:, 0:NQ], in1=l2.h[:, 0:NQ], op=ALU.mult), R=[o2.r, l2.r], W=[o2.r])
            t1 = o1
            S.op("dve", lambda e: e.scalar_tensor_tensor(out=t1.h[:, 0:NQ], in0=o2.h[:, 0:NQ], scalar=dcol(V_NEGLAM), in1=o1.h[:, 0:NQ], op0=ALU.mult, op1=ALU.add),
                 R=[o1.r, o2.r, drv.r], W=[t1.r])
            sq = xns.next()
            S.op("dve", lambda e: e.tensor_tensor(out=sq.h[:, 0:NQ], in0=t1.h[:, 0:NQ], in1=t1.h[:, 0:NQ], op=ALU.mult), R=[t1.r], W=[sq.r])
            Mb = PS[self.si % 4]
            self.si += 1

            def tail():
                S.op("pe", lambda e: e.matmul(Mb.h[:, 0:NQ], lhsT=ones.h[:], rhs=sq.h[:, 0:NQ], start=True, stop=True), R=[sq.r, ones.r], W=[Mb.r])
                r1 = l1
                S.op("dve", lambda e: e.tensor_scalar(out=r1.h[:, 0:NQ], in0=Mb.h[:, 0:NQ], scalar1=1.0 / 128, scalar2=EPS, op0=ALU.mult, op1=ALU.add), R=[Mb.r, r1.r], W=[r1.r])
                S.op("act", lambda e: e.activation(out=r1.h[:, 0:NQ], in_=r1.h[:, 0:NQ], func=AF.Ln), R=[r1.r], W=[r1.r])
                S.op("act", lambda e: e.activation(out=r1.h[:, 0:NQ], in_=r1.h[:, 0:NQ], func=AF.Exp, scale=-0.5), R=[r1.r], W=[r1.r])
                S.op("dve", lambda e: e.scalar_tensor_tensor(out=dstV.ap(h, qoff, qoff + NQ), in0=t1.h[:, 0:NQ], scalar=dcol(V_GAINS), in1=r1.h[:, 0:NQ],
                                                             op0=ALU.mult, op1=ALU.mult),
                     R=[t1.r, r1.r, drv.r], W=[dstV.res(h)])
            return tail

    def attend_prompt(j):
        par = j % 2
        tail = None
        for h in range(8):
            at = Attn(h, 512, 0, (j + 1) * 8)
            nblk = 0
            for jj in range(j + 1):
                kst, vst = ksts.next(), vsts.next()
                S.op("sp", (lambda kst=kst, jj=jj, h=h: lambda e: e.dma_start(out=kst.h[:], in_=KTsc[:, h, jj * 1024:(jj + 1) * 1024]))(),
                     R=[R_kt[2 * jj], R_kt[2 * jj + 1]], W=[kst.r], dma=True)
                S.op("sp", (lambda vst=vst, jj=jj, h=h: lambda e: e.dma_start(out=vst.h[:], in_=Vsc[:, jj * 8:(jj + 1) * 8, h * 128:(h + 1) * 128]))(),
                     R=[R_v[jj * 8 + k] for k in range(8)], W=[vst.r], dma=True)
                for kb in range(8):
                    mask = None
                    if jj == j:
                        ab = kb // 4
                        bias_ap = dcol(V_ZERO) if ab == 0 else dcol(V_BIASB + par)
                        mask = (identM.h[:, 2 * par + ab, :], dmask.h[:, kb % 4, :], bias_ap)
                    at.block(kst.h[:, kb * 128:(kb + 1) * 128], vst.h[:, kb, :], 128, [kst.r], [vst.r], mask)
                    nblk += 1
                    if nblk == min(12, (j + 1) * 8) and tail is not None:
                        tail()
                        tail = None
            tail = at.finish(T8b)
        tail()

    def attend_sample():
        stail = [None]
        for s in range(4):
            S.op("sp", (lambda s=s: lambda e: e.dma_start(out=vnew.h[0:64, :], in_=Vsm_sc[s]))(), R=[R_vsm[s], vnew.r], W=[vnew.r], dma=True)
            for h in range(8):
                stg, vst, kst = vsts.next(), vsts.next(), ksts.next()
                S.op("pool", (lambda stg=stg, s=s, h=h: lambda e: e.dma_start(out=stg.h[:], in_=ck[s, :, h * 128:(h + 1) * 128].rearrange("(kb p) c -> p kb c", p=128)))(),
                     W=[stg.r], dma=True)
                S.op("pool", (lambda vst=vst, s=s, h=h: lambda e: e.dma_start(out=vst.h[:], in_=cv[s, :, h * 128:(h + 1) * 128].rearrange("(kb p) c -> p kb c", p=128)))(),
                     W=[vst.r], dma=True)
                ps = next_ps(0, 4)
                psb = ps.h[:].bitcast(BF16)

                def fn(e, stg=stg, psb=psb):
                    for kb in range(8):
                        ins = e.transpose(out=psb[:, kb * 128:(kb + 1) * 128], in_=stg.h[:, kb, :], identity=ident.h[:])
                    return ins
                S.op("pe", fn, R=[stg.r, ident.r], W=[ps.r])
                S.op("dve", (lambda kst=kst, psb=psb: lambda e: e.tensor_copy(out=kst.h[:], in_=psb[:, 0:1024]))(), R=[ps.r], W=[kst.r])
                at = Attn(h, 64, s * 64, 9)
                for kb in range(8):
                    at.block(kst.h[:, kb * 128:(kb + 1) * 128], vst.h[:, kb, :], 128, [kst.r], [vst.r], None)
                    if kb == 7 and stail[0] is not None:
                        stail[0]()
                        stail[0] = None
                at.block(T8b.h[:, h, s * 64:(s + 1) * 64], vnew.h[:, h * 128:(h + 1) * 128], 64, [T8b.u[h]], [vnew.r], None)
                stail[0] = at.finish(onS)
        stail[0]()

    def phaseB(src_rows, NT, attend, onV, ydst):
        S.stage(6)
        norm_T(src_rows, NT, C_GMIX, T8a, keep=True)
        wQ = [wload(Wsc_in[:, :, n * 512:(n + 1) * 512], RW["q"]) for n in range(2)]
        for oc in range(8):
            ps = next_ps(0, 4)
            chain_fm(ps, wQ[oc // 4], oc % 4, T8a, NT)
            S.op("dve", (lambda ps=ps, oc=oc: lambda e: e.tensor_copy(out=QA.h[0:64, oc, 0:NT], in_=ps.h[0:64, 0:NT]))(), R=[ps.r], W=[QA.u[oc]])
            S.op("act", (lambda ps=ps, oc=oc: lambda e: e.activation(out=QB.h[64:128, oc, 0:NT], in_=ps.h[64:128, 0:NT], func=AF.Copy))(), R=[ps.r], W=[QB.u[oc]])
        wG = [wload(Wsc_in[:, :, 4096 + n * 512:4096 + (n + 1) * 512], RW["gab"]) for n in range(2)]
        for oc in range(8):
            ps = next_ps(0, 4)
            chain_fm(ps, wG[oc // 4], oc % 4, T8a, NT)
            gt = tmpF.next()
            S.op("act", (lambda ps=ps, gt=gt: lambda e: e.activation(out=gt.h[:, 0:NT], in_=ps.h[:, 0:NT], func=AF.Gelu_apprx_tanh))(), R=[ps.r], W=[gt.r])
            S.op("dve", (lambda gt=gt, oc=oc: lambda e: e.tensor_tensor(out=T8d.h[:, oc, 0:NT], in0=gt.h[:, 0:NT], in1=ysel.h[:, oc, 0:NT], op=ALU.mult))(),
                 R=[gt.r, ysel.u[oc]], W=[T8d.u[oc]])
        S.stage(7)
        attend()
        S.stage(8)
        for n in range(2):
            wGA = wload(Wsc_in[:, :, 5120 + n * 512:5120 + (n + 1) * 512], RW["gab"])
            wBA = wload(Wsc_ba[:, :, n * 512:(n + 1) * 512], RW["ba"])
            parts = []
            for ocl in range(4):
                psa = next_ps(0, 4)
                chain_fm(psa, wGA, ocl, T8a, NT)
                sa = tmpF.next()
                S.op("act", (lambda psa=psa, sa=sa: lambda e: e.activation(out=sa.h[:, 0:NT], in_=psa.h[:, 0:NT], func=AF.Sigmoid))(), R=[psa.r], W=[sa.r])
                psA = next_ps(0, 4)
                chain_fm(psA, wBA, ocl, onV, NT)
                S.op("dve", (lambda psA=psA, sa=sa: lambda e: e.tensor_tensor(out=sa.h[:, 0:NT], in0=psA.h[:, 0:NT], in1=sa.h[:, 0:NT], op=ALU.mult))(),
                     R=[psA.r, sa.r], W=[sa.r])
                parts.append(sa)
            wGB = wload(Wsc_in[:, :, 6144 + n * 512:6144 + (n + 1) * 512], RW["gab"])
            wBL = wload(Wsc_bl[:, :, n * 512:(n + 1) * 512], RW["bl"])
            for ocl in range(4):
                oc = n * 4 + ocl
                psg = next_ps(0, 4)
                chain_fm(psg, wGB, ocl, T8a, NT)
                sbt = tmpF.next()
                S.op("act", (lambda psg=psg, sbt=sbt: lambda e: e.activation(out=sbt.h[:, 0:NT], in_=psg.h[:, 0:NT], func=AF.Sigmoid))(), R=[psg.r], W=[sbt.r])
                psL = next_ps(0, 4)
                chain_fm(psL, wBL, ocl, T8d, NT)
                S.op("dve", (lambda psL=psL, sbt=sbt: lambda e: e.tensor_tensor(out=sbt.h[:, 0:NT], in0=psL.h[:, 0:NT], in1=sbt.h[:, 0:NT], op=ALU.mult))(),
                     R=[psL.r, sbt.r], W=[sbt.r])
                sa = parts[ocl]
                S.op("dve", (lambda sa=sa, sbt=sbt, oc=oc: lambda e: e.tensor_tensor(out=mergedV.ap(oc, 0, NT), in0=sa.h[:, 0:NT], in1=sbt.h[:, 0:NT], op=ALU.add))(),
                     R=[sa.r, sbt.r], W=[mergedV.res(oc)])
        S.stage(9)
        wO = [wload(Wsc_o[:, :, n * 512:(n + 1) * 512], RW["o"]) for n in range(2)]
        for t in range(NT // 128):
            for n in range(2):
                ps = next_ps(6, 2)
                chain_tm(ps, wO[n], mergedV, t * 128, 128)
                S.op("dve", (lambda ps=ps, t=t, n=n: lambda e: e.tensor_tensor(out=xres.h[:, t, n * 512:(n + 1) * 512], in0=ps.h[:, 0:512],
                                                                            in1=xres.h[:, t, n * 512:(n + 1) * 512], op=ALU.add))(),
                     R=[ps.r, xres.u[t]], W=[xres.u[t]])
        S.stage(10)
        norm_T(None, NT, C_GMLP, T8a, keep=True)
        for ob in range(8):
            wU = wload(Wsc_up[:, :, ob * 512:(ob + 1) * 512], RW["up"])
            for ocl in range(4):
                oc = ob * 4 + ocl
                ps = next_ps(0, 4)
                chain_fm(ps, wU, ocl, T8a, NT)
                rl = tmpF.next()
                S.op("act", (lambda ps=ps, rl=rl: lambda e: e.activation(out=rl.h[:, 0:NT], in_=ps.h[:, 0:NT], func=AF.Relu))(), R=[ps.r], W=[rl.r])
                S.op("pool", (lambda rl=rl, oc=oc: lambda e: e.tensor_tensor(out=hid.h[:, oc * 512:oc * 512 + NT], in0=rl.h[:, 0:NT], in1=rl.h[:, 0:NT], op=ALU.mult))(),
                     R=[rl.r], W=[hid.u[oc]])
        ntt = NT // 128
        for n in range(2):
            for kg in range(4):
                wD = wload(Wsc_dn[:, kg * 8:(kg + 1) * 8, n * 512:(n + 1) * 512], RW["dn"])
                for t in range(ntt):
                    ps = PS[4 + t]

                    def fn(e, ps=ps, t=t, kg=kg, wD=wD):
                        for k8 in range(8):
                            kc = kg * 8 + k8
                            ins = e.matmul(ps.h[:, 0:512], lhsT=hid.h[:, kc * 512 + t * 128:kc * 512 + (t + 1) * 128], rhs=wD.h[:, k8, :],
                                           start=(kc == 0), stop=(kc == 31))
                        return ins
                    S.op("pe", fn, R=[wD.r] + hid.u[kg * 8:(kg + 1) * 8], W=[ps.r])
            for t in range(ntt):
                ps = PS[4 + t]
                S.op("dve", (lambda ps=ps, t=t, n=n: lambda e: e.tensor_tensor(out=xres.h[:, t, n * 512:(n + 1) * 512], in0=ps.h[:, 0:512],
                                                                            in1=xres.h[:, t, n * 512:(n + 1) * 512], op=ALU.add))(),
                     R=[ps.r, xres.u[t]], W=[xres.u[t]])
        S.stage(11)
        for t in range(ntt):
            jk = xns.next()
            st = rstd_of(xres.h[:, t, :], [xres.u[t]], D, jk)
            S.op("dve", (lambda st=st, t=t: lambda e: e.scalar_tensor_tensor(out=xres.h[:, t, :], in0=xres.h[:, t, :], scalar=st.h[:, 2:3], in1=nfb.h[:],
                                                                            op0=ALU.mult, op1=ALU.mult))(),
                 R=[st.r, nfb.r, xres.u[t]], W=[xres.u[t]])
            S.op("pool", (lambda t=t: lambda e: e.dma_start(out=ydst[t * 128:(t + 1) * 128, :], in_=xres.h[:, t, :]))(), R=[xres.u[t]], dma=True)

    rest_done = False
    if do_prompt:
        for j in range(NP):
            for slot in range(2):
                i = 2 * j + slot
                phaseA(xs[i * 512:(i + 1) * 512, :], 512, 1, hist_p, hprev_p, C_SEL + 2 * (j % 2) + slot, slot == 0,
                       kdst=lambda t, i=i: k_all[i * 512 + t * 128:i * 512 + (t + 1) * 128, :],
                       vdst=lambda t, i=i: v_all[i * 512 + t * 128:i * 512 + (t + 1) * 128, :],
                       ktsc_dst=(KTsc[:, :, i * 512:(i + 1) * 512], R_kt[i]),
                       vsc_dst=lambda t, i=i: (Vsc[:, i * 4 + t, :], R_v[i * 4 + t]), TT=128)
                if not rest_done:
                    S.stage(5.5)
                    prologue_rest()
                    rest_done = True
            phaseB(xo[j * 512:(j + 1) * 512, :], 512, (lambda j=j: attend_prompt(j)), T8b, y_own[j * 512:(j + 1) * 512, :])
        S.op("pool", lambda e: e.dma_start(out=conv_p, in_=hist_p.h[:, :, 0, :]), R=hist_p.u, dma=True)
        S.op("pool", lambda e: e.dma_start(out=lru_p, in_=hprev_p.h[:, :, 0]), R=hprev_p.u, dma=True)
    if do_sample:
        phaseA(xsm, 256, 4, hist_s, hprev_s, C_SEL + 4, True,
               kdst=lambda t: k_smp[t * 64:(t + 1) * 64, :], vdst=lambda t: v_smp[t * 64:(t + 1) * 64, :],
               ktsc_dst=None, vsc_dst=lambda t: (Vsm_sc[t], R_vsm[t]), TT=64)
        if not rest_done:
            prologue_rest()
            rest_done = True
        phaseB(xsm, 256, attend_sample, onS, y_smp)
        S.op("pool", lambda e: e.dma_start(out=conv_s, in_=hist_s.h[:]), R=hist_s.u, dma=True)
        S.op("pool", lambda e: e.dma_start(out=lru_s, in_=hprev_s.h[:]), R=hprev_s.u, dma=True)

    S.finish()
    from contextlib import ExitStack
    with ExitStack() as es:
        for i in range(len(S.sems)):
            S.sems[i] = es.enter_context(nc.semaphore("s%d" % i))
        block = es.enter_context(nc.Block())

        @block.tensor
        def _(e):
            S.replay("pe", e)

        @block.scalar
        def _(e):
            S.replay("act", e)

        @block.vector
        def _(e):
            S.replay("dve", e)

        @block.gpsimd
        def _(e):
            S.replay("pool", e)

        @block.sync
        def _(e):
            S.replay("sp", e)
    return nc, S


def _colvec(v):
    return np.ascontiguousarray(np.asarray(v, np.float32).reshape(8, 128).T)


def _own_index(half, j):
    return 2 * j + ((half + j) % 2)


def make_in_maps(inp, NP, n_cores=8):
    f = lambda a: np.ascontiguousarray(np.asarray(a, np.float32))
    x_prompt, x_sample = f(inp["x_prompt"]), f(inp["x_sample"])
    ck_all = f(inp["cache_k"])[0].reshape(32, 1024, D)
    cv_all = f(inp["cache_v"])[0].reshape(32, 1024, D)
    sconv, slru = f(inp["state_conv"])[0], f(inp["state_lru"])[0]
    w_r, w_i = f(inp["w_rgate"])[0], f(inp["w_igate"])[0]
    wr_bd = np.zeros((8, 128, 128), np.float32)
    wi_bd = np.zeros((8, 128, 128), np.float32)
    for cc in range(8):
        for k in range(2):
            wr_bd[cc, k * 64:(k + 1) * 64, k * 64:(k + 1) * 64] = w_r[2 * cc + k]
            wi_bd[cc, k * 64:(k + 1) * 64, k * 64:(k + 1) * 64] = w_i[2 * cc + k]
    nfb = np.ascontiguousarray(np.broadcast_to(f(inp["norm_final"])[None, :], (128, D)))
    lamb = np.ascontiguousarray(np.broadcast_to(
        np.concatenate([f(inp["lambda_q"])[0].reshape(-1), f(inp["lambda_k"])[0].reshape(-1)])[None, :], (128, 256)))
    dmask = np.zeros((4, 128, 512), np.float32)
    kk = np.arange(128)[:, None]
    qq = np.arange(512)[None, :]
    for r in range(4):
        dmask[r] = np.where((2 * r + kk // 64) <= (qq // 64), 0.0, NEG)
    shared = dict(
        w_in=f(inp["w_in"])[0], w_ba=f(inp["w_branch_attn"])[0], w_bl=f(inp["w_branch_lru"])[0], w_o=f(inp["w_out"])[0],
        w_up=f(inp["w_mlp_up"])[0], w_dn=f(inp["w_mlp_down"])[0], wr_bd=wr_bd, wi_bd=wi_bd, nfb=nfb, lamb=lamb, dmask=dmask)
    vbase = np.zeros((128, 93), np.float32)
    vbase[:, 0:8] = _colvec(inp["norm_mix"][0])
    vbase[:, 8:16] = _colvec(inp["norm_mlp"][0])
    cw = f(inp["conv_w"])[0]
    for jj in range(4):
        vbase[:, 16 + jj:48:4] = _colvec(cw[jj])
    vbase[:, 48:56] = _colvec(inp["conv_b"][0])
    vbase[:, 56:64] = _colvec(inp["b_rgate"][0])
    vbase[:, 64:72] = _colvec(inp["b_igate"][0])
    vbase[:, 72:80] = _colvec(inp["lru_lambda"][0])
    vbase[:, 80] = f(inp["head_gain"])[0]
    vbase[:, 85] = 1.0
    maps = []
    for c in range(n_cores):
        b, half = c // 2, c % 2
        v = vbase.copy()
        for par in range(2):
            o = (half + par) % 2
            v[:, 81 + 2 * par] = 1.0 if o == 0 else 0.0
            v[:, 82 + 2 * par] = 1.0 if o == 1 else 0.0
        xs = x_prompt[b, :NP * 1024]
        xo = np.concatenate([xs[_own_index(half, j) * 512:(_own_index(half, j) + 1) * 512] for j in range(NP)], axis=0)
        sq = slice(4 * c, 4 * c + 4)
        m = dict(shared)
        m.update(xs=np.ascontiguousarray(xs), xo=np.ascontiguousarray(xo),
                 xsm=np.ascontiguousarray(x_sample[sq].reshape(256, D)),
                 ck=np.ascontiguousarray(ck_all[sq]), cv=np.ascontiguousarray(cv_all[sq]),
                 sconvT=np.ascontiguousarray(sconv[sq].reshape(4, 3, 8, 128).transpose(3, 2, 0, 1)),
                 slruT=np.ascontiguousarray(slru[sq].reshape(4, 8, 128).transpose(2, 1, 0)),
                 vecs=v)
        maps.append(m)
    return maps


def assemble(results, NP, n_cores=8):
    SEQ = NP * 1024
    B = n_cores // 2
    y_prompt = np.zeros((B, SEQ, D), np.float32)
    y_sample = np.zeros((4 * n_cores, 64, D), np.float32)
    k_prompt = np.zeros((1, B, SEQ, 8, 2, 64), np.float32)
    v_prompt = np.zeros((1, B, SEQ, 8, 128), np.float32)
    conv_prompt = np.zeros((1, B, 3, D), np.float32)
    lru_prompt = np.zeros((1, B, D), np.float32)
    k_sample = np.zeros((1, 4 * n_cores, 64, 8, 2, 64), np.float32)
    v_sample = np.zeros((1, 4 * n_cores, 64, 8, 128), np.float32)
    conv_sample = np.zeros((1, 4 * n_cores, 3, D), np.float32)
    lru_sample = np.zeros((1, 4 * n_cores, D), np.float32)
    for c in range(n_cores):
        r = results[c]
        b, half = c // 2, c % 2
        for j in range(NP):
            i = _own_index(half, j)
            y_prompt[b, i * 512:(i + 1) * 512] = r["y_own"][j * 512:(j + 1) * 512]
        if half == 0:
            k_prompt[0, b] = r["k_all"].reshape(SEQ, 8, 2, 64)
            v_prompt[0, b] = r["v_all"].reshape(SEQ, 8, 128)
            conv_prompt[0, b] = r["conv_p"].transpose(2, 1, 0).reshape(3, D)
            lru_prompt[0, b] = r["lru_p"].T.reshape(D)
        sq = slice(4 * c, 4 * c + 4)
        y_sample[sq] = r["y_smp"].reshape(4, 64, D)
        k_sample[0, sq] = r["k_smp"].reshape(4, 64, 8, 2, 64)
        v_sample[0, sq] = r["v_smp"].reshape(4, 64, 8, 128)
        conv_sample[0, sq] = r["conv_s"].transpose(2, 3, 1, 0).reshape(4, 3, D)
        lru_sample[0, sq] = r["lru_s"].transpose(2, 1, 0).reshape(4, D)
    return (y_prompt, y_sample, k_prompt, v_prompt, conv_prompt, lru_prompt, k_sample, v_sample, conv_sample, lru_sample)


def kernel(**inputs):
    NP = inputs["x_prompt"].shape[1] // 1024
    nc, _ = build_program(NP)
    maps = make_in_maps(inputs, NP)
    res = run_bass_kernel_spmd(nc, maps, core_ids=list(range(8)))
    return assemble(res.results, NP)
```
